# Optimizing a Trainium2 kernel written in Bass

```python
import math
import jax, jax.numpy as jnp
from jax import lax
import numpy as np

D_MODEL = 1024
BATCH = 4
SEQ = 8192
DEPTH = 4

CONV_WIDTH = D_MODEL // 2
CONV_K = 3
RWKV_WIDTH = D_MODEL - CONV_WIDTH
HEAD_SIZE = 64
RWKV_HEADS = RWKV_WIDTH // HEAD_SIZE
D_DECAY_LORA = 64
D_AAA_LORA = 64
D_GATE_LORA = 128
D_FF = -(-8 * D_MODEL // (3 * 256)) * 256
N_MOD = 6
DEEPNORM_ALPHA = (2.0 * DEPTH) ** 0.25
DEEPNORM_BETA = (8.0 * DEPTH) ** -0.25
LN_EPS = 1e-5
GN_EPS = 64e-5

N_CONV_COLS = 3 * CONV_WIDTH
N_SHIFT_COLS = 3 * RWKV_WIDTH + D_DECAY_LORA + D_AAA_LORA + D_GATE_LORA
N_IN_COLS = N_CONV_COLS + N_SHIFT_COLS

kernel_name = "hymba_conv_rwkv7_deepnorm_adaln"


def layer_norm(x, g, b):
    xf = x.astype(jnp.float32)
    mu = jnp.mean(xf, axis=-1, keepdims=True)
    var = jnp.mean(jnp.square(xf - mu), axis=-1, keepdims=True)
    y = (xf - mu) * lax.rsqrt(var + LN_EPS)
    return (y * g.astype(jnp.float32) + b.astype(jnp.float32)).astype(x.dtype)


def token_shift(p):
    return jnp.pad(p, ((0, 0), (1, 0), (0, 0)))[:, :-1, :]


def heads(t):
    return t.reshape(t.shape[:-1] + (RWKV_HEADS, HEAD_SIZE))


def short_conv_mixer(p_conv, conv_w):
    b_gate, c_gate, u = jnp.split(p_conv, 3, axis=-1)
    z = c_gate * u
    zc = lax.conv_general_dilated(
        z, conv_w.astype(z.dtype)[:, None, :], window_strides=(1,),
        padding=[(CONV_K - 1, 0)],
        dimension_numbers=('NWC', 'WIO', 'NWC'),
        feature_group_count=CONV_WIDTH)
    return b_gate * zc


def rwkv7_scan(r, decay, k, v, a_vec, b_vec):
    def step(state, inp):
        r_t, w_t, k_t, v_t, a_t, b_t = inp
        sa = jnp.einsum('bhvk,bhk->bhv', state, a_t)
        state = (state * w_t[:, :, None, :]
                 + sa[..., None] * b_t[:, :, None, :]
                 + v_t[..., None] * k_t[:, :, None, :])
        y_t = jnp.einsum('bhvk,bhk->bhv', state, r_t)
        return state, y_t
    xs = tuple(jnp.moveaxis(t, 1, 0) for t in (r, decay, k, v, a_vec, b_vec))
    bsz = r.shape[0]
    s0 = jnp.zeros((bsz, RWKV_HEADS, HEAD_SIZE, HEAD_SIZE), jnp.float32)
    _, ys = lax.scan(step, s0, xs)
    return jnp.moveaxis(ys, 0, 1)


def rwkv7_mixer(p_shift, mu_shift, w0, w_decay_up, a0, a_up, g_up, k_k, k_a, r_k, lnx_g, lnx_b):
    p = p_shift + mu_shift * (token_shift(p_shift) - p_shift)
    i1 = RWKV_WIDTH
    i2 = 2 * RWKV_WIDTH
    i3 = 3 * RWKV_WIDTH
    i4 = i3 + D_DECAY_LORA
    i5 = i4 + D_AAA_LORA
    r, k, v, w_lo, a_lo, g_lo = jnp.split(p, [i1, i2, i3, i4, i5], axis=-1)
    w_log = -jax.nn.softplus(-(w0 + jnp.tanh(w_lo) @ w_decay_up)) - 0.5
    decay = jnp.exp(-jnp.exp(w_log.astype(jnp.float32)))
    a = jax.nn.sigmoid(a0 + a_lo @ a_up)
    g = jax.nn.sigmoid(g_lo) @ g_up
    kk = heads((k * k_k).astype(jnp.float32))
    kk = kk * lax.rsqrt(jnp.maximum(jnp.sum(jnp.square(kk), -1, keepdims=True), 1e-24))
    k = k * (1.0 + (a - 1.0) * k_a)
    rh = heads(r).astype(jnp.float32)
    kh = heads(k).astype(jnp.float32)
    vh = heads(v).astype(jnp.float32)
    ah = heads(a).astype(jnp.float32)
    y = rwkv7_scan(rh, heads(decay), kh, vh, -kk, kk * ah)
    mu = jnp.mean(y, -1, keepdims=True)
    var = jnp.mean(jnp.square(y - mu), -1, keepdims=True)
    yn = (y - mu) * lax.rsqrt(var + GN_EPS)
    yn = yn.reshape(y.shape[:-2] + (RWKV_WIDTH,)) * lnx_g + lnx_b
    bonus = jnp.sum(rh * kh * r_k.astype(jnp.float32), -1, keepdims=True) * vh
    out = yn + bonus.reshape(y.shape[:-2] + (RWKV_WIDTH,))
    return (out * g).astype(p_shift.dtype)


def setup_inputs(seed: int = 0) -> dict:
    key = jax.random.key(seed)
    ks = jax.random.split(key, 24)

    def nrm(k, shape, scale):
        return jax.random.normal(k, shape, jnp.float32) * scale

    L, D = DEPTH, D_MODEL
    return {
        "x": nrm(ks[0], (BATCH, SEQ, D), 1.0),
        "c": nrm(ks[1], (BATCH, D), 1.0),
        "w_mod": nrm(ks[2], (L, D, N_MOD * D), 0.1 * D ** -0.5),
        "b_mod": nrm(ks[3], (L, N_MOD * D), 0.02),
        "w_in": nrm(ks[4], (L, D, N_IN_COLS), D ** -0.5),
        "conv_w": nrm(ks[5], (L, CONV_K, CONV_WIDTH), CONV_K ** -0.5),
        "mu_shift": jax.random.uniform(ks[6], (L, N_SHIFT_COLS), jnp.float32),
        "w0": jax.random.uniform(ks[7], (L, RWKV_WIDTH), jnp.float32, -5.0, -1.0),
        "w_decay_up": nrm(ks[8], (L, D_DECAY_LORA, RWKV_WIDTH), 0.1 * D_DECAY_LORA ** -0.5),
        "a0": nrm(ks[9], (L, RWKV_WIDTH), 0.1),
        "a_up": nrm(ks[10], (L, D_AAA_LORA, RWKV_WIDTH), 0.1 * D_AAA_LORA ** -0.5),
        "g_up": nrm(ks[11], (L, D_GATE_LORA, RWKV_WIDTH), D_GATE_LORA ** -0.5),
        "k_k": 0.85 + nrm(ks[12], (L, RWKV_WIDTH), 0.02),
        "k_a": 1.0 + nrm(ks[13], (L, RWKV_WIDTH), 0.02),
        "r_k": nrm(ks[14], (L, RWKV_HEADS, HEAD_SIZE), 0.1),
        "lnx_g": 1.0 + nrm(ks[15], (L, RWKV_WIDTH), 0.02),
        "lnx_b": nrm(ks[16], (L, RWKV_WIDTH), 0.02),
        "w_out": nrm(ks[17], (L, D, D), DEEPNORM_BETA * D ** -0.5),
        "ln1_g": 1.0 + nrm(ks[18], (L, D), 0.02),
        "ln1_b": nrm(ks[19], (L, D), 0.02),
        "w_ffn_in": nrm(ks[20], (L, D, 2 * D_FF), D ** -0.5),
        "w_ffn_out": nrm(ks[21], (L, D_FF, D), DEEPNORM_BETA * D_FF ** -0.5),
        "ln2_g": 1.0 + nrm(ks[22], (L, D), 0.02),
        "ln2_b": nrm(ks[23], (L, D), 0.02),
    }


def reference(x, c, w_mod, b_mod, w_in, conv_w, mu_shift, w0, w_decay_up, a0, a_up, g_up,
              k_k, k_a, r_k, lnx_g, lnx_b, w_out, ln1_g, ln1_b, w_ffn_in, w_ffn_out,
              ln2_g, ln2_b):
    silu_c = jax.nn.silu(c)
    for l in range(DEPTH):
        mod = silu_c @ w_mod[l] + b_mod[l]
        sh1, sc1, gt1, sh2, sc2, gt2 = [m[:, None, :] for m in jnp.split(mod, N_MOD, axis=-1)]

        h = x * (1.0 + sc1) + sh1
        p = h @ w_in[l]
        y_conv = short_conv_mixer(p[..., :N_CONV_COLS], conv_w[l])
        y_rwkv = rwkv7_mixer(p[..., N_CONV_COLS:], mu_shift[l], w0[l], w_decay_up[l], a0[l],
                             a_up[l], g_up[l], k_k[l], k_a[l], r_k[l], lnx_g[l], lnx_b[l])
        mix = jnp.concatenate([y_conv, y_rwkv], axis=-1) @ w_out[l]
        x = layer_norm(DEEPNORM_ALPHA * x + (1.0 + gt1) * mix, ln1_g[l], ln1_b[l])

        h = x * (1.0 + sc2) + sh2
        gate, up = jnp.split(h @ w_ffn_in[l], 2, axis=-1)
        f = (jax.nn.silu(gate) * up) @ w_ffn_out[l]
        x = layer_norm(DEEPNORM_ALPHA * x + (1.0 + gt2) * f, ln2_g[l], ln2_b[l])
    return x
```

```python
import math
import os
KSTOP = int(os.environ.get('KSTOP', '99'))
import numpy as np
from contextlib import ExitStack
import concourse.bass as bass
import concourse.mybir as mybir
from concourse.bass_utils import run_bass_kernel_spmd

F32 = mybir.dt.float32
BF16 = mybir.dt.bfloat16
AF = mybir.ActivationFunctionType
ALU = mybir.AluOpType
AX = mybir.AxisListType

D = 1024
NB = 8
DFF = 2816
NF = 22
NSLAB = 28
NMOD = 12
NV = 134
SLAB = 4096
ALPHA = (2.0 * 4) ** 0.25
LN_EPS = 1e-5
GN_EPS = 64e-5
CDEC = math.exp(-0.5)

V_MU = 0
V_W0 = 14
V_A0 = 18
V_KK = 22
V_KA = 26
V_RK = 30
V_LG = 34
V_LB = 38
V_CW = 42
V_L1G = 54
V_L1B = 62
V_L2G = 70
V_L2B = 78
V_BM = 86
DB1, DA1, DG1, DB2, DA2, DG2 = 0, 8, 16, 24, 32, 40


class _Rec:
    def __getattr__(self, name):
        return lambda *a, **k: (name, a, k)


_REC = _Rec()


class Prog:
    EPOCH = 16000

    def __init__(self, nc, stack):
        self.nc = nc
        self.stack = stack
        self.engs = ('sp', 'pe', 'act', 'dve', 'pool')
        self.streams = {e: [] for e in self.engs}
        self.count = {e: 0 for e in self.engs}
        self.clock = {e: {} for e in self.engs}
        self.iclock = {}
        self.sems = {}
        self.last_write = {}
        self.reads = {}
        self.chan_count = {}
        self.nwaits = 0
        self.nops = 0

    def _sem(self, name):
        if name not in self.sems:
            self.sems[name] = self.stack.enter_context(self.nc.semaphore(name.replace(':', '_')))
        return self.sems[name]

    def _wait(self, eng, dep, force=False):
        e2, n = dep
        if n <= 0:
            return
        if not force and self.clock[eng].get(e2, 0) >= n:
            return
        if e2 in self.engs:
            ep = (n - 1) // self.EPOCH
            sem = self._sem(f"{e2}:{ep}")
            val = n - ep * self.EPOCH
        else:
            sem = self._sem(e2)
            val = 16 * n
        self.streams[eng].append(('wait', sem, val))
        self.nwaits += 1
        ck = self.clock[eng]
        for k, v in self.iclock.get(dep, {}).items():
            if ck.get(k, 0) < v:
                ck[k] = v
        if ck.get(e2, 0) < n:
            ck[e2] = n

    def _deps(self, eng, ident, reads, writes):
        deps = set()
        for k in reads:
            lw = self.last_write.get(k)
            if lw is not None:
                deps.add(lw)
        for k in writes:
            lw = self.last_write.get(k)
            if lw is not None:
                deps.add(lw)
            for r in self.reads.get(k, ()):
                deps.add(r)
        for d in sorted(deps):
            e2, n = d
            if e2 == eng:
                if eng != 'pe' and eng != 'sp':
                    self._wait(eng, d)
                continue
            self._wait(eng, d)
        for k in reads:
            self.reads.setdefault(k, []).append(ident)
        for k in writes:
            self.last_write[k] = ident
            self.reads[k] = []

    def op(self, eng, fn, reads=(), writes=()):
        n = self.count[eng] + 1
        ident = (eng, n)
        self._deps(eng, ident, reads, writes)
        self.count[eng] = n
        ep = (n - 1) // self.EPOCH
        sem = self._sem(f"{eng}:{ep}")
        self.streams[eng].append(('op', fn(_REC), sem, 1))
        self.iclock[ident] = dict(self.clock[eng])
        self.nops += 1

    def dma(self, chan, out, in_, reads=(), writes=(), eng='sp'):
        n = self.chan_count.get(chan, 0) + 1
        ident = (chan, n)
        self._deps(eng, ident, reads, writes)
        self.chan_count[chan] = n
        sem = self._sem(chan)
        self.streams[eng].append(('op', ('dma_start', (), dict(out=out, in_=in_)), sem, 16))
        self.iclock[ident] = dict(self.clock[eng])
        self.nops += 1

    def barrier(self):
        for e in self.engs:
            for e2 in self.engs:
                if e2 != e and e2 != 'sp':
                    self._wait(e, (e2, self.count[e2]))
            for ch, n in self.chan_count.items():
                self._wait(e, (ch, n))
        self.last_write = {}
        self.reads = {}

    def emit(self):
        with self.nc.Block() as block:
            for ename, deco in (('sp', block.sync), ('pe', block.tensor), ('act', block.scalar),
                                ('dve', block.vector), ('pool', block.gpsimd)):
                stream = self.streams[ename]

                def body(e, stream=stream):
                    for it in stream:
                        if it[0] == 'wait':
                            e.wait_ge(it[1], it[2])
                        else:
                            name, a, k = it[1]
                            getattr(e, name)(*a, **k).then_inc(it[2], it[3])
                deco(body)
        self.streams = {e: [] for e in self.engs}


def build_nc(SEQ, L, T):
    NCH = T // 128
    NT = SEQ // T
    nc = bass.Bass("TRN2", target_bir_lowering=False)
    xT_d = nc.dram_tensor("xT", [D, SEQ], F32, kind="ExternalInput").ap()
    cT_d = nc.dram_tensor("cT", [128, 8], F32, kind="ExternalInput").ap()
    vecs_d = nc.dram_tensor("vecs", [128, L * NV], F32, kind="ExternalInput").ap()
    wslab_d = nc.dram_tensor("wslab", [L * NSLAB, 128, SLAB], F32, kind="ExternalInput").ap()
    wmod_d = nc.dram_tensor("wmod", [L * NMOD, 128, SLAB], F32, kind="ExternalInput").ap()
    lw_d = nc.dram_tensor("lw", [128, L * 512], F32, kind="ExternalInput").ap()
    gu_d = nc.dram_tensor("gu", [128, L * 512], F32, kind="ExternalInput").ap()
    consts_d = nc.dram_tensor("consts", [128, 1408], F32, kind="ExternalInput").ap()
    yT_d = nc.dram_tensor("yT", [D, SEQ], F32, kind="ExternalOutput").ap()
    wsb_d = nc.dram_tensor("wsb", [L * NSLAB, 128, SLAB], BF16, kind="Internal").ap()

    with ExitStack() as st:
        P = Prog(nc, st)

        def sb(name, shape, dt, stack=st):
            return stack.enter_context(nc.sbuf_tensor("s_" + name, shape, dt))

        def ps(name, shape, dt):
            return st.enter_context(nc.psum_tensor("p_" + name, shape, dt))

        mm0 = ps("mm0", [128, 512], F32)
        mm1 = ps("mm1", [128, 512], F32)
        sA = ps("sA", [128, 512], F32)
        sB = ps("sB", [128, 512], F32)
        tpf = ps("tpf", [128, 512], F32)
        tpb = ps("tpb", [128, 1024], BF16)
        xu = ps("xu", [128, 512], F32)
        hp = ps("hp", [128, 512], F32)

        consts = sb("consts", [128, 1408], F32)
        ident = consts[:, 0:128]
        mask4 = consts[:, 128:640]
        maskN4 = consts[:, 640:1152]
        bones = consts[:, 1152:1280]
        ones = consts[:, 1280:1408]
        identb = sb("identb", [128, 128], BF16)
        onesb = sb("onesb", [128, 128], BF16)
        vecs = sb("vecs", [128, L * NV], F32)
        cT = sb("cT", [128, 8], F32)
        scT = sb("scT", [128, 8], F32)
        mod = sb("mod", [128, L * 48], F32)
        der = sb("der", [128, L * 48], F32)
        omka = sb("omka", [128, L * 4], F32)
        lwb = sb("lwb", [128, L * 512], BF16)
        gub = sb("gub", [128, L * 512], BF16)
        Hm = sb("Hm", [128, L, 4, 64], F32)
        Hb = sb("Hb", [128, L, 4, 64], BF16)
        hist = sb("hist", [128, L * 14], F32)
        zh = sb("zh", [128, L, 4, 2], F32)

        def V(l, col, n=1):
            return vecs[:, l * NV + col: l * NV + col + n]

        def DER(l, col, n=1):
            return der[:, l * 48 + col: l * 48 + col + n]

        with ExitStack() as pst:
            stg = [sb(f"stg{i}", [128, SLAB], F32, pst) for i in range(2)]
            stb = [sb(f"stb{i}", [128, SLAB], BF16, pst) for i in range(2)]
            P.dma('c0', consts[:], consts_d[:, :], writes=['consts'])
            P.dma('c1', vecs[:], vecs_d[:, :], writes=['vecs'])
            P.dma('c2', cT[:], cT_d[:, :], writes=['cT'])
            P.op('dve', lambda e: e.tensor_copy(out=identb[:], in_=ident), reads=['consts'], writes=['identb'])
            P.op('dve', lambda e: e.tensor_copy(out=onesb[:], in_=ones), reads=['consts'], writes=['onesb'])
            P.op('act', lambda e: e.activation(out=scT[:], in_=cT[:], func=AF.Silu), reads=['cT'], writes=['scT'])
            P.op('pool', lambda e: e.memset(Hm[:], 0.0), writes=['Hm'])
            P.op('pool', lambda e: e.memset(Hb[:], 0.0), writes=['Hb'])
            P.op('pool', lambda e: e.memset(hist[:], 0.0), writes=['hist'])
            P.op('pool', lambda e: e.memset(zh[:], 0.0), writes=['zh'])
            P.dma('stg0', stg[0][:, 0:L * 512], lw_d[:, :], writes=['stg0'])
            P.op('dve', lambda e: e.tensor_copy(out=lwb[:], in_=stg[0][:, 0:L * 512]), reads=['stg0'], writes=['lwb'])
            P.dma('stg1', stg[1][:, 0:L * 512], gu_d[:, :], writes=['stg1'])
            P.op('dve', lambda e: e.tensor_copy(out=gub[:], in_=stg[1][:, 0:L * 512]), reads=['stg1'], writes=['gub'])
            q = 0
            for l in range(L):
                for s in range(NMOD):
                    b = q % 2
                    q += 1
                    P.dma(f'stg{b}', stg[b][:], wmod_d[l * NMOD + s, :, :], writes=[f'stg{b}'])
                    for jj in range(4):
                        col = l * 48 + s * 4 + jj
                        for kc in range(8):
                            P.op('pe', lambda e, b=b, jj=jj, kc=kc, col=col: e.matmul(
                                hp[:, col:col + 1], lhsT=stg[b][:, kc * 512 + jj * 128: kc * 512 + jj * 128 + 128],
                                rhs=scT[:, kc:kc + 1], start=(kc == 0), stop=(kc == 7)),
                                reads=[f'stg{b}', 'scT'], writes=['hp'])
            for l in range(L):
                P.op('dve', lambda e, l=l: e.tensor_tensor(out=mod[:, l * 48:(l + 1) * 48], in0=hp[:, l * 48:(l + 1) * 48],
                                                          in1=V(l, V_BM, 48), op=ALU.add),
                     reads=['vecs'], writes=['hp', 'mod'])
                for (dst, src, kind) in ((DB1, 0, 'c'), (DA1, 8, 'p1'), (DG1, 16, 'g'),
                                         (DB2, 24, 'c'), (DA2, 32, 'p1'), (DG2, 40, 'g')):
                    o = DER(l, dst, 8)
                    i = mod[:, l * 48 + src: l * 48 + src + 8]
                    if kind == 'c':
                        P.op('pool', lambda e, o=o, i=i: e.tensor_copy(out=o, in_=i), reads=['mod'], writes=['der'])
                    elif kind == 'p1':
                        P.op('pool', lambda e, o=o, i=i: e.tensor_scalar_add(out=o, in0=i, scalar1=1.0), reads=['mod'], writes=['der'])
                    else:
                        P.op('pool', lambda e, o=o, i=i: e.tensor_scalar(out=o, in0=i, scalar1=1.0, scalar2=1.0 / ALPHA,
                                                                        op0=ALU.add, op1=ALU.mult), reads=['mod'], writes=['der'])
                P.op('pool', lambda e, l=l: e.tensor_scalar(out=omka[:, l * 4:(l + 1) * 4], in0=V(l, V_KA, 4), scalar1=-1.0, scalar2=1.0,
                                                            op0=ALU.mult, op1=ALU.add), reads=['vecs'], writes=['omka'])
            cast_engs = ('dve', 'pool', 'act')
            for i in range(L * NSLAB):
                b = q % 2
                q += 1
                P.dma(f'stg{b}', stg[b][:], wslab_d[i, :, :], writes=[f'stg{b}'])
                ce = cast_engs[i % 3]
                if ce == 'act':
                    P.op('act', lambda e, b=b: e.copy(out=stb[b][:], in_=stg[b][:]), reads=[f'stg{b}'], writes=[f'stb{b}'])
                else:
                    P.op(ce, lambda e, b=b: e.tensor_copy(out=stb[b][:], in_=stg[b][:]), reads=[f'stg{b}'], writes=[f'stb{b}'])
                P.dma(f'stb{b}', wsb_d[i, :, :], stb[b][:], reads=[f'stb{b}'])
            P.barrier()
            P.emit()

        NSLOT = 4
        ring = sb("ring", [128, NSLOT, SLAB], BF16)
        xT = sb("xT", [128, NB, T], F32)
        hT = sb("hT", [128, NB, T], BF16)
        ycat = sb("ycat", [128, NB, T], BF16)
        zbuf = sb("zbuf", [128, 4, T + 2], F32)
        usb = sb("usb", [128, 4, T], F32)
        rT = sb("rT", [128, 4, T], F32)
        kT = sb("kT", [128, 4, T], F32)
        vT = sb("vT", [128, 4, T], F32)
        gsb = sb("gsb", [128, 4, T], F32)
        bonus = sb("bonus", [128, 4, T], F32)
        praw = [sb(f"praw{i}", [128, T + 1], F32) for i in range(2)]
        dtmp = [sb(f"dtmp{i}", [128, T], F32) for i in range(2)]
        waf = sb("waf", [128, T], F32)
        glf = sb("glf", [128, T], F32)
        tw = sb("tw", [128, T], BF16)
        sgT = sb("sgT", [128, T], BF16)
        tnames = ['sg', 'aa', 'cs', 'csx', 'eP', 'eN', 'eX', 'eT', 'kkr', 'sq', 'kk', 't1', 'kmod', 'bvec', 'prod']
        tmp = {n: sb("t_" + n, [128, T], F32) for n in tnames}
        nb = sb("nb", [128, NCH], F32)
        dCt = sb("dCt", [128, NCH, 4], F32)
        AR = sb("AR", [128, 4, NCH, 256], BF16)
        KT = sb("KT", [128, 4, T], BF16)
        BT = sb("BT", [128, 4, T], BF16)
        KTL = sb("KTL", [128, 4, T], BF16)
        BTL = sb("BTL", [128, 4, T], BF16)
        Vf = sb("Vf", [128, 512], F32)
        Vb = sb("Vb", [128, 512], BF16)
        KTLt = sb("KTLt", [128, 512], BF16)
        BTLt = sb("BTLt", [128, 512], BF16)
        Sc = sb("Sc", [128, 8, 512], BF16)
        Nm = [sb(f"Nm{i}", [128, 8, 128], BF16) for i in range(2)]
        Am = [sb(f"Am{i}", [128, 8, 128], BF16) for i in range(2)]
        Pm = [sb(f"Pm{i}", [128, 8, 128], BF16) for i in range(2)]
        Xb = sb("Xb", [128, 512], BF16)
        Ub = sb("Ub", [128, 512], BF16)
        ysb = sb("ysb", [128, 512], F32)
        ysq = sb("ysq", [128, 512], F32)
        ync = sb("ync", [128, 512], F32)
        ynb = sb("ynb", [128, 512], BF16)
        gst = sb("gst", [128, 40], F32)
        otmp = [sb(f"otmp{i}", [128, 128], F32) for i in range(2)]
        zb = [sb(f"zb{i}", [128, T], BF16) for i in range(2)]
        zq = [sb(f"zq{i}", [128, T], BF16) for i in range(2)]
        lnm = sb("lnm", [128, T], F32)
        lnv = sb("lnv", [128, T], F32)
        lnr = sb("lnr", [128, T], F32)
        act = sb("act", [128, NF, T], BF16)
        slt = [sb(f"slt{i}", [128, T], F32) for i in range(2)]

        total_slabs = NT * L * NSLAB
        state = {'loaded': 0, 'mmq': 0}

        def ensure_loaded(upto):
            while state['loaded'] <= min(upto, total_slabs - 1):
                i = state['loaded']
                slot = i % NSLOT
                P.dma(f'ring{slot}', ring[:, slot, :], wsb_d[i % (L * NSLAB), :, :], writes=[f'ring{slot}'])
                state['loaded'] += 1

        def use_slab(i):
            ensure_loaded(i + NSLOT - 1)
            return i % NSLOT

        def mmbank():
            state['mmq'] += 1
            return (mm0, 'mm0') if state['mmq'] % 2 else (mm1, 'mm1')

        def dense(bank, bkey, slot, stride, off, rhs_t, rkeys, nk):
            for kc in range(nk):
                P.op('pe', lambda e, kc=kc: e.matmul(bank[:, 0:T], lhsT=ring[:, slot, kc * stride + off: kc * stride + off + 128],
                                                     rhs=rhs_t[:, kc, :], start=(kc == 0), stop=(kc == nk - 1)),
                     reads=[f'ring{slot}'] + rkeys, writes=[bkey])

        def layernorm(l, gcol, bcol, Acol, Bcol, lnext):
            for j in range(NB):
                b = j % 2
                P.op('pool', lambda e, j=j, b=b: e.tensor_copy(out=zb[b][:], in_=xT[:, j, :]), reads=[f'xT{j}'], writes=[f'zb{b}'])
                P.op('act', lambda e, j=j, b=b: e.activation(out=zq[b][:], in_=xT[:, j, :], func=AF.Square), reads=[f'xT{j}'], writes=[f'zq{b}'])
                P.op('pe', lambda e, j=j, b=b: e.matmul(mm0[:, 0:T], lhsT=onesb[:], rhs=zb[b][:], start=(j == 0), stop=(j == NB - 1)),
                     reads=[f'zb{b}'], writes=['mm0'])
                P.op('pe', lambda e, j=j, b=b: e.matmul(mm1[:, 0:T], lhsT=onesb[:], rhs=zq[b][:], start=(j == 0), stop=(j == NB - 1)),
                     reads=[f'zq{b}'], writes=['mm1'])
            P.op('act', lambda e: e.activation(out=lnm[:], in_=mm0[:, 0:T], func=AF.Copy, scale=1.0 / D), writes=['mm0', 'lnm'])
            P.op('act', lambda e: e.activation(out=lnv[:], in_=mm0[:, 0:T], func=AF.Square, scale=1.0 / D), writes=['mm0', 'lnv'])
            P.op('dve', lambda e: e.scalar_tensor_tensor(out=lnv[:], in0=mm1[:, 0:T], scalar=1.0 / D, in1=lnv[:], op0=ALU.mult, op1=ALU.subtract),
                 writes=['mm1', 'lnv'])
            P.op('act', lambda e: e.activation(out=lnr[:], in_=lnv[:], func=AF.Ln, bias=LN_EPS / (ALPHA * ALPHA)), reads=['lnv'], writes=['lnr'])
            P.op('act', lambda e: e.activation(out=lnr[:], in_=lnr[:], func=AF.Exp, scale=-0.5), writes=['lnr'])
            for j in range(NB):
                P.op('dve', lambda e, j=j: e.tensor_tensor(out=xT[:, j, :], in0=xT[:, j, :], in1=lnm[:], op=ALU.subtract),
                     reads=['lnm'], writes=[f'xT{j}'])
                P.op('pool', lambda e, j=j: e.tensor_tensor(out=xT[:, j, :], in0=xT[:, j, :], in1=lnr[:], op=ALU.mult),
                     reads=['lnr'], writes=[f'xT{j}'])
                P.op('act', lambda e, j=j: e.activation(out=xT[:, j, :], in_=xT[:, j, :], func=AF.Identity,
                                                        scale=V(l, gcol + j), bias=V(l, bcol + j)), writes=[f'xT{j}'])
                if lnext is not None:
                    P.op('act', lambda e, j=j: e.activation(out=hT[:, j, :], in_=xT[:, j, :], func=AF.Identity,
                                                            scale=DER(lnext, Acol + j), bias=DER(lnext, Bcol + j)),
                         reads=[f'xT{j}'], writes=[f'hT{j}'])

        hkeys = [f'hT{j}' for j in range(NB)]
        ykeys = [f'ycat{j}' for j in range(NB)]
        si = 0
        for ti in range(NT):
            t0 = ti * T
            P.dma('xin', xT[:], xT_d[:, t0:t0 + T].rearrange("(j p) t -> p j t", p=128), writes=[f'xT{j}' for j in range(NB)])
            for j in range(NB):
                P.op('act', lambda e, j=j: e.activation(out=hT[:, j, :], in_=xT[:, j, :], func=AF.Identity,
                                                        scale=DER(0, DA1 + j), bias=DER(0, DB1 + j)),
                     reads=[f'xT{j}'], writes=[f'hT{j}'])
            for l in range(L):
                if KSTOP < 1:
                    break
                slot = use_slab(si); si += 1
                for j in range(4):
                    bank, bk = mmbank()
                    dense(bank, bk, slot, 512, j * 128, hT, hkeys, NB)
                    P.op('act', lambda e, j=j, bank=bank: e.copy(out=usb[:, j, :], in_=bank[:, 0:T]), writes=[bk, f'usb{j}'])
                slot = use_slab(si); si += 1
                for j in range(4):
                    bank, bk = mmbank()
                    dense(bank, bk, slot, 512, j * 128, hT, hkeys, NB)
                    P.op('pool', lambda e, j=j: e.tensor_copy(out=zbuf[:, j, 0:2], in_=zh[:, l, j, :]), reads=[f'zh{l}_{j}'], writes=[f'zbuf{j}'])
                    P.op('dve', lambda e, j=j, bank=bank: e.tensor_tensor(out=zbuf[:, j, 2:T + 2], in0=bank[:, 0:T], in1=usb[:, j, :], op=ALU.mult),
                         reads=[f'usb{j}'], writes=[bk, f'zbuf{j}'])
                    P.op('pool', lambda e, j=j: e.tensor_copy(out=zh[:, l, j, :], in_=zbuf[:, j, T:T + 2]), reads=[f'zbuf{j}'], writes=[f'zh{l}_{j}'])
                    P.op('pool', lambda e, j=j: e.tensor_scalar(out=usb[:, j, :], in0=zbuf[:, j, 0:T], scalar1=V(l, V_CW + j), scalar2=None, op0=ALU.mult),
                         reads=[f'zbuf{j}'], writes=[f'usb{j}'])
                    P.op('dve', lambda e, j=j: e.scalar_tensor_tensor(out=usb[:, j, :], in0=zbuf[:, j, 1:T + 1], scalar=V(l, V_CW + 4 + j), in1=usb[:, j, :],
                                                                      op0=ALU.mult, op1=ALU.add), reads=[f'zbuf{j}'], writes=[f'usb{j}'])
                    P.op('dve', lambda e, j=j: e.scalar_tensor_tensor(out=usb[:, j, :], in0=zbuf[:, j, 2:T + 2], scalar=V(l, V_CW + 8 + j), in1=usb[:, j, :],
                                                                       op0=ALU.mult, op1=ALU.add), reads=[f'zbuf{j}'], writes=[f'usb{j}'])
                slot = use_slab(si); si += 1
                for j in range(4):
                    bank, bk = mmbank()
                    dense(bank, bk, slot, 512, j * 128, hT, hkeys, NB)
                    P.op('dve', lambda e, j=j, bank=bank: e.tensor_tensor(out=ycat[:, j, :], in0=bank[:, 0:T], in1=usb[:, j, :], op=ALU.mult),
                         reads=[f'usb{j}'], writes=[bk, f'ycat{j}'])

                if KSTOP < 2:
                    break
                pq = [0]

                def shift_lerp(bank, bk, mucol, dst, dkey):
                    b = pq[0] % 2
                    pq[0] += 1
                    hc = hist[:, l * 14 + mucol: l * 14 + mucol + 1]
                    hk = f'hist{l}_{mucol}'
                    P.op('pool', lambda e: e.tensor_copy(out=praw[b][:, 0:1], in_=hc), reads=[hk], writes=[f'praw{b}'])
                    P.op('act', lambda e: e.copy(out=praw[b][:, 1:T + 1], in_=bank[:, 0:T]), writes=[bk, f'praw{b}'])
                    P.op('pool', lambda e: e.tensor_copy(out=hc, in_=praw[b][:, T:T + 1]), reads=[f'praw{b}'], writes=[hk])
                    P.op('dve', lambda e: e.tensor_tensor(out=dtmp[b][:], in0=praw[b][:, 0:T], in1=praw[b][:, 1:T + 1], op=ALU.subtract),
                         reads=[f'praw{b}'], writes=[f'dtmp{b}'])
                    P.op('dve', lambda e: e.scalar_tensor_tensor(out=dst, in0=dtmp[b][:], scalar=V(l, V_MU + mucol), in1=praw[b][:, 1:T + 1],
                                                                 op0=ALU.mult, op1=ALU.add), reads=[f'dtmp{b}', f'praw{b}'], writes=[dkey])

                for gi, (dstT, nm) in enumerate(((rT, 'rT'), (kT, 'kT'), (vT, 'vT'))):
                    slot = use_slab(si); si += 1
                    for j in range(4):
                        bank, bk = mmbank()
                        dense(bank, bk, slot, 512, j * 128, hT, hkeys, NB)
                        shift_lerp(bank, bk, gi * 4 + j, dstT[:, j, :], f'{nm}{j}')
                slot = use_slab(si); si += 1
                bank, bk = mmbank()
                dense(bank, bk, slot, 256, 0, hT, hkeys, NB)
                shift_lerp(bank, bk, 12, waf[:], 'waf')
                P.op('act', lambda e: e.activation(out=tw[0:64, :], in_=waf[0:64, :], func=AF.Tanh), reads=['waf'], writes=['tw'])
                P.op('pool', lambda e: e.tensor_copy(out=tw[64:128, :], in_=waf[64:128, :]), reads=['waf'], writes=['tw'])
                bank, bk = mmbank()
                dense(bank, bk, slot, 256, 128, hT, hkeys, NB)
                shift_lerp(bank, bk, 13, glf[:], 'glf')
                P.op('act', lambda e: e.activation(out=sgT[:], in_=glf[:], func=AF.Sigmoid), reads=['glf'], writes=['sgT'])

                if KSTOP < 3:
                    break
                tm = tmp
                for cc in range(4):
                    wcol = l * 512 + cc * 128
                    bank, bk = mmbank()
                    P.op('pe', lambda e, bank=bank, wcol=wcol: e.matmul(bank[:, 0:T], lhsT=lwb[0:64, wcol:wcol + 128], rhs=tw[0:64, :], start=True, stop=True),
                         reads=['tw'], writes=[bk])
                    P.op('act', lambda e, bank=bank, cc=cc: e.activation(out=tm['sg'][:], in_=bank[:, 0:T], func=AF.Sigmoid, bias=V(l, V_W0 + cc)),
                         writes=[bk, 'sg'])
                    bank, bk = mmbank()
                    P.op('pe', lambda e, bank=bank, wcol=wcol: e.matmul(bank[:, 0:T], lhsT=lwb[64:128, wcol:wcol + 128], rhs=tw[64:128, :], start=True, stop=True),
                         reads=['tw'], writes=[bk])
                    P.op('act', lambda e, bank=bank, cc=cc: e.activation(out=tm['aa'][:], in_=bank[:, 0:T], func=AF.Sigmoid, bias=V(l, V_A0 + cc)),
                         writes=[bk, 'aa'])
                    bank, bk = mmbank()
                    P.op('pe', lambda e, bank=bank, wcol=wcol: e.matmul(bank[:, 0:T], lhsT=gub[:, wcol:wcol + 128], rhs=sgT[:], start=True, stop=True),
                         reads=['sgT'], writes=[bk])
                    P.op('act', lambda e, bank=bank, cc=cc: e.copy(out=gsb[:, cc, :], in_=bank[:, 0:T]), writes=[bk, f'gsb{cc}'])
                    for ch in range(NCH):
                        cs_ = slice(ch * 128, (ch + 1) * 128)
                        P.op('dve', lambda e, cs_=cs_: e.tensor_tensor_scan(out=tm['cs'][:, cs_], data0=ones, data1=tm['sg'][:, cs_], initial=0.0,
                                                                            op0=ALU.mult, op1=ALU.add), reads=['sg'], writes=['cs'])
                    P.op('pool', lambda e: e.tensor_tensor(out=tm['csx'][:], in0=tm['cs'][:], in1=tm['sg'][:], op=ALU.subtract), reads=['cs', 'sg'], writes=['csx'])
                    P.op('act', lambda e: e.activation(out=tm['eP'][:], in_=tm['cs'][:], func=AF.Exp, scale=-CDEC), reads=['cs'], writes=['eP'])
                    P.op('act', lambda e: e.activation(out=tm['eN'][:], in_=tm['cs'][:], func=AF.Exp, scale=CDEC), reads=['cs'], writes=['eN'])
                    P.op('act', lambda e: e.activation(out=tm['eX'][:], in_=tm['csx'][:], func=AF.Exp, scale=-CDEC), reads=['csx'], writes=['eX'])
                    P.op('dve', lambda e: e.tensor_scalar(out=nb[:], in0=tm['cs'][:].rearrange("p (c t) -> p c t", t=128)[:, :, 127], scalar1=-CDEC, scalar2=None, op0=ALU.mult),
                         reads=['cs'], writes=['nb'])
                    P.op('act', lambda e, cc=cc: e.activation(out=dCt[:, :, cc], in_=nb[:], func=AF.Exp), reads=['nb'], writes=['dCt'])
                    for ch in range(NCH):
                        cs_ = slice(ch * 128, (ch + 1) * 128)
                        P.op('act', lambda e, cs_=cs_, ch=ch: e.activation(out=tm['eT'][:, cs_], in_=tm['cs'][:, cs_], func=AF.Exp, scale=CDEC, bias=nb[:, ch:ch + 1]),
                             reads=['cs', 'nb'], writes=['eT'])
                    P.op('dve', lambda e, cc=cc: e.tensor_scalar(out=tm['kkr'][:], in0=kT[:, cc, :], scalar1=V(l, V_KK + cc), scalar2=None, op0=ALU.mult),
                         reads=[f'kT{cc}'], writes=['kkr'])
                    P.op('act', lambda e: e.activation(out=tm['sq'][:], in_=tm['kkr'][:], func=AF.Square), reads=['kkr'], writes=['sq'])
                    bank, bk = mmbank()
                    P.op('pe', lambda e, bank=bank: e.matmul(bank[:, 0:T], lhsT=bones, rhs=tm['sq'][:], start=True, stop=True), reads=['sq'], writes=[bk])
                    P.op('dve', lambda e, bank=bank: e.tensor_scalar(out=tm['sq'][:], in0=bank[:, 0:T], scalar1=1e-24, scalar2=None, op0=ALU.max),
                         writes=[bk, 'sq'])
                    P.op('act', lambda e: e.activation(out=tm['sq'][:], in_=tm['sq'][:], func=AF.Ln), writes=['sq'])
                    P.op('act', lambda e: e.activation(out=tm['sq'][:], in_=tm['sq'][:], func=AF.Exp, scale=-0.5), writes=['sq'])
                    P.op('pool', lambda e: e.tensor_tensor(out=tm['kk'][:], in0=tm['kkr'][:], in1=tm['sq'][:], op=ALU.mult), reads=['kkr', 'sq'], writes=['kk'])
                    P.op('dve', lambda e, cc=cc: e.tensor_scalar(out=tm['t1'][:], in0=tm['aa'][:], scalar1=V(l, V_KA + cc), scalar2=omka[:, l * 4 + cc: l * 4 + cc + 1],
                                                                 op0=ALU.mult, op1=ALU.add), reads=['aa'], writes=['t1'])
                    P.op('pool', lambda e, cc=cc: e.tensor_tensor(out=tm['kmod'][:], in0=kT[:, cc, :], in1=tm['t1'][:], op=ALU.mult), reads=[f'kT{cc}', 't1'], writes=['kmod'])
                    P.op('pool', lambda e: e.tensor_tensor(out=tm['bvec'][:], in0=tm['kk'][:], in1=tm['aa'][:], op=ALU.mult), reads=['kk', 'aa'], writes=['bvec'])
                    P.op('dve', lambda e, cc=cc: e.scalar_tensor_tensor(out=tm['prod'][:], in0=rT[:, cc, :], scalar=V(l, V_RK + cc), in1=tm['kmod'][:], op0=ALU.mult, op1=ALU.mult),
                         reads=[f'rT{cc}', 'kmod'], writes=['prod'])
                    bank, bk = mmbank()
                    P.op('pe', lambda e, bank=bank: e.matmul(bank[:, 0:T], lhsT=bones, rhs=tm['prod'][:], start=True, stop=True), reads=['prod'], writes=[bk])
                    P.op('dve', lambda e, bank=bank, cc=cc: e.tensor_tensor(out=bonus[:, cc, :], in0=bank[:, 0:T], in1=vT[:, cc, :], op=ALU.mult),
                         reads=[f'vT{cc}'], writes=[bk, f'bonus{cc}'])
                    v3 = lambda a: a.rearrange("p (c t) -> p c t", t=128)
                    P.op('dve', lambda e, cc=cc: e.scalar_tensor_tensor(out=AR[:, cc, :, 0:128], in0=v3(tm['kk'][:]), scalar=-1.0, in1=v3(tm['eX'][:]), op0=ALU.mult, op1=ALU.mult),
                         reads=['kk', 'eX'], writes=[f'AR{cc}'])
                    P.op('pool', lambda e, cc=cc: e.tensor_tensor(out=AR[:, cc, :, 128:256], in0=v3(rT[:, cc, :]), in1=v3(tm['eP'][:]), op=ALU.mult),
                         reads=[f'rT{cc}', 'eP'], writes=[f'AR{cc}'])
                    P.op('dve', lambda e, cc=cc: e.tensor_tensor(out=KT[:, cc, :], in0=tm['kmod'][:], in1=tm['eN'][:], op=ALU.mult), reads=['kmod', 'eN'], writes=[f'KT{cc}'])
                    P.op('pool', lambda e, cc=cc: e.tensor_tensor(out=BT[:, cc, :], in0=tm['bvec'][:], in1=tm['eN'][:], op=ALU.mult), reads=['bvec', 'eN'], writes=[f'BT{cc}'])
                    P.op('dve', lambda e, cc=cc: e.tensor_tensor(out=KTL[:, cc, :], in0=tm['kmod'][:], in1=tm['eT'][:], op=ALU.mult), reads=['kmod', 'eT'], writes=[f'KTL{cc}'])
                    P.op('pool', lambda e, cc=cc: e.tensor_tensor(out=BTL[:, cc, :], in0=tm['bvec'][:], in1=tm['eT'][:], op=ALU.mult), reads=['bvec', 'eT'], writes=[f'BTL{cc}'])

                if KSTOP < 4:
                    break
                for ch in range(NCH):
                    cs_ = slice(ch * 128, (ch + 1) * 128)
                    for cc in range(4):
                        P.op('pe', lambda e, cc=cc: e.transpose(out=tpf[:, cc * 128:(cc + 1) * 128], in_=vT[:, cc, cs_], identity=ident),
                             reads=[f'vT{cc}'], writes=['tpf'])
                    P.op('act', lambda e: e.copy(out=Vf[:], in_=tpf[:]), writes=['tpf', 'Vf'])
                    P.op('pool', lambda e: e.tensor_copy(out=Vb[:], in_=Vf[:]), reads=['Vf'], writes=['Vb'])
                    for cc in range(4):
                        P.op('pe', lambda e, cc=cc: e.transpose(out=tpb[:, cc * 128:(cc + 1) * 128], in_=KTL[:, cc, cs_], identity=identb[:]),
                             reads=[f'KTL{cc}'], writes=['tpb'])
                    P.op('dve', lambda e: e.tensor_copy(out=KTLt[:], in_=tpb[:, 0:512]), writes=['tpb', 'KTLt'])
                    for cc in range(4):
                        P.op('pe', lambda e, cc=cc: e.transpose(out=tpb[:, cc * 128:(cc + 1) * 128], in_=BTL[:, cc, cs_], identity=identb[:]),
                             reads=[f'BTL{cc}'], writes=['tpb'])
                    P.op('act', lambda e: e.copy(out=BTLt[:], in_=tpb[:, 0:512]), writes=['tpb', 'BTLt'])
                    for h in range(8):
                        cc, po = h // 2, 64 * (h % 2)
                        bank, bk = (sA, 'sA') if h % 2 == 0 else (sB, 'sB')
                        P.op('pe', lambda e, cc=cc, po=po, bank=bank: e.matmul(bank[:, 0:256], lhsT=BT[po:po + 64, cc, cs_], rhs=AR[po:po + 64, cc, ch, :], start=True, stop=True),
                             reads=[f'BT{cc}', f'AR{cc}'], writes=[bk])
                        P.op('pe', lambda e, cc=cc, po=po, bank=bank: e.matmul(bank[:, 256:512], lhsT=KT[po:po + 64, cc, cs_], rhs=AR[po:po + 64, cc, ch, :], start=True, stop=True),
                             reads=[f'KT{cc}', f'AR{cc}'], writes=[bk])
                        P.op('dve', lambda e, h=h, bank=bank: e.tensor_tensor(out=Sc[:, h, :], in0=bank[:], in1=mask4, op=ALU.mult), writes=[bk, f'Sc{h}'])
                    for g in range(2):
                        bank, bk = (sA, 'sA') if g == 0 else (sB, 'sB')
                        po = 64 * g
                        for hh in range(4):
                            h = 2 * hh + g
                            cc = h // 2
                            P.op('pe', lambda e, cc=cc, po=po, bank=bank, hh=hh: e.matmul(bank[:, hh * 128:(hh + 1) * 128], lhsT=AR[po:po + 64, cc, ch, 0:128], rhs=BT[po:po + 64, cc, cs_],
                                                                                          start=True, stop=True), reads=[f'BT{cc}', f'AR{cc}'], writes=[bk])
                        P.op('dve', lambda e, g=g, bank=bank: e.tensor_tensor(out=Nm[0][:, g::2, :], in0=bank[:].rearrange("p (h t) -> p h t", t=128),
                                                                              in1=maskN4.rearrange("p (h t) -> p h t", t=128), op=ALU.mult), writes=[bk, 'Nm0_0', 'Nm0_1'])
                    for g in range(2):
                        P.op('pool', lambda e, g=g: e.tensor_tensor(out=Pm[0][:, 4 * g:4 * g + 4, :], in0=Sc[:, 4 * g:4 * g + 4, 0:128],
                                                                    in1=identb[:].unsqueeze(1).to_broadcast([128, 4, 128]), op=ALU.add),
                             reads=[f'Sc{4 * g + i}' for i in range(4)], writes=[f'Pm0_{g}'])
                    for lev in range(6):
                        ci, ni = lev % 2, (lev + 1) % 2
                        for g in range(2):
                            hs = [4 * g + i for i in range(4)]
                            if lev == 0:
                                Acur = lambda h: Sc[:, h, 0:128]
                                akeys = [f'Sc{h}' for h in hs]
                            else:
                                Acur = lambda h, ci=ci: Am[ci][:, h, :]
                                akeys = [f'Am{ci}_{g}']
                            nkeys = [f'Nm{ci}_{g}']
                            if lev < 5:
                                for hh, h in enumerate(hs):
                                    P.op('pe', lambda e, hh=hh, h=h, Acur=Acur, ci=ci: e.matmul(sA[:, hh * 128:(hh + 1) * 128], lhsT=Nm[ci][:, h, :], rhs=Acur(h), start=True, stop=True),
                                         reads=akeys + nkeys, writes=['sA'])
                                P.op('act', lambda e, g=g, ni=ni: e.copy(out=Am[ni][:, 4 * g:4 * g + 4, :], in_=sA[:].rearrange("p (h t) -> p h t", t=128)),
                                     writes=['sA', f'Am{ni}_{g}'])
                            for hh, h in enumerate(hs):
                                P.op('pe', lambda e, hh=hh, h=h, Acur=Acur, ci=ci: e.matmul(sB[:, hh * 128:(hh + 1) * 128], lhsT=Acur(h), rhs=Nm[ci][:, h, :], start=True, stop=True),
                                     reads=akeys + nkeys, writes=['sB'])
                            P.op('dve', lambda e, g=g, ni=ni: e.tensor_copy(out=Nm[ni][:, 4 * g:4 * g + 4, :], in_=sB[:].rearrange("p (h t) -> p h t", t=128)),
                                 writes=['sB', f'Nm{ni}_{g}'])
                            bank, bk = (mm0, 'mm0') if g == 0 else (mm1, 'mm1')
                            for hh, h in enumerate(hs):
                                P.op('pe', lambda e, hh=hh, h=h, ni=ni, ci=ci, bank=bank: e.matmul(bank[:, hh * 128:(hh + 1) * 128], lhsT=Nm[ni][:, h, :], rhs=Pm[ci][:, h, :], start=True, stop=True),
                                     reads=[f'Nm{ni}_{g}', f'Pm{ci}_{g}'], writes=[bk])
                            P.op('dve', lambda e, g=g, ni=ni, ci=ci, bank=bank: e.tensor_tensor(out=Pm[ni][:, 4 * g:4 * g + 4, :], in0=bank[:].rearrange("p (h t) -> p h t", t=128),
                                                                                              in1=Pm[ci][:, 4 * g:4 * g + 4, :], op=ALU.add),
                                 reads=[f'Pm{ci}_{g}'], writes=[bk, f'Pm{ni}_{g}'])
                    Pf = Pm[0]
                    pfk = ['Pm0_0', 'Pm0_1']
                    sck = [f'Sc{h}' for h in range(8)]
                    for h in range(8):
                        cc, po = h // 2, 64 * (h % 2)
                        hs_ = slice(h * 64, (h + 1) * 64)
                        P.op('pe', lambda e, h=h, hs_=hs_: e.matmul(xu[:, hs_], lhsT=Sc[:, h, 256:384], rhs=Vb[:, hs_], start=True, stop=False),
                             reads=[f'Sc{h}', 'Vb'], writes=['xu'])
                        P.op('pe', lambda e, cc=cc, po=po, hs_=hs_: e.matmul(xu[:, hs_], lhsT=AR[po:po + 64, cc, ch, 0:128], rhs=Hb[po:po + 64, l, cc, :], start=False, stop=True),
                             reads=[f'AR{cc}', f'Hb{l}'], writes=['xu'])
                    P.op('act', lambda e: e.copy(out=Xb[:], in_=xu[:]), writes=['xu', 'Xb'])
                    for h in range(8):
                        hs_ = slice(h * 64, (h + 1) * 64)
                        P.op('pe', lambda e, h=h, hs_=hs_: e.matmul(xu[:, hs_], lhsT=Pf[:, h, :], rhs=Xb[:, hs_], start=True, stop=True),
                             reads=pfk + ['Xb'], writes=['xu'])
                    P.op('dve', lambda e: e.tensor_copy(out=Ub[:], in_=xu[:]), writes=['xu', 'Ub'])
                    for h in range(8):
                        cc, po = h // 2, 64 * (h % 2)
                        hs_ = slice(h * 64, (h + 1) * 64)
                        P.op('pe', lambda e, cc=cc, po=po, hs_=hs_: e.matmul(tpf[:, hs_], lhsT=AR[po:po + 64, cc, ch, 128:256], rhs=Hb[po:po + 64, l, cc, :], start=True, stop=False),
                             reads=[f'AR{cc}', f'Hb{l}'], writes=['tpf'])
                        P.op('pe', lambda e, h=h, hs_=hs_: e.matmul(tpf[:, hs_], lhsT=Sc[:, h, 128:256], rhs=Ub[:, hs_], start=False, stop=False),
                             reads=[f'Sc{h}', 'Ub'], writes=['tpf'])
                        P.op('pe', lambda e, h=h, hs_=hs_: e.matmul(tpf[:, hs_], lhsT=Sc[:, h, 384:512], rhs=Vb[:, hs_], start=False, stop=True),
                             reads=[f'Sc{h}', 'Vb'], writes=['tpf'])
                    for h in range(8):
                        cc, po = h // 2, 64 * (h % 2)
                        hs_ = slice(h * 64, (h + 1) * 64)
                        P.op('pe', lambda e, cc=cc, po=po, hs_=hs_: e.matmul(hp[po:po + 64, cc * 64:(cc + 1) * 64], lhsT=BTLt[:, hs_], rhs=Ub[:, hs_], start=True, stop=False),
                             reads=['BTLt', 'Ub'], writes=['hp'])
                        P.op('pe', lambda e, cc=cc, po=po, hs_=hs_: e.matmul(hp[po:po + 64, cc * 64:(cc + 1) * 64], lhsT=KTLt[:, hs_], rhs=Vb[:, hs_], start=False, stop=True),
                             reads=['KTLt', 'Vb'], writes=['hp'])
                    P.op('dve', lambda e: e.tensor_tensor(out=Hm[:, l, :, :], in0=Hm[:, l, :, :], in1=dCt[:, ch, :].unsqueeze(2).to_broadcast([128, 4, 64]), op=ALU.mult),
                         reads=['dCt', f'Hb{l}'], writes=[f'Hm{l}'])
                    P.op('dve', lambda e: e.tensor_tensor(out=Hm[:, l, :, :], in0=Hm[:, l, :, :], in1=hp[:, 0:256].rearrange("p (c v) -> p c v", v=64), op=ALU.add),
                         writes=['hp', f'Hm{l}'])
                    P.op('pool', lambda e: e.tensor_copy(out=Hb[:, l, :, :], in_=Hm[:, l, :, :]), reads=[f'Hm{l}'], writes=[f'Hb{l}'])
                    P.op('act', lambda e: e.copy(out=ysb[:], in_=tpf[:]), writes=['tpf', 'ysb'])
                    P.op('pool', lambda e: e.tensor_tensor(out=ysq[:], in0=ysb[:], in1=ysb[:], op=ALU.mult), reads=['ysb'], writes=['ysq'])
                    y3 = lambda a: a.rearrange("p (h v) -> p h v", v=64)
                    P.op('dve', lambda e: e.tensor_reduce(out=gst[:, 0:8], in_=y3(ysb[:]), axis=AX.X, op=ALU.add), reads=['ysb'], writes=['gst0'])
                    P.op('dve', lambda e: e.tensor_reduce(out=gst[:, 8:16], in_=y3(ysq[:]), axis=AX.X, op=ALU.add), reads=['ysq'], writes=['gst1'])
                    P.op('dve', lambda e: e.tensor_scalar(out=gst[:, 16:24], in0=gst[:, 0:8], scalar1=1.0 / 64, scalar2=None, op0=ALU.mult), reads=['gst0'], writes=['gst2'])
                    P.op('pool', lambda e: e.tensor_tensor(out=gst[:, 24:32], in0=gst[:, 16:24], in1=gst[:, 16:24], op=ALU.mult), reads=['gst2'], writes=['gst3'])
                    P.op('dve', lambda e: e.scalar_tensor_tensor(out=gst[:, 32:40], in0=gst[:, 8:16], scalar=1.0 / 64, in1=gst[:, 24:32], op0=ALU.mult, op1=ALU.subtract),
                         reads=['gst1', 'gst3'], writes=['gst4'])
                    P.op('act', lambda e: e.activation(out=gst[:, 24:32], in_=gst[:, 32:40], func=AF.Ln, bias=GN_EPS), reads=['gst4'], writes=['gst3'])
                    P.op('act', lambda e: e.activation(out=gst[:, 24:32], in_=gst[:, 24:32], func=AF.Exp, scale=-0.5), writes=['gst3'])
                    P.op('dve', lambda e: e.tensor_tensor(out=y3(ync[:]), in0=y3(ysb[:]), in1=gst[:, 16:24].unsqueeze(2).to_broadcast([128, 8, 64]), op=ALU.subtract),
                         reads=['ysb', 'gst2'], writes=['ync'])
                    P.op('pool', lambda e: e.tensor_tensor(out=y3(ynb[:]), in0=y3(ync[:]), in1=gst[:, 24:32].unsqueeze(2).to_broadcast([128, 8, 64]), op=ALU.mult),
                         reads=['ync', 'gst3'], writes=['ynb'])
                    for cc in range(4):
                        P.op('pe', lambda e, cc=cc: e.transpose(out=tpb[:, cc * 128:(cc + 1) * 128], in_=ynb[:, cc * 128:(cc + 1) * 128], identity=identb[:]),
                             reads=['ynb'], writes=['tpb'])
                    for cc in range(4):
                        b = cc % 2
                        P.op('dve', lambda e, cc=cc, b=b: e.tensor_scalar(out=otmp[b][:], in0=tpb[:, cc * 128:(cc + 1) * 128], scalar1=V(l, V_LG + cc), scalar2=V(l, V_LB + cc),
                                                                          op0=ALU.mult, op1=ALU.add), writes=['tpb', f'otmp{b}'])
                        P.op('pool', lambda e, cc=cc, b=b: e.tensor_tensor(out=otmp[b][:], in0=otmp[b][:], in1=bonus[:, cc, cs_], op=ALU.add),
                             reads=[f'bonus{cc}'], writes=[f'otmp{b}'])
                        P.op('dve', lambda e, cc=cc, b=b: e.tensor_tensor(out=ycat[:, 4 + cc, cs_], in0=otmp[b][:], in1=gsb[:, cc, cs_], op=ALU.mult),
                             reads=[f'otmp{b}', f'gsb{cc}'], writes=[f'ycat{4 + cc}'])

                if KSTOP < 5:
                    break
                for half in range(2):
                    slot = use_slab(si); si += 1
                    for jj in range(4):
                        j = half * 4 + jj
                        bank, bk = mmbank()
                        dense(bank, bk, slot, 512, jj * 128, ycat, ykeys, NB)
                        P.op('dve', lambda e, j=j, bank=bank: e.scalar_tensor_tensor(out=xT[:, j, :], in0=bank[:, 0:T], scalar=DER(l, DG1 + j), in1=xT[:, j, :], op0=ALU.mult, op1=ALU.add),
                             writes=[bk, f'xT{j}'])
                layernorm(l, V_L1G, V_L1B, DA2, DB2, l)
                if KSTOP < 6:
                    break
                for s in range(11):
                    slot = use_slab(si); si += 1
                    for qq in range(2):
                        for kc in range(NB):
                            P.op('pe', lambda e, kc=kc, qq=qq, slot=slot: e.matmul(mm0[:, 0:T], lhsT=ring[:, slot, kc * 512 + qq * 128: kc * 512 + qq * 128 + 128], rhs=hT[:, kc, :],
                                                                                   start=(kc == 0), stop=(kc == NB - 1)), reads=[f'ring{slot}'] + hkeys, writes=['mm0'])
                        for kc in range(NB):
                            P.op('pe', lambda e, kc=kc, qq=qq, slot=slot: e.matmul(mm1[:, 0:T], lhsT=ring[:, slot, kc * 512 + 256 + qq * 128: kc * 512 + 256 + qq * 128 + 128], rhs=hT[:, kc, :],
                                                                                   start=(kc == 0), stop=(kc == NB - 1)), reads=[f'ring{slot}'] + hkeys, writes=['mm1'])
                        b = qq
                        P.op('act', lambda e, b=b: e.activation(out=slt[b][:], in_=mm0[:, 0:T], func=AF.Silu), writes=['mm0', f'slt{b}'])
                        P.op('dve', lambda e, b=b, s=s, qq=qq: e.tensor_tensor(out=act[:, 2 * s + qq, :], in0=mm1[:, 0:T], in1=slt[b][:], op=ALU.mult),
                             reads=[f'slt{b}'], writes=['mm1', f'act{2 * s + qq}'])
                akeys_ = [f'act{f}' for f in range(NF)]
                for j in range(NB):
                    slot = use_slab(si); si += 1
                    bank, bk = mmbank()
                    dense(bank, bk, slot, 128, 0, act, akeys_, NF)
                    P.op('dve', lambda e, j=j, bank=bank: e.scalar_tensor_tensor(out=xT[:, j, :], in0=bank[:, 0:T], scalar=DER(l, DG2 + j), in1=xT[:, j, :], op0=ALU.mult, op1=ALU.add),
                         writes=[bk, f'xT{j}'])
                layernorm(l, V_L2G, V_L2B, DA1, DB1, (l + 1) if l + 1 < L else None)
            P.dma('yout', yT_d[:, t0:t0 + T].rearrange("(j p) t -> p j t", p=128), xT[:], reads=[f'xT{j}' for j in range(NB)])
        P._wait('sp', ('yout', P.chan_count['yout']))
        P.emit()
        build_nc.stats = (P.nops, P.nwaits)
    return nc


def _slab_std(W, c0, ncols):
    blk = W.reshape(8, 128, W.shape[1])[:, :, c0:c0 + ncols].transpose(1, 0, 2).reshape(128, 8 * ncols)
    out = np.zeros((128, SLAB), np.float32)
    out[:, :8 * ncols] = blk
    return out


def _vec_cols(v):
    v = np.asarray(v, np.float32).reshape(-1, 128)
    return v.T


def prepare_shared(inp, L):
    wslab = np.zeros((L * NSLAB, 128, SLAB), np.float32)
    wmod = np.zeros((L * NMOD, 128, SLAB), np.float32)
    vecs = np.zeros((128, L, NV), np.float32)
    lw = np.zeros((128, L, 512), np.float32)
    gu = np.zeros((128, L, 512), np.float32)
    for l in range(L):
        Win = np.asarray(inp["w_in"][l], np.float32)
        s = l * NSLAB
        for k, c0 in enumerate((1024, 512, 0, 1536, 2048, 2560)):
            wslab[s + k] = _slab_std(Win, c0, 512)
        wslab[s + 6] = _slab_std(Win, 3072, 256)
        Wout = np.asarray(inp["w_out"][l], np.float32)
        wslab[s + 7] = _slab_std(Wout, 0, 512)
        wslab[s + 8] = _slab_std(Wout, 512, 512)
        W1 = np.asarray(inp["w_ffn_in"][l], np.float32)
        W1r = W1.reshape(8, 128, 2 * DFF)
        for k in range(11):
            g = W1r[:, :, 256 * k:256 * k + 256]
            u = W1r[:, :, DFF + 256 * k:DFF + 256 * k + 256]
            wslab[s + 9 + k] = np.concatenate([g, u], axis=2).transpose(1, 0, 2).reshape(128, SLAB)
        W2 = np.asarray(inp["w_ffn_out"][l], np.float32).reshape(NF, 128, D)
        for j in range(8):
            wslab[s + 20 + j, :, :NF * 128] = W2[:, :, j * 128:(j + 1) * 128].transpose(1, 0, 2).reshape(128, NF * 128)
        Wm = np.asarray(inp["w_mod"][l], np.float32)
        for k in range(NMOD):
            wmod[l * NMOD + k] = _slab_std(Wm, 512 * k, 512)
        mu = np.asarray(inp["mu_shift"][l], np.float32)
        cols = [_vec_cols(mu)]
        for nm in ("w0", "a0", "k_k", "k_a"):
            cols.append(_vec_cols(inp[nm][l]))
        cols.append(_vec_cols(np.asarray(inp["r_k"][l]).reshape(-1)))
        cols.append(_vec_cols(inp["lnx_g"][l]))
        cols.append(_vec_cols(inp["lnx_b"][l]))
        cw = np.asarray(inp["conv_w"][l], np.float32)
        for tap in range(3):
            cols.append(_vec_cols(cw[tap]))
        for nm in ("ln1_g", "ln1_b", "ln2_g", "ln2_b"):
            cols.append(_vec_cols(inp[nm][l]))
        cols.append(_vec_cols(inp["b_mod"][l]))
        vecs[:, l, :] = np.concatenate(cols, axis=1)
        lw[0:64, l, :] = np.asarray(inp["w_decay_up"][l], np.float32)
        lw[64:128, l, :] = np.asarray(inp["a_up"][l], np.float32)
        gu[:, l, :] = np.asarray(inp["g_up"][l], np.float32)
    consts = np.zeros((128, 1408), np.float32)
    consts[:, 0:128] = np.eye(128)
    s_idx = np.arange(128)[:, None]
    t_idx = np.arange(128)[None, :]
    strict = (s_idx < t_idx).astype(np.float32)
    incl = (s_idx <= t_idx).astype(np.float32)
    consts[:, 128:640] = np.concatenate([strict, incl, strict, incl], axis=1)
    lowN = (t_idx < s_idx).astype(np.float32)
    consts[:, 640:1152] = np.concatenate([lowN] * 4, axis=1)
    consts[:, 1152:1280] = ((s_idx // 64) == (t_idx // 64)).astype(np.float32)
    consts[:, 1280:1408] = 1.0
    return dict(wslab=wslab, wmod=wmod, vecs=np.ascontiguousarray(vecs.reshape(128, L * NV)),
                lw=np.ascontiguousarray(lw.reshape(128, L * 512)), gu=np.ascontiguousarray(gu.reshape(128, L * 512)), consts=consts)


def run(inp, SEQ, L, T, runner):
    x = np.asarray(inp["x"], np.float32)
    c = np.asarray(inp["c"], np.float32)
    B = x.shape[0]
    shared = prepare_shared(inp, L)
    nc = build_nc(SEQ, L, T)
    in_maps = []
    for core in range(8):
        b = core % B
        m = dict(shared)
        m["xT"] = np.ascontiguousarray(x[b].T)
        m["cT"] = np.ascontiguousarray(c[b].reshape(8, 128).T)
        in_maps.append(m)
    res = runner(nc, in_maps)
    out = np.stack([np.ascontiguousarray(res[b]["yT"].T) for b in range(B)], axis=0)
    return out.astype(np.float32)


def kernel(**inputs):
    def runner(nc, in_maps):
        r = run_bass_kernel_spmd(nc, in_maps, core_ids=list(range(8)))
        return r.results
    return run(inputs, 8192, 4, 256, runner)
```

```python
import math
import os
KSTOP = int(os.environ.get('KSTOP', '99'))
import numpy as np
from contextlib import ExitStack
import concourse.bass as bass
import concourse.mybir as mybir
from concourse.bass_utils import run_bass_kernel_spmd

F32 = mybir.dt.float32
BF16 = mybir.dt.bfloat16
AF = mybir.ActivationFunctionType
ALU = mybir.AluOpType
AX = mybir.AxisListType

D = 1024
NB = 8
DFF = 2816
NF = 22
NSLAB = 28
NMOD = 12
NV = 134
SLAB = 4096
ALPHA = (2.0 * 4) ** 0.25
LN_EPS = 1e-5
GN_EPS = 64e-5
CDEC = math.exp(-0.5)

V_MU = 0
V_W0 = 14
V_A0 = 18
V_KK = 22
V_KA = 26
V_RK = 30
V_LG = 34
V_LB = 38
V_CW = 42
V_L1G = 54
V_L1B = 62
V_L2G = 70
V_L2B = 78
V_BM = 86
DB1, DA1, DG1, DB2, DA2, DG2 = 0, 8, 16, 24, 32, 40


class _Rec:
    def __getattr__(self, name):
        return lambda *a, **k: (name, a, k)


_REC = _Rec()


class Prog:
    EPOCH = 16000

    def __init__(self, nc, stack):
        self.nc = nc
        self.stack = stack
        self.engs = ('sp', 'pe', 'act', 'dve', 'pool')
        self.streams = {e: [] for e in self.engs}
        self.count = {e: 0 for e in self.engs}
        self.clock = {e: {} for e in self.engs}
        self.iclock = {}
        self.sems = {}
        self.last_write = {}
        self.reads = {}
        self.chan_count = {}
        self.nwaits = 0
        self.nops = 0

    def _sem(self, name):
        if name not in self.sems:
            self.sems[name] = self.stack.enter_context(self.nc.semaphore(name.replace(':', '_')))
        return self.sems[name]

    def _wait(self, eng, dep, force=False):
        e2, n = dep
        if n <= 0:
            return
        if not force and self.clock[eng].get(e2, 0) >= n:
            return
        if e2 in self.engs:
            ep = (n - 1) // self.EPOCH
            sem = self._sem(f"{e2}:{ep}")
            val = n - ep * self.EPOCH
        else:
            sem = self._sem(e2)
            val = 16 * n
        self.streams[eng].append(('wait', sem, val))
        self.nwaits += 1
        ck = self.clock[eng]
        for k, v in self.iclock.get(dep, {}).items():
            if ck.get(k, 0) < v:
                ck[k] = v
        if ck.get(e2, 0) < n:
            ck[e2] = n

    def _deps(self, eng, ident, reads, writes):
        deps = set()
        for k in reads:
            lw = self.last_write.get(k)
            if lw is not None:
                deps.add(lw)
        for k in writes:
            lw = self.last_write.get(k)
            if lw is not None:
                deps.add(lw)
            for r in self.reads.get(k, ()):
                deps.add(r)
        for d in sorted(deps):
            e2, n = d
            if e2 == eng:
                if eng != 'pe' and eng != 'sp':
                    self._wait(eng, d)
                continue
            self._wait(eng, d)
        for k in reads:
            self.reads.setdefault(k, []).append(ident)
        for k in writes:
            self.last_write[k] = ident
            self.reads[k] = []

    def op(self, eng, fn, reads=(), writes=(), inc=True):
        n = self.count[eng] + 1
        ident = (eng, n)
        self._deps(eng, ident, reads, writes)
        self.nops += 1
        if not inc:
            self.streams[eng].append(('op', fn(_REC), None, 0))
            return
        self.count[eng] = n
        ep = (n - 1) // self.EPOCH
        sem = self._sem(f"{eng}:{ep}")
        self.streams[eng].append(('op', fn(_REC), sem, 1))
        self.iclock[ident] = dict(self.clock[eng])

    def dma(self, chan, out, in_, reads=(), writes=(), eng='sp'):
        n = self.chan_count.get(chan, 0) + 1
        ident = (chan, n)
        self._deps(eng, ident, reads, writes)
        self.chan_count[chan] = n
        sem = self._sem(chan)
        self.streams[eng].append(('op', ('dma_start', (), dict(out=out, in_=in_)), sem, 16))
        self.iclock[ident] = dict(self.clock[eng])
        self.nops += 1

    def barrier(self):
        for e in self.engs:
            for e2 in self.engs:
                if e2 != e and e2 != 'sp':
                    self._wait(e, (e2, self.count[e2]))
            for ch, n in self.chan_count.items():
                self._wait(e, (ch, n))
        self.last_write = {}
        self.reads = {}

    def emit(self):
        with self.nc.Block() as block:
            for ename, deco in (('sp', block.sync), ('pe', block.tensor), ('act', block.scalar),
                                ('dve', block.vector), ('pool', block.gpsimd)):
                stream = self.streams[ename]

                def body(e, stream=stream):
                    for it in stream:
                        if it[0] == 'wait':
                            e.wait_ge(it[1], it[2])
                        else:
                            name, a, k = it[1]
                            ins = getattr(e, name)(*a, **k)
                            if it[2] is not None:
                                ins.then_inc(it[2], it[3])
                deco(body)
        self.streams = {e: [] for e in self.engs}


def build_nc(SEQ, L, T):
    NCH = T // 128
    NT = SEQ // T
    nc = bass.Bass("TRN2", target_bir_lowering=False)
    xT_d = nc.dram_tensor("xT", [D, SEQ], F32, kind="ExternalInput").ap()
    cT_d = nc.dram_tensor("cT", [128, 8], F32, kind="ExternalInput").ap()
    vecs_d = nc.dram_tensor("vecs", [128, L * NV], F32, kind="ExternalInput").ap()
    wslab_d = nc.dram_tensor("wslab", [L * NSLAB, 128, SLAB], F32, kind="ExternalInput").ap()
    wmod_d = nc.dram_tensor("wmod", [L * NMOD, 128, SLAB], F32, kind="ExternalInput").ap()
    lw_d = nc.dram_tensor("lw", [128, L * 512], F32, kind="ExternalInput").ap()
    gu_d = nc.dram_tensor("gu", [128, L * 512], F32, kind="ExternalInput").ap()
    consts_d = nc.dram_tensor("consts", [128, 1408], F32, kind="ExternalInput").ap()
    yT_d = nc.dram_tensor("yT", [D, SEQ], F32, kind="ExternalOutput").ap()
    wsb_d = nc.dram_tensor("wsb", [L * NSLAB, 128, SLAB], BF16, kind="Internal").ap()

    with ExitStack() as st:
        P = Prog(nc, st)

        def sb(name, shape, dt, stack=st):
            return stack.enter_context(nc.sbuf_tensor("s_" + name, shape, dt))

        def ps(name, shape, dt):
            return st.enter_context(nc.psum_tensor("p_" + name, shape, dt))

        mm0 = ps("mm0", [128, 512], F32)
        mm1 = ps("mm1", [128, 512], F32)
        sA = ps("sA", [128, 512], F32)
        sB = ps("sB", [128, 512], F32)
        tpf = ps("tpf", [128, 512], F32)
        tpb = ps("tpb", [128, 1024], BF16)
        xu = ps("xu", [128, 512], F32)
        hp = ps("hp", [128, 512], F32)

        consts = sb("consts", [128, 1408], F32)
        ident = consts[:, 0:128]
        mask4 = consts[:, 128:640]
        maskN4 = consts[:, 640:1152]
        bones = consts[:, 1152:1280]
        ones = consts[:, 1280:1408]
        identb = sb("identb", [128, 128], BF16)
        onesb = sb("onesb", [128, 128], BF16)
        vecs = sb("vecs", [128, L * NV], F32)
        cT = sb("cT", [128, 8], F32)
        scT = sb("scT", [128, 8], F32)
        mod = sb("mod", [128, L * 48], F32)
        der = sb("der", [128, L * 48], F32)
        omka = sb("omka", [128, L * 4], F32)
        lwb = sb("lwb", [128, L * 512], BF16)
        gub = sb("gub", [128, L * 512], BF16)
        Hm = sb("Hm", [128, L, 4, 64], F32)
        Hb = sb("Hb", [128, L, 4, 64], BF16)
        hist = sb("hist", [128, L * 14], F32)
        zh = sb("zh", [128, L, 4, 2], F32)

        def V(l, col, n=1):
            return vecs[:, l * NV + col: l * NV + col + n]

        def DER(l, col, n=1):
            return der[:, l * 48 + col: l * 48 + col + n]

        with ExitStack() as pst:
            stg = [sb(f"stg{i}", [128, SLAB], F32, pst) for i in range(2)]
            stb = [sb(f"stb{i}", [128, SLAB], BF16, pst) for i in range(2)]
            P.dma('c0', consts[:], consts_d[:, :], writes=['consts'])
            P.dma('c1', vecs[:], vecs_d[:, :], writes=['vecs'])
            P.dma('c2', cT[:], cT_d[:, :], writes=['cT'])
            P.op('dve', lambda e: e.tensor_copy(out=identb[:], in_=ident), reads=['consts'], writes=['identb'])
            P.op('dve', lambda e: e.tensor_copy(out=onesb[:], in_=ones), reads=['consts'], writes=['onesb'])
            P.op('act', lambda e: e.activation(out=scT[:], in_=cT[:], func=AF.Silu), reads=['cT'], writes=['scT'])
            P.op('pool', lambda e: e.memset(Hm[:], 0.0), writes=['Hm'])
            P.op('pool', lambda e: e.memset(Hb[:], 0.0), writes=['Hb'])
            P.op('pool', lambda e: e.memset(hist[:], 0.0), writes=['hist'])
            P.op('pool', lambda e: e.memset(zh[:], 0.0), writes=['zh'])
            P.dma('stg0', stg[0][:, 0:L * 512], lw_d[:, :], writes=['stg0'])
            P.op('dve', lambda e: e.tensor_copy(out=lwb[:], in_=stg[0][:, 0:L * 512]), reads=['stg0'], writes=['lwb'])
            P.dma('stg1', stg[1][:, 0:L * 512], gu_d[:, :], writes=['stg1'])
            P.op('dve', lambda e: e.tensor_copy(out=gub[:], in_=stg[1][:, 0:L * 512]), reads=['stg1'], writes=['gub'])
            q = 0
            for l in range(L):
                for s in range(NMOD):
                    b = q % 2
                    q += 1
                    P.dma(f'stg{b}', stg[b][:], wmod_d[l * NMOD + s, :, :], writes=[f'stg{b}'])
                    for jj in range(4):
                        col = l * 48 + s * 4 + jj
                        for kc in range(8):
                            P.op('pe', lambda e, b=b, jj=jj, kc=kc, col=col: e.matmul(
                                hp[:, col:col + 1], lhsT=stg[b][:, kc * 512 + jj * 128: kc * 512 + jj * 128 + 128],
                                rhs=scT[:, kc:kc + 1], start=(kc == 0), stop=(kc == 7)),
                                reads=[f'stg{b}', 'scT'], writes=['hp'], inc=(kc == 7))
            for l in range(L):
                P.op('dve', lambda e, l=l: e.tensor_tensor(out=mod[:, l * 48:(l + 1) * 48], in0=hp[:, l * 48:(l + 1) * 48],
                                                          in1=V(l, V_BM, 48), op=ALU.add),
                     reads=['vecs'], writes=['hp', 'mod'])
                for (dst, src, kind) in ((DB1, 0, 'c'), (DA1, 8, 'p1'), (DG1, 16, 'g'),
                                         (DB2, 24, 'c'), (DA2, 32, 'p1'), (DG2, 40, 'g')):
                    o = DER(l, dst, 8)
                    i = mod[:, l * 48 + src: l * 48 + src + 8]
                    if kind == 'c':
                        P.op('pool', lambda e, o=o, i=i: e.tensor_copy(out=o, in_=i), reads=['mod'], writes=['der'])
                    elif kind == 'p1':
                        P.op('pool', lambda e, o=o, i=i: e.tensor_scalar_add(out=o, in0=i, scalar1=1.0), reads=['mod'], writes=['der'])
                    else:
                        P.op('pool', lambda e, o=o, i=i: e.tensor_scalar(out=o, in0=i, scalar1=1.0, scalar2=1.0 / ALPHA,
                                                                        op0=ALU.add, op1=ALU.mult), reads=['mod'], writes=['der'])
                P.op('pool', lambda e, l=l: e.tensor_scalar(out=omka[:, l * 4:(l + 1) * 4], in0=V(l, V_KA, 4), scalar1=-1.0, scalar2=1.0,
                                                            op0=ALU.mult, op1=ALU.add), reads=['vecs'], writes=['omka'])
            cast_engs = ('dve', 'pool', 'act')
            for i in range(L * NSLAB):
                b = q % 2
                q += 1
                P.dma(f'stg{b}', stg[b][:], wslab_d[i, :, :], writes=[f'stg{b}'])
                ce = cast_engs[i % 3]
                if ce == 'act':
                    P.op('act', lambda e, b=b: e.copy(out=stb[b][:], in_=stg[b][:]), reads=[f'stg{b}'], writes=[f'stb{b}'])
                else:
                    P.op(ce, lambda e, b=b: e.tensor_copy(out=stb[b][:], in_=stg[b][:]), reads=[f'stg{b}'], writes=[f'stb{b}'])
                P.dma(f'stb{b}', wsb_d[i, :, :], stb[b][:], reads=[f'stb{b}'])
            P.barrier()
            P.emit()

        NSLOT = 4
        ring = sb("ring", [128, NSLOT, SLAB], BF16)
        xT = sb("xT", [128, NB, T], F32)
        hT = sb("hT", [128, NB, T], BF16)
        ycat = sb("ycat", [128, NB, T], BF16)
        zbuf = sb("zbuf", [128, 4, T + 2], F32)
        usb = sb("usb", [128, 4, T], F32)
        rT = sb("rT", [128, 4, T], F32)
        kT = sb("kT", [128, 4, T], F32)
        vT = sb("vT", [128, 4, T], F32)
        gsb = sb("gsb", [128, 4, T], F32)
        bonus = sb("bonus", [128, 4, T], F32)
        praw = [sb(f"praw{i}", [128, T + 1], F32) for i in range(2)]
        dtmp = [sb(f"dtmp{i}", [128, T], F32) for i in range(2)]
        waf = sb("waf", [128, T], F32)
        glf = sb("glf", [128, T], F32)
        tw = sb("tw", [128, T], BF16)
        sgT = sb("sgT", [128, T], BF16)
        tnames = ['sg', 'aa', 'cs', 'csx', 'eP', 'eN', 'eX', 'eT', 'kkr', 'sq', 'kk', 't1', 'kmod', 'bvec', 'prod']
        tmp = {n: sb("t_" + n, [128, T], F32) for n in tnames}
        nb = sb("nb", [128, NCH], F32)
        dCt = sb("dCt", [128, NCH, 4], F32)
        AR = sb("AR", [128, 4, NCH, 256], BF16)
        KT = sb("KT", [128, 4, T], BF16)
        BT = sb("BT", [128, 4, T], BF16)
        KTL = sb("KTL", [128, 4, T], BF16)
        BTL = sb("BTL", [128, 4, T], BF16)
        Vf = sb("Vf", [128, 512], F32)
        Vb = sb("Vb", [128, 512], BF16)
        KTLt = sb("KTLt", [128, 512], BF16)
        BTLt = sb("BTLt", [128, 512], BF16)
        Sc = sb("Sc", [128, 8, 512], BF16)
        Nm = [sb(f"Nm{i}", [128, 8, 128], BF16) for i in range(2)]
        Am = [sb(f"Am{i}", [128, 8, 128], BF16) for i in range(2)]
        Pm = [sb(f"Pm{i}", [128, 8, 128], BF16) for i in range(2)]
        Xb = sb("Xb", [128, 512], BF16)
        Ub = sb("Ub", [128, 512], BF16)
        ysb = sb("ysb", [128, 512], F32)
        ysq = sb("ysq", [128, 512], F32)
        ync = sb("ync", [128, 512], F32)
        ynb = sb("ynb", [128, 512], BF16)
        gst = sb("gst", [128, 40], F32)
        otmp = [sb(f"otmp{i}", [128, 128], F32) for i in range(2)]
        zb = [sb(f"zb{i}", [128, T], BF16) for i in range(2)]
        zq = [sb(f"zq{i}", [128, T], BF16) for i in range(2)]
        lnm = sb("lnm", [128, T], F32)
        lnv = sb("lnv", [128, T], F32)
        lnr = sb("lnr", [128, T], F32)
        act = sb("act", [128, NF, T], BF16)
        slt = [sb(f"slt{i}", [128, T], F32) for i in range(2)]

        total_slabs = NT * L * NSLAB
        state = {'loaded': 0, 'mmq': 0}

        def ensure_loaded(upto):
            while state['loaded'] <= min(upto, total_slabs - 1):
                i = state['loaded']
                slot = i % NSLOT
                P.dma(f'ring{slot}', ring[:, slot, :], wsb_d[i % (L * NSLAB), :, :], writes=[f'ring{slot}'])
                state['loaded'] += 1

        def use_slab(i):
            ensure_loaded(i + NSLOT - 1)
            return i % NSLOT

        rot = [(mm0, 'mm0'), (mm1, 'mm1'), (sA, 'sA'), (sB, 'sB'), (xu, 'xu'), (hp, 'hp')]

        def mmbank():
            state['mmq'] += 1
            return rot[state['mmq'] % len(rot)]

        def dense(bank, bkey, slot, stride, off, rhs_t, rkeys, nk):
            for kc in range(nk):
                P.op('pe', lambda e, kc=kc: e.matmul(bank[:, 0:T], lhsT=ring[:, slot, kc * stride + off: kc * stride + off + 128],
                                                     rhs=rhs_t[:, kc, :], start=(kc == 0), stop=(kc == nk - 1)),
                     reads=[f'ring{slot}'] + rkeys, writes=[bkey], inc=(kc == nk - 1))

        def layernorm(l, gcol, bcol, Acol, Bcol, lnext):
            for j in range(NB):
                b = j % 2
                P.op('pool', lambda e, j=j, b=b: e.tensor_copy(out=zb[b][:], in_=xT[:, j, :]), reads=[f'xT{j}'], writes=[f'zb{b}'])
                P.op('act', lambda e, j=j, b=b: e.activation(out=zq[b][:], in_=xT[:, j, :], func=AF.Square), reads=[f'xT{j}'], writes=[f'zq{b}'])
                P.op('pe', lambda e, j=j, b=b: e.matmul(mm0[:, 0:T], lhsT=onesb[:], rhs=zb[b][:], start=(j == 0), stop=(j == NB - 1)),
                     reads=[f'zb{b}'], writes=['mm0'])
                P.op('pe', lambda e, j=j, b=b: e.matmul(mm1[:, 0:T], lhsT=onesb[:], rhs=zq[b][:], start=(j == 0), stop=(j == NB - 1)),
                     reads=[f'zq{b}'], writes=['mm1'])
            P.op('act', lambda e: e.activation(out=lnm[:], in_=mm0[:, 0:T], func=AF.Copy, scale=1.0 / D), writes=['mm0', 'lnm'])
            P.op('act', lambda e: e.activation(out=lnv[:], in_=mm0[:, 0:T], func=AF.Square, scale=1.0 / D), writes=['mm0', 'lnv'])
            P.op('dve', lambda e: e.scalar_tensor_tensor(out=lnv[:], in0=mm1[:, 0:T], scalar=1.0 / D, in1=lnv[:], op0=ALU.mult, op1=ALU.subtract),
                 writes=['mm1', 'lnv'])
            P.op('act', lambda e: e.activation(out=lnr[:], in_=lnv[:], func=AF.Ln, bias=LN_EPS / (ALPHA * ALPHA)), reads=['lnv'], writes=['lnr'])
            P.op('act', lambda e: e.activation(out=lnr[:], in_=lnr[:], func=AF.Exp, scale=-0.5), writes=['lnr'])
            for j in range(NB):
                P.op('dve', lambda e, j=j: e.tensor_tensor(out=xT[:, j, :], in0=xT[:, j, :], in1=lnm[:], op=ALU.subtract),
                     reads=['lnm'], writes=[f'xT{j}'])
                P.op('pool', lambda e, j=j: e.tensor_tensor(out=xT[:, j, :], in0=xT[:, j, :], in1=lnr[:], op=ALU.mult),
                     reads=['lnr'], writes=[f'xT{j}'])
                P.op('act', lambda e, j=j: e.activation(out=xT[:, j, :], in_=xT[:, j, :], func=AF.Identity,
                                                        scale=V(l, gcol + j), bias=V(l, bcol + j)), writes=[f'xT{j}'])
                if lnext is not None:
                    P.op('act', lambda e, j=j: e.activation(out=hT[:, j, :], in_=xT[:, j, :], func=AF.Identity,
                                                            scale=DER(lnext, Acol + j), bias=DER(lnext, Bcol + j)),
                         reads=[f'xT{j}'], writes=[f'hT{j}'])

        hkeys = [f'hT{j}' for j in range(NB)]
        ykeys = [f'ycat{j}' for j in range(NB)]
        si = 0
        for ti in range(NT):
            t0 = ti * T
            P.dma('xin', xT[:], xT_d[:, t0:t0 + T].rearrange("(j p) t -> p j t", p=128), writes=[f'xT{j}' for j in range(NB)])
            for j in range(NB):
                P.op('act', lambda e, j=j: e.activation(out=hT[:, j, :], in_=xT[:, j, :], func=AF.Identity,
                                                        scale=DER(0, DA1 + j), bias=DER(0, DB1 + j)),
                     reads=[f'xT{j}'], writes=[f'hT{j}'])
            for l in range(L):
                if KSTOP < 1:
                    break
                slot = use_slab(si); si += 1
                for j in range(4):
                    bank, bk = mmbank()
                    dense(bank, bk, slot, 512, j * 128, hT, hkeys, NB)
                    P.op('act', lambda e, j=j, bank=bank: e.copy(out=usb[:, j, :], in_=bank[:, 0:T]), writes=[bk, f'usb{j}'])
                slot = use_slab(si); si += 1
                for j in range(4):
                    bank, bk = mmbank()
                    dense(bank, bk, slot, 512, j * 128, hT, hkeys, NB)
                    P.op('pool', lambda e, j=j: e.tensor_copy(out=zbuf[:, j, 0:2], in_=zh[:, l, j, :]), reads=[f'zh{l}_{j}'], writes=[f'zbuf{j}'])
                    P.op('dve', lambda e, j=j, bank=bank: e.tensor_tensor(out=zbuf[:, j, 2:T + 2], in0=bank[:, 0:T], in1=usb[:, j, :], op=ALU.mult),
                         reads=[f'usb{j}'], writes=[bk, f'zbuf{j}'])
                    P.op('pool', lambda e, j=j: e.tensor_copy(out=zh[:, l, j, :], in_=zbuf[:, j, T:T + 2]), reads=[f'zbuf{j}'], writes=[f'zh{l}_{j}'])
                    P.op('pool', lambda e, j=j: e.tensor_scalar(out=usb[:, j, :], in0=zbuf[:, j, 0:T], scalar1=V(l, V_CW + j), scalar2=None, op0=ALU.mult),
                         reads=[f'zbuf{j}'], writes=[f'usb{j}'])
                    P.op('dve', lambda e, j=j: e.scalar_tensor_tensor(out=usb[:, j, :], in0=zbuf[:, j, 1:T + 1], scalar=V(l, V_CW + 4 + j), in1=usb[:, j, :],
                                                                      op0=ALU.mult, op1=ALU.add), reads=[f'zbuf{j}'], writes=[f'usb{j}'])
                    P.op('dve', lambda e, j=j: e.scalar_tensor_tensor(out=usb[:, j, :], in0=zbuf[:, j, 2:T + 2], scalar=V(l, V_CW + 8 + j), in1=usb[:, j, :],
                                                                       op0=ALU.mult, op1=ALU.add), reads=[f'zbuf{j}'], writes=[f'usb{j}'])
                slot = use_slab(si); si += 1
                for j in range(4):
                    bank, bk = mmbank()
                    dense(bank, bk, slot, 512, j * 128, hT, hkeys, NB)
                    P.op('dve', lambda e, j=j, bank=bank: e.tensor_tensor(out=ycat[:, j, :], in0=bank[:, 0:T], in1=usb[:, j, :], op=ALU.mult),
                         reads=[f'usb{j}'], writes=[bk, f'ycat{j}'])

                if KSTOP < 2:
                    break
                pq = [0]

                def shift_lerp(bank, bk, mucol, dst, dkey):
                    b = pq[0] % 2
                    pq[0] += 1
                    hc = hist[:, l * 14 + mucol: l * 14 + mucol + 1]
                    hk = f'hist{l}_{mucol}'
                    P.op('pool', lambda e: e.tensor_copy(out=praw[b][:, 0:1], in_=hc), reads=[hk], writes=[f'praw{b}'])
                    P.op('act', lambda e: e.copy(out=praw[b][:, 1:T + 1], in_=bank[:, 0:T]), writes=[bk, f'praw{b}'])
                    P.op('pool', lambda e: e.tensor_copy(out=hc, in_=praw[b][:, T:T + 1]), reads=[f'praw{b}'], writes=[hk])
                    P.op('dve', lambda e: e.tensor_tensor(out=dtmp[b][:], in0=praw[b][:, 0:T], in1=praw[b][:, 1:T + 1], op=ALU.subtract),
                         reads=[f'praw{b}'], writes=[f'dtmp{b}'])
                    P.op('dve', lambda e: e.scalar_tensor_tensor(out=dst, in0=dtmp[b][:], scalar=V(l, V_MU + mucol), in1=praw[b][:, 1:T + 1],
                                                                 op0=ALU.mult, op1=ALU.add), reads=[f'dtmp{b}', f'praw{b}'], writes=[dkey])

                for gi, (dstT, nm) in enumerate(((rT, 'rT'), (kT, 'kT'), (vT, 'vT'))):
                    slot = use_slab(si); si += 1
                    for j in range(4):
                        bank, bk = mmbank()
                        dense(bank, bk, slot, 512, j * 128, hT, hkeys, NB)
                        shift_lerp(bank, bk, gi * 4 + j, dstT[:, j, :], f'{nm}{j}')
                slot = use_slab(si); si += 1
                bank, bk = mmbank()
                dense(bank, bk, slot, 256, 0, hT, hkeys, NB)
                shift_lerp(bank, bk, 12, waf[:], 'waf')
                P.op('act', lambda e: e.activation(out=tw[0:64, :], in_=waf[0:64, :], func=AF.Tanh), reads=['waf'], writes=['tw'])
                P.op('pool', lambda e: e.tensor_copy(out=tw[64:128, :], in_=waf[64:128, :]), reads=['waf'], writes=['tw'])
                bank, bk = mmbank()
                dense(bank, bk, slot, 256, 128, hT, hkeys, NB)
                shift_lerp(bank, bk, 13, glf[:], 'glf')
                P.op('act', lambda e: e.activation(out=sgT[:], in_=glf[:], func=AF.Sigmoid), reads=['glf'], writes=['sgT'])

                if KSTOP < 3:
                    break
                tm = tmp
                for cc in range(4):
                    wcol = l * 512 + cc * 128
                    bank, bk = mmbank()
                    P.op('pe', lambda e, bank=bank, wcol=wcol: e.matmul(bank[:, 0:T], lhsT=lwb[0:64, wcol:wcol + 128], rhs=tw[0:64, :], start=True, stop=True),
                         reads=['tw'], writes=[bk])
                    P.op('act', lambda e, bank=bank, cc=cc: e.activation(out=tm['sg'][:], in_=bank[:, 0:T], func=AF.Sigmoid, bias=V(l, V_W0 + cc)),
                         writes=[bk, 'sg'])
                    bank, bk = mmbank()
                    P.op('pe', lambda e, bank=bank, wcol=wcol: e.matmul(bank[:, 0:T], lhsT=lwb[64:128, wcol:wcol + 128], rhs=tw[64:128, :], start=True, stop=True),
                         reads=['tw'], writes=[bk])
                    P.op('act', lambda e, bank=bank, cc=cc: e.activation(out=tm['aa'][:], in_=bank[:, 0:T], func=AF.Sigmoid, bias=V(l, V_A0 + cc)),
                         writes=[bk, 'aa'])
                    bank, bk = mmbank()
                    P.op('pe', lambda e, bank=bank, wcol=wcol: e.matmul(bank[:, 0:T], lhsT=gub[:, wcol:wcol + 128], rhs=sgT[:], start=True, stop=True),
                         reads=['sgT'], writes=[bk])
                    P.op('act', lambda e, bank=bank, cc=cc: e.copy(out=gsb[:, cc, :], in_=bank[:, 0:T]), writes=[bk, f'gsb{cc}'])
                    for ch in range(NCH):
                        cs_ = slice(ch * 128, (ch + 1) * 128)
                        P.op('dve', lambda e, cs_=cs_: e.tensor_tensor_scan(out=tm['cs'][:, cs_], data0=ones, data1=tm['sg'][:, cs_], initial=0.0,
                                                                            op0=ALU.mult, op1=ALU.add), reads=['sg'], writes=['cs'])
                    P.op('pool', lambda e: e.tensor_tensor(out=tm['csx'][:], in0=tm['cs'][:], in1=tm['sg'][:], op=ALU.subtract), reads=['cs', 'sg'], writes=['csx'])
                    P.op('act', lambda e: e.activation(out=tm['eP'][:], in_=tm['cs'][:], func=AF.Exp, scale=-CDEC), reads=['cs'], writes=['eP'])
                    P.op('act', lambda e: e.activation(out=tm['eN'][:], in_=tm['cs'][:], func=AF.Exp, scale=CDEC), reads=['cs'], writes=['eN'])
                    P.op('act', lambda e: e.activation(out=tm['eX'][:], in_=tm['csx'][:], func=AF.Exp, scale=-CDEC), reads=['csx'], writes=['eX'])
                    P.op('dve', lambda e: e.tensor_scalar(out=nb[:], in0=tm['cs'][:].rearrange("p (c t) -> p c t", t=128)[:, :, 127], scalar1=-CDEC, scalar2=None, op0=ALU.mult),
                         reads=['cs'], writes=['nb'])
                    P.op('act', lambda e, cc=cc: e.activation(out=dCt[:, :, cc], in_=nb[:], func=AF.Exp), reads=['nb'], writes=['dCt'])
                    for ch in range(NCH):
                        cs_ = slice(ch * 128, (ch + 1) * 128)
                        P.op('act', lambda e, cs_=cs_, ch=ch: e.activation(out=tm['eT'][:, cs_], in_=tm['cs'][:, cs_], func=AF.Exp, scale=CDEC, bias=nb[:, ch:ch + 1]),
                             reads=['cs', 'nb'], writes=['eT'])
                    P.op('dve', lambda e, cc=cc: e.tensor_scalar(out=tm['kkr'][:], in0=kT[:, cc, :], scalar1=V(l, V_KK + cc), scalar2=None, op0=ALU.mult),
                         reads=[f'kT{cc}'], writes=['kkr'])
                    P.op('act', lambda e: e.activation(out=tm['sq'][:], in_=tm['kkr'][:], func=AF.Square), reads=['kkr'], writes=['sq'])
                    bank, bk = mmbank()
                    P.op('pe', lambda e, bank=bank: e.matmul(bank[:, 0:T], lhsT=bones, rhs=tm['sq'][:], start=True, stop=True), reads=['sq'], writes=[bk])
                    P.op('dve', lambda e, bank=bank: e.tensor_scalar(out=tm['sq'][:], in0=bank[:, 0:T], scalar1=1e-24, scalar2=None, op0=ALU.max),
                         writes=[bk, 'sq'])
                    P.op('act', lambda e: e.activation(out=tm['sq'][:], in_=tm['sq'][:], func=AF.Ln), writes=['sq'])
                    P.op('act', lambda e: e.activation(out=tm['sq'][:], in_=tm['sq'][:], func=AF.Exp, scale=-0.5), writes=['sq'])
                    P.op('pool', lambda e: e.tensor_tensor(out=tm['kk'][:], in0=tm['kkr'][:], in1=tm['sq'][:], op=ALU.mult), reads=['kkr', 'sq'], writes=['kk'])
                    P.op('dve', lambda e, cc=cc: e.tensor_scalar(out=tm['t1'][:], in0=tm['aa'][:], scalar1=V(l, V_KA + cc), scalar2=omka[:, l * 4 + cc: l * 4 + cc + 1],
                                                                 op0=ALU.mult, op1=ALU.add), reads=['aa'], writes=['t1'])
                    P.op('pool', lambda e, cc=cc: e.tensor_tensor(out=tm['kmod'][:], in0=kT[:, cc, :], in1=tm['t1'][:], op=ALU.mult), reads=[f'kT{cc}', 't1'], writes=['kmod'])
                    P.op('pool', lambda e: e.tensor_tensor(out=tm['bvec'][:], in0=tm['kk'][:], in1=tm['aa'][:], op=ALU.mult), reads=['kk', 'aa'], writes=['bvec'])
                    P.op('dve', lambda e, cc=cc: e.scalar_tensor_tensor(out=tm['prod'][:], in0=rT[:, cc, :], scalar=V(l, V_RK + cc), in1=tm['kmod'][:], op0=ALU.mult, op1=ALU.mult),
                         reads=[f'rT{cc}', 'kmod'], writes=['prod'])
                    bank, bk = mmbank()
                    P.op('pe', lambda e, bank=bank: e.matmul(bank[:, 0:T], lhsT=bones, rhs=tm['prod'][:], start=True, stop=True), reads=['prod'], writes=[bk])
                    P.op('dve', lambda e, bank=bank, cc=cc: e.tensor_tensor(out=bonus[:, cc, :], in0=bank[:, 0:T], in1=vT[:, cc, :], op=ALU.mult),
                         reads=[f'vT{cc}'], writes=[bk, f'bonus{cc}'])
                    v3 = lambda a: a.rearrange("p (c t) -> p c t", t=128)
                    P.op('dve', lambda e, cc=cc: e.scalar_tensor_tensor(out=AR[:, cc, :, 0:128], in0=v3(tm['kk'][:]), scalar=-1.0, in1=v3(tm['eX'][:]), op0=ALU.mult, op1=ALU.mult),
                         reads=['kk', 'eX'], writes=[f'AR{cc}'])
                    P.op('pool', lambda e, cc=cc: e.tensor_tensor(out=AR[:, cc, :, 128:256], in0=v3(rT[:, cc, :]), in1=v3(tm['eP'][:]), op=ALU.mult),
                         reads=[f'rT{cc}', 'eP'], writes=[f'AR{cc}'])
                    P.op('dve', lambda e, cc=cc: e.tensor_tensor(out=KT[:, cc, :], in0=tm['kmod'][:], in1=tm['eN'][:], op=ALU.mult), reads=['kmod', 'eN'], writes=[f'KT{cc}'])
                    P.op('pool', lambda e, cc=cc: e.tensor_tensor(out=BT[:, cc, :], in0=tm['bvec'][:], in1=tm['eN'][:], op=ALU.mult), reads=['bvec', 'eN'], writes=[f'BT{cc}'])
                    P.op('dve', lambda e, cc=cc: e.tensor_tensor(out=KTL[:, cc, :], in0=tm['kmod'][:], in1=tm['eT'][:], op=ALU.mult), reads=['kmod', 'eT'], writes=[f'KTL{cc}'])
                    P.op('pool', lambda e, cc=cc: e.tensor_tensor(out=BTL[:, cc, :], in0=tm['bvec'][:], in1=tm['eT'][:], op=ALU.mult), reads=['bvec', 'eT'], writes=[f'BTL{cc}'])

                if KSTOP < 4:
                    break
                for ch in range(NCH):
                    cs_ = slice(ch * 128, (ch + 1) * 128)
                    for cc in range(4):
                        P.op('pe', lambda e, cc=cc: e.transpose(out=tpf[:, cc * 128:(cc + 1) * 128], in_=vT[:, cc, cs_], identity=ident),
                             reads=[f'vT{cc}'], writes=['tpf'], inc=(cc == 3))
                    P.op('act', lambda e: e.copy(out=Vf[:], in_=tpf[:]), writes=['tpf', 'Vf'])
                    P.op('pool', lambda e: e.tensor_copy(out=Vb[:], in_=Vf[:]), reads=['Vf'], writes=['Vb'])
                    for cc in range(4):
                        P.op('pe', lambda e, cc=cc: e.transpose(out=tpb[:, cc * 128:(cc + 1) * 128], in_=KTL[:, cc, cs_], identity=identb[:]),
                             reads=[f'KTL{cc}'], writes=['tpb'], inc=(cc == 3))
                    P.op('dve', lambda e: e.tensor_copy(out=KTLt[:], in_=tpb[:, 0:512]), writes=['tpb', 'KTLt'])
                    for cc in range(4):
                        P.op('pe', lambda e, cc=cc: e.transpose(out=tpb[:, cc * 128:(cc + 1) * 128], in_=BTL[:, cc, cs_], identity=identb[:]),
                             reads=[f'BTL{cc}'], writes=['tpb'], inc=(cc == 3))
                    P.op('act', lambda e: e.copy(out=BTLt[:], in_=tpb[:, 0:512]), writes=['tpb', 'BTLt'])
                    for h in range(8):
                        cc, po = h // 2, 64 * (h % 2)
                        bank, bk = (sA, 'sA') if h % 2 == 0 else (sB, 'sB')
                        P.op('pe', lambda e, cc=cc, po=po, bank=bank: e.matmul(bank[:, 0:256], lhsT=BT[po:po + 64, cc, cs_], rhs=AR[po:po + 64, cc, ch, :], start=True, stop=True),
                             reads=[f'BT{cc}', f'AR{cc}'], writes=[bk], inc=False)
                        P.op('pe', lambda e, cc=cc, po=po, bank=bank: e.matmul(bank[:, 256:512], lhsT=KT[po:po + 64, cc, cs_], rhs=AR[po:po + 64, cc, ch, :], start=True, stop=True),
                             reads=[f'KT{cc}', f'AR{cc}'], writes=[bk])
                        P.op('dve', lambda e, h=h, bank=bank: e.tensor_tensor(out=Sc[:, h, :], in0=bank[:], in1=mask4, op=ALU.mult), writes=[bk, f'Sc{h}'])
                    for g in range(2):
                        bank, bk = (sA, 'sA') if g == 0 else (sB, 'sB')
                        po = 64 * g
                        for hh in range(4):
                            h = 2 * hh + g
                            cc = h // 2
                            P.op('pe', lambda e, cc=cc, po=po, bank=bank, hh=hh: e.matmul(bank[:, hh * 128:(hh + 1) * 128], lhsT=AR[po:po + 64, cc, ch, 0:128], rhs=BT[po:po + 64, cc, cs_],
                                                                                          start=True, stop=True), reads=[f'BT{cc}', f'AR{cc}'], writes=[bk], inc=(hh == 3))
                        P.op('dve', lambda e, g=g, bank=bank: e.tensor_tensor(out=Nm[0][:, g::2, :], in0=bank[:].rearrange("p (h t) -> p h t", t=128),
                                                                              in1=maskN4.rearrange("p (h t) -> p h t", t=128), op=ALU.mult), writes=[bk, 'Nm0_0', 'Nm0_1'])
                    for g in range(2):
                        P.op('pool', lambda e, g=g: e.tensor_tensor(out=Pm[0][:, 4 * g:4 * g + 4, :], in0=Sc[:, 4 * g:4 * g + 4, 0:128],
                                                                    in1=identb[:].unsqueeze(1).to_broadcast([128, 4, 128]), op=ALU.add),
                             reads=[f'Sc{4 * g + i}' for i in range(4)], writes=[f'Pm0_{g}'])
                    abank = [(sA, 'sA'), (xu, 'xu')]
                    nbank = [(sB, 'sB'), (hp, 'hp')]
                    pbank = [(mm0, 'mm0'), (mm1, 'mm1')]
                    for lev in range(6):
                        ci, ni = lev % 2, (lev + 1) % 2
                        for g in range(2):
                            hs = [4 * g + i for i in range(4)]
                            if lev == 0:
                                Acur = lambda h: Sc[:, h, 0:128]
                                akeys = [f'Sc{h}' for h in hs]
                            else:
                                Acur = lambda h, ci=ci: Am[ci][:, h, :]
                                akeys = [f'Am{ci}_{g}']
                            nkeys = [f'Nm{ci}_{g}']
                            if lev < 5:
                                bank, bk = abank[g]
                                for hh, h in enumerate(hs):
                                    P.op('pe', lambda e, hh=hh, h=h, Acur=Acur, ci=ci, bank=bank: e.matmul(bank[:, hh * 128:(hh + 1) * 128], lhsT=Nm[ci][:, h, :], rhs=Acur(h), start=True, stop=True),
                                         reads=akeys + nkeys, writes=[bk], inc=(hh == 3))
                                P.op('act', lambda e, g=g, ni=ni, bank=bank: e.copy(out=Am[ni][:, 4 * g:4 * g + 4, :], in_=bank[:].rearrange("p (h t) -> p h t", t=128)),
                                     writes=[bk, f'Am{ni}_{g}'])
                            bank, bk = nbank[g]
                            for hh, h in enumerate(hs):
                                P.op('pe', lambda e, hh=hh, h=h, Acur=Acur, ci=ci, bank=bank: e.matmul(bank[:, hh * 128:(hh + 1) * 128], lhsT=Acur(h), rhs=Nm[ci][:, h, :], start=True, stop=True),
                                     reads=akeys + nkeys, writes=[bk], inc=(hh == 3))
                            P.op('dve', lambda e, g=g, ni=ni, bank=bank: e.tensor_copy(out=Nm[ni][:, 4 * g:4 * g + 4, :], in_=bank[:].rearrange("p (h t) -> p h t", t=128)),
                                 writes=[bk, f'Nm{ni}_{g}'])
                        for g in range(2):
                            hs = [4 * g + i for i in range(4)]
                            bank, bk = pbank[g]
                            for hh, h in enumerate(hs):
                                P.op('pe', lambda e, hh=hh, h=h, ni=ni, ci=ci, bank=bank: e.matmul(bank[:, hh * 128:(hh + 1) * 128], lhsT=Nm[ni][:, h, :], rhs=Pm[ci][:, h, :], start=True, stop=True),
                                     reads=[f'Nm{ni}_{g}', f'Pm{ci}_{g}'], writes=[bk], inc=(hh == 3))
                            P.op('dve', lambda e, g=g, ni=ni, ci=ci, bank=bank: e.tensor_tensor(out=Pm[ni][:, 4 * g:4 * g + 4, :], in0=bank[:].rearrange("p (h t) -> p h t", t=128),
                                                                                              in1=Pm[ci][:, 4 * g:4 * g + 4, :], op=ALU.add),
                                 reads=[f'Pm{ci}_{g}'], writes=[bk, f'Pm{ni}_{g}'])
                    Pf = Pm[0]
                    pfk = ['Pm0_0', 'Pm0_1']
                    sck = [f'Sc{h}' for h in range(8)]
                    for h in range(8):
                        cc, po = h // 2, 64 * (h % 2)
                        hs_ = slice(h * 64, (h + 1) * 64)
                        P.op('pe', lambda e, h=h, hs_=hs_: e.matmul(xu[:, hs_], lhsT=Sc[:, h, 256:384], rhs=Vb[:, hs_], start=True, stop=False),
                             reads=[f'Sc{h}', 'Vb'], writes=['xu'], inc=False)
                        P.op('pe', lambda e, cc=cc, po=po, hs_=hs_: e.matmul(xu[:, hs_], lhsT=AR[po:po + 64, cc, ch, 0:128], rhs=Hb[po:po + 64, l, cc, :], start=False, stop=True),
                             reads=[f'AR{cc}', f'Hb{l}'], writes=['xu'], inc=(h == 7))
                    P.op('act', lambda e: e.copy(out=Xb[:], in_=xu[:]), writes=['xu', 'Xb'])
                    for h in range(8):
                        hs_ = slice(h * 64, (h + 1) * 64)
                        P.op('pe', lambda e, h=h, hs_=hs_: e.matmul(xu[:, hs_], lhsT=Pf[:, h, :], rhs=Xb[:, hs_], start=True, stop=True),
                             reads=pfk + ['Xb'], writes=['xu'], inc=(h == 7))
                    P.op('dve', lambda e: e.tensor_copy(out=Ub[:], in_=xu[:]), writes=['xu', 'Ub'])
                    for h in range(8):
                        cc, po = h // 2, 64 * (h % 2)
                        hs_ = slice(h * 64, (h + 1) * 64)
                        P.op('pe', lambda e, cc=cc, po=po, hs_=hs_: e.matmul(tpf[:, hs_], lhsT=AR[po:po + 64, cc, ch, 128:256], rhs=Hb[po:po + 64, l, cc, :], start=True, stop=False),
                             reads=[f'AR{cc}', f'Hb{l}'], writes=['tpf'], inc=False)
                        P.op('pe', lambda e, h=h, hs_=hs_: e.matmul(tpf[:, hs_], lhsT=Sc[:, h, 128:256], rhs=Ub[:, hs_], start=False, stop=False),
                             reads=[f'Sc{h}', 'Ub'], writes=['tpf'], inc=False)
                        P.op('pe', lambda e, h=h, hs_=hs_: e.matmul(tpf[:, hs_], lhsT=Sc[:, h, 384:512], rhs=Vb[:, hs_], start=False, stop=True),
                             reads=[f'Sc{h}', 'Vb'], writes=['tpf'], inc=(h == 7))
                    for h in range(8):
                        cc, po = h // 2, 64 * (h % 2)
                        hs_ = slice(h * 64, (h + 1) * 64)
                        P.op('pe', lambda e, cc=cc, po=po, hs_=hs_: e.matmul(hp[po:po + 64, cc * 64:(cc + 1) * 64], lhsT=BTLt[:, hs_], rhs=Ub[:, hs_], start=True, stop=False),
                             reads=['BTLt', 'Ub'], writes=['hp'], inc=False)
                        P.op('pe', lambda e, cc=cc, po=po, hs_=hs_: e.matmul(hp[po:po + 64, cc * 64:(cc + 1) * 64], lhsT=KTLt[:, hs_], rhs=Vb[:, hs_], start=False, stop=True),
                             reads=['KTLt', 'Vb'], writes=['hp'], inc=(h == 7))
                    P.op('dve', lambda e: e.tensor_tensor(out=Hm[:, l, :, :], in0=Hm[:, l, :, :], in1=dCt[:, ch, :].unsqueeze(2).to_broadcast([128, 4, 64]), op=ALU.mult),
                         reads=['dCt', f'Hb{l}'], writes=[f'Hm{l}'])
                    P.op('dve', lambda e: e.tensor_tensor(out=Hm[:, l, :, :], in0=Hm[:, l, :, :], in1=hp[:, 0:256].rearrange("p (c v) -> p c v", v=64), op=ALU.add),
                         writes=['hp', f'Hm{l}'])
                    P.op('pool', lambda e: e.tensor_copy(out=Hb[:, l, :, :], in_=Hm[:, l, :, :]), reads=[f'Hm{l}'], writes=[f'Hb{l}'])
                    P.op('act', lambda e: e.copy(out=ysb[:], in_=tpf[:]), writes=['tpf', 'ysb'])
                    P.op('pool', lambda e: e.tensor_tensor(out=ysq[:], in0=ysb[:], in1=ysb[:], op=ALU.mult), reads=['ysb'], writes=['ysq'])
                    y3 = lambda a: a.rearrange("p (h v) -> p h v", v=64)
                    P.op('dve', lambda e: e.tensor_reduce(out=gst[:, 0:8], in_=y3(ysb[:]), axis=AX.X, op=ALU.add), reads=['ysb'], writes=['gst0'])
                    P.op('dve', lambda e: e.tensor_reduce(out=gst[:, 8:16], in_=y3(ysq[:]), axis=AX.X, op=ALU.add), reads=['ysq'], writes=['gst1'])
                    P.op('dve', lambda e: e.tensor_scalar(out=gst[:, 16:24], in0=gst[:, 0:8], scalar1=1.0 / 64, scalar2=None, op0=ALU.mult), reads=['gst0'], writes=['gst2'])
                    P.op('pool', lambda e: e.tensor_tensor(out=gst[:, 24:32], in0=gst[:, 16:24], in1=gst[:, 16:24], op=ALU.mult), reads=['gst2'], writes=['gst3'])
                    P.op('dve', lambda e: e.scalar_tensor_tensor(out=gst[:, 32:40], in0=gst[:, 8:16], scalar=1.0 / 64, in1=gst[:, 24:32], op0=ALU.mult, op1=ALU.subtract),
                         reads=['gst1', 'gst3'], writes=['gst4'])
                    P.op('act', lambda e: e.activation(out=gst[:, 24:32], in_=gst[:, 32:40], func=AF.Ln, bias=GN_EPS), reads=['gst4'], writes=['gst3'])
                    P.op('act', lambda e: e.activation(out=gst[:, 24:32], in_=gst[:, 24:32], func=AF.Exp, scale=-0.5), writes=['gst3'])
                    P.op('dve', lambda e: e.tensor_tensor(out=y3(ync[:]), in0=y3(ysb[:]), in1=gst[:, 16:24].unsqueeze(2).to_broadcast([128, 8, 64]), op=ALU.subtract),
                         reads=['ysb', 'gst2'], writes=['ync'])
                    P.op('pool', lambda e: e.tensor_tensor(out=y3(ynb[:]), in0=y3(ync[:]), in1=gst[:, 24:32].unsqueeze(2).to_broadcast([128, 8, 64]), op=ALU.mult),
                         reads=['ync', 'gst3'], writes=['ynb'])
                    for cc in range(4):
                        P.op('pe', lambda e, cc=cc: e.transpose(out=tpb[:, cc * 128:(cc + 1) * 128], in_=ynb[:, cc * 128:(cc + 1) * 128], identity=identb[:]),
                             reads=['ynb'], writes=['tpb'], inc=(cc == 3))
                    for cc in range(4):
                        b = cc % 2
                        P.op('dve', lambda e, cc=cc, b=b: e.tensor_scalar(out=otmp[b][:], in0=tpb[:, cc * 128:(cc + 1) * 128], scalar1=V(l, V_LG + cc), scalar2=V(l, V_LB + cc),
                                                                          op0=ALU.mult, op1=ALU.add), writes=['tpb', f'otmp{b}'])
                        P.op('pool', lambda e, cc=cc, b=b: e.tensor_tensor(out=otmp[b][:], in0=otmp[b][:], in1=bonus[:, cc, cs_], op=ALU.add),
                             reads=[f'bonus{cc}'], writes=[f'otmp{b}'])
                        P.op('dve', lambda e, cc=cc, b=b: e.tensor_tensor(out=ycat[:, 4 + cc, cs_], in0=otmp[b][:], in1=gsb[:, cc, cs_], op=ALU.mult),
                             reads=[f'otmp{b}', f'gsb{cc}'], writes=[f'ycat{4 + cc}'])

                if KSTOP < 5:
                    break
                for half in range(2):
                    slot = use_slab(si); si += 1
                    for jj in range(4):
                        j = half * 4 + jj
                        bank, bk = mmbank()
                        dense(bank, bk, slot, 512, jj * 128, ycat, ykeys, NB)
                        P.op('dve', lambda e, j=j, bank=bank: e.scalar_tensor_tensor(out=xT[:, j, :], in0=bank[:, 0:T], scalar=DER(l, DG1 + j), in1=xT[:, j, :], op0=ALU.mult, op1=ALU.add),
                             writes=[bk, f'xT{j}'])
                layernorm(l, V_L1G, V_L1B, DA2, DB2, l)
                if KSTOP < 6:
                    break
                for s in range(11):
                    slot = use_slab(si); si += 1
                    for qq in range(2):
                        (gbank, gk), (ubank, uk) = [((mm0, 'mm0'), (mm1, 'mm1')), ((sA, 'sA'), (sB, 'sB')), ((xu, 'xu'), (hp, 'hp'))][(2 * s + qq) % 3]
                        for kc in range(NB):
                            P.op('pe', lambda e, kc=kc, qq=qq, slot=slot, gbank=gbank: e.matmul(gbank[:, 0:T], lhsT=ring[:, slot, kc * 512 + qq * 128: kc * 512 + qq * 128 + 128], rhs=hT[:, kc, :],
                                                                                                start=(kc == 0), stop=(kc == NB - 1)), reads=[f'ring{slot}'] + hkeys, writes=[gk], inc=(kc == NB - 1))
                        for kc in range(NB):
                            P.op('pe', lambda e, kc=kc, qq=qq, slot=slot, ubank=ubank: e.matmul(ubank[:, 0:T], lhsT=ring[:, slot, kc * 512 + 256 + qq * 128: kc * 512 + 256 + qq * 128 + 128], rhs=hT[:, kc, :],
                                                                                                start=(kc == 0), stop=(kc == NB - 1)), reads=[f'ring{slot}'] + hkeys, writes=[uk], inc=(kc == NB - 1))
                        b = qq
                        P.op('act', lambda e, b=b, gbank=gbank: e.activation(out=slt[b][:], in_=gbank[:, 0:T], func=AF.Silu), writes=[gk, f'slt{b}'])
                        P.op('dve', lambda e, b=b, s=s, qq=qq, ubank=ubank: e.tensor_tensor(out=act[:, 2 * s + qq, :], in0=ubank[:, 0:T], in1=slt[b][:], op=ALU.mult),
                             reads=[f'slt{b}'], writes=[uk, f'act{2 * s + qq}'])
                akeys_ = [f'act{f}' for f in range(NF)]
                for j in range(NB):
                    slot = use_slab(si); si += 1
                    bank, bk = mmbank()
                    dense(bank, bk, slot, 128, 0, act, akeys_, NF)
                    P.op('dve', lambda e, j=j, bank=bank: e.scalar_tensor_tensor(out=xT[:, j, :], in0=bank[:, 0:T], scalar=DER(l, DG2 + j), in1=xT[:, j, :], op0=ALU.mult, op1=ALU.add),
                         writes=[bk, f'xT{j}'])
                layernorm(l, V_L2G, V_L2B, DA1, DB1, (l + 1) if l + 1 < L else None)
            P.dma('yout', yT_d[:, t0:t0 + T].rearrange("(j p) t -> p j t", p=128), xT[:], reads=[f'xT{j}' for j in range(NB)])
        P._wait('sp', ('yout', P.chan_count['yout']))
        P.emit()
        build_nc.stats = (P.nops, P.nwaits)
    return nc


def _slab_std(W, c0, ncols):
    blk = W.reshape(8, 128, W.shape[1])[:, :, c0:c0 + ncols].transpose(1, 0, 2).reshape(128, 8 * ncols)
    out = np.zeros((128, SLAB), np.float32)
    out[:, :8 * ncols] = blk
    return out


def _vec_cols(v):
    v = np.asarray(v, np.float32).reshape(-1, 128)
    return v.T


def prepare_shared(inp, L):
    wslab = np.zeros((L * NSLAB, 128, SLAB), np.float32)
    wmod = np.zeros((L * NMOD, 128, SLAB), np.float32)
    vecs = np.zeros((128, L, NV), np.float32)
    lw = np.zeros((128, L, 512), np.float32)
    gu = np.zeros((128, L, 512), np.float32)
    for l in range(L):
        Win = np.asarray(inp["w_in"][l], np.float32)
        s = l * NSLAB
        for k, c0 in enumerate((1024, 512, 0, 1536, 2048, 2560)):
            wslab[s + k] = _slab_std(Win, c0, 512)
        wslab[s + 6] = _slab_std(Win, 3072, 256)
        Wout = np.asarray(inp["w_out"][l], np.float32)
        wslab[s + 7] = _slab_std(Wout, 0, 512)
        wslab[s + 8] = _slab_std(Wout, 512, 512)
        W1 = np.asarray(inp["w_ffn_in"][l], np.float32)
        W1r = W1.reshape(8, 128, 2 * DFF)
        for k in range(11):
            g = W1r[:, :, 256 * k:256 * k + 256]
            u = W1r[:, :, DFF + 256 * k:DFF + 256 * k + 256]
            wslab[s + 9 + k] = np.concatenate([g, u], axis=2).transpose(1, 0, 2).reshape(128, SLAB)
        W2 = np.asarray(inp["w_ffn_out"][l], np.float32).reshape(NF, 128, D)
        for j in range(8):
            wslab[s + 20 + j, :, :NF * 128] = W2[:, :, j * 128:(j + 1) * 128].transpose(1, 0, 2).reshape(128, NF * 128)
        Wm = np.asarray(inp["w_mod"][l], np.float32)
        for k in range(NMOD):
            wmod[l * NMOD + k] = _slab_std(Wm, 512 * k, 512)
        mu = np.asarray(inp["mu_shift"][l], np.float32)
        cols = [_vec_cols(mu)]
        for nm in ("w0", "a0", "k_k", "k_a"):
            cols.append(_vec_cols(inp[nm][l]))
        cols.append(_vec_cols(np.asarray(inp["r_k"][l]).reshape(-1)))
        cols.append(_vec_cols(inp["lnx_g"][l]))
        cols.append(_vec_cols(inp["lnx_b"][l]))
        cw = np.asarray(inp["conv_w"][l], np.float32)
        for tap in range(3):
            cols.append(_vec_cols(cw[tap]))
        for nm in ("ln1_g", "ln1_b", "ln2_g", "ln2_b"):
            cols.append(_vec_cols(inp[nm][l]))
        cols.append(_vec_cols(inp["b_mod"][l]))
        vecs[:, l, :] = np.concatenate(cols, axis=1)
        lw[0:64, l, :] = np.asarray(inp["w_decay_up"][l], np.float32)
        lw[64:128, l, :] = np.asarray(inp["a_up"][l], np.float32)
        gu[:, l, :] = np.asarray(inp["g_up"][l], np.float32)
    consts = np.zeros((128, 1408), np.float32)
    consts[:, 0:128] = np.eye(128)
    s_idx = np.arange(128)[:, None]
    t_idx = np.arange(128)[None, :]
    strict = (s_idx < t_idx).astype(np.float32)
    incl = (s_idx <= t_idx).astype(np.float32)
    consts[:, 128:640] = np.concatenate([strict, incl, strict, incl], axis=1)
    lowN = (t_idx < s_idx).astype(np.float32)
    consts[:, 640:1152] = np.concatenate([lowN] * 4, axis=1)
    consts[:, 1152:1280] = ((s_idx // 64) == (t_idx // 64)).astype(np.float32)
    consts[:, 1280:1408] = 1.0
    return dict(wslab=wslab, wmod=wmod, vecs=np.ascontiguousarray(vecs.reshape(128, L * NV)),
                lw=np.ascontiguousarray(lw.reshape(128, L * 512)), gu=np.ascontiguousarray(gu.reshape(128, L * 512)), consts=consts)


def run(inp, SEQ, L, T, runner):
    x = np.asarray(inp["x"], np.float32)
    c = np.asarray(inp["c"], np.float32)
    B = x.shape[0]
    shared = prepare_shared(inp, L)
    nc = build_nc(SEQ, L, T)
    in_maps = []
    for core in range(8):
        b = core % B
        m = dict(shared)
        m["xT"] = np.ascontiguousarray(x[b].T)
        m["cT"] = np.ascontiguousarray(c[b].reshape(8, 128).T)
        in_maps.append(m)
    res = runner(nc, in_maps)
    out = np.stack([np.ascontiguousarray(res[b]["yT"].T) for b in range(B)], axis=0)
    return out.astype(np.float32)


def kernel(**inputs):
    def runner(nc, in_maps):
        r = run_bass_kernel_spmd(nc, in_maps, core_ids=list(range(8)))
        return r.results
    return run(inputs, 8192, 4, 256, runner)
```

```python
import math
import os
KSTOP = int(os.environ.get('KSTOP', '99'))
import numpy as np
from contextlib import ExitStack
import concourse.bass as bass
import concourse.mybir as mybir
from concourse.bass_utils import run_bass_kernel_spmd

F32 = mybir.dt.float32
BF16 = mybir.dt.bfloat16
AF = mybir.ActivationFunctionType
ALU = mybir.AluOpType
AX = mybir.AxisListType

D = 1024
NB = 8
DFF = 2816
NF = 22
NSLAB = 28
NMOD = 12
NV = 134
SLAB = 4096
ALPHA = (2.0 * 4) ** 0.25
LN_EPS = 1e-5
GN_EPS = 64e-5
CDEC = math.exp(-0.5)

V_MU = 0
V_W0 = 14
V_A0 = 18
V_KK = 22
V_KA = 26
V_RK = 30
V_LG = 34
V_LB = 38
V_CW = 42
V_L1G = 54
V_L1B = 62
V_L2G = 70
V_L2B = 78
V_BM = 86
DB1, DA1, DG1, DB2, DA2, DG2 = 0, 8, 16, 24, 32, 40


class _Rec:
    def __getattr__(self, name):
        return lambda *a, **k: (name, a, k)


_REC = _Rec()


class Prog:
    EPOCH = 16000

    def __init__(self, nc, stack):
        self.nc = nc
        self.stack = stack
        self.engs = ('sp', 'pe', 'act', 'dve', 'pool')
        self.streams = {e: [] for e in self.engs}
        self.count = {e: 0 for e in self.engs}
        self.clock = {e: {} for e in self.engs}
        self.iclock = {}
        self.sems = {}
        self.last_write = {}
        self.reads = {}
        self.chan_count = {}
        self.nwaits = 0
        self.nops = 0

    def _sem(self, name):
        if name not in self.sems:
            self.sems[name] = self.stack.enter_context(self.nc.semaphore(name.replace(':', '_')))
        return self.sems[name]

    def _wait(self, eng, dep, force=False):
        e2, n = dep
        if n <= 0:
            return
        if not force and self.clock[eng].get(e2, 0) >= n:
            return
        if e2 in self.engs:
            ep = (n - 1) // self.EPOCH
            sem = self._sem(f"{e2}:{ep}")
            val = n - ep * self.EPOCH
        else:
            sem = self._sem(e2)
            val = 16 * n
        self.streams[eng].append(('wait', sem, val))
        self.nwaits += 1
        ck = self.clock[eng]
        for k, v in self.iclock.get(dep, {}).items():
            if ck.get(k, 0) < v:
                ck[k] = v
        if ck.get(e2, 0) < n:
            ck[e2] = n

    def _deps(self, eng, ident, reads, writes):
        deps = set()
        for k in reads:
            lw = self.last_write.get(k)
            if lw is not None:
                deps.add(lw)
        for k in writes:
            lw = self.last_write.get(k)
            if lw is not None:
                deps.add(lw)
            for r in self.reads.get(k, ()):
                deps.add(r)
        for d in sorted(deps):
            e2, n = d
            if e2 == eng:
                if eng != 'pe' and eng != 'sp':
                    self._wait(eng, d)
                continue
            self._wait(eng, d)
        for k in reads:
            self.reads.setdefault(k, []).append(ident)
        for k in writes:
            self.last_write[k] = ident
            self.reads[k] = []

    def op(self, eng, fn, reads=(), writes=(), inc=True):
        n = self.count[eng] + 1
        ident = (eng, n)
        self._deps(eng, ident, reads, writes)
        self.nops += 1
        if not inc:
            self.streams[eng].append(('op', fn(_REC), None, 0))
            return
        self.count[eng] = n
        ep = (n - 1) // self.EPOCH
        sem = self._sem(f"{eng}:{ep}")
        self.streams[eng].append(('op', fn(_REC), sem, 1))
        self.iclock[ident] = dict(self.clock[eng])

    def dma(self, chan, out, in_, reads=(), writes=(), eng='sp'):
        n = self.chan_count.get(chan, 0) + 1
        ident = (chan, n)
        self._deps(eng, ident, reads, writes)
        self.chan_count[chan] = n
        sem = self._sem(chan)
        self.streams[eng].append(('op', ('dma_start', (), dict(out=out, in_=in_)), sem, 16))
        self.iclock[ident] = dict(self.clock[eng])
        self.nops += 1

    def barrier(self):
        for e in self.engs:
            for e2 in self.engs:
                if e2 != e and e2 != 'sp':
                    self._wait(e, (e2, self.count[e2]))
            for ch, n in self.chan_count.items():
                self._wait(e, (ch, n))
        self.last_write = {}
        self.reads = {}

    def emit(self):
        with self.nc.Block() as block:
            for ename, deco in (('sp', block.sync), ('pe', block.tensor), ('act', block.scalar),
                                ('dve', block.vector), ('pool', block.gpsimd)):
                stream = self.streams[ename]

                def body(e, stream=stream):
                    for it in stream:
                        if it[0] == 'wait':
                            e.wait_ge(it[1], it[2])
                        else:
                            name, a, k = it[1]
                            ins = getattr(e, name)(*a, **k)
                            if it[2] is not None:
                                ins.then_inc(it[2], it[3])
                deco(body)
        self.streams = {e: [] for e in self.engs}


def build_nc(SEQ, L, T):
    NCH = T // 128
    NT = SEQ // T
    nc = bass.Bass("TRN2", target_bir_lowering=False)
    xT_d = nc.dram_tensor("xT", [D, SEQ], F32, kind="ExternalInput").ap()
    cT_d = nc.dram_tensor("cT", [128, 8], F32, kind="ExternalInput").ap()
    vecs_d = nc.dram_tensor("vecs", [128, L * NV], F32, kind="ExternalInput").ap()
    wslab_d = nc.dram_tensor("wslab", [L * NSLAB, 128, SLAB], F32, kind="ExternalInput").ap()
    wmod_d = nc.dram_tensor("wmod", [L * NMOD, 128, SLAB], F32, kind="ExternalInput").ap()
    lw_d = nc.dram_tensor("lw", [128, L * 512], F32, kind="ExternalInput").ap()
    gu_d = nc.dram_tensor("gu", [128, L * 512], F32, kind="ExternalInput").ap()
    consts_d = nc.dram_tensor("consts", [128, 1408], F32, kind="ExternalInput").ap()
    yT_d = nc.dram_tensor("yT", [D, SEQ], F32, kind="ExternalOutput").ap()
    wsb_d = nc.dram_tensor("wsb", [L * NSLAB, 128, SLAB], BF16, kind="Internal").ap()

    with ExitStack() as st:
        P = Prog(nc, st)

        def sb(name, shape, dt, stack=st):
            return stack.enter_context(nc.sbuf_tensor("s_" + name, shape, dt))

        def ps(name, shape, dt):
            return st.enter_context(nc.psum_tensor("p_" + name, shape, dt))

        mm0 = ps("mm0", [128, 512], F32)
        mm1 = ps("mm1", [128, 512], F32)
        sA = ps("sA", [128, 512], F32)
        sB = ps("sB", [128, 512], F32)
        tpf = ps("tpf", [128, 512], F32)
        tpb = ps("tpb", [128, 1024], BF16)
        xu = ps("xu", [128, 512], F32)
        hp = ps("hp", [128, 512], F32)

        consts = sb("consts", [128, 1408], F32)
        ident = consts[:, 0:128]
        mask4 = consts[:, 128:640]
        maskN4 = consts[:, 640:1152]
        bones = consts[:, 1152:1280]
        ones = consts[:, 1280:1408]
        identb = sb("identb", [128, 128], BF16)
        onesb = sb("onesb", [128, 128], BF16)
        vecs = sb("vecs", [128, L * NV], F32)
        cT = sb("cT", [128, 8], F32)
        scT = sb("scT", [128, 8], F32)
        mod = sb("mod", [128, L * 48], F32)
        der = sb("der", [128, L * 48], F32)
        omka = sb("omka", [128, L * 4], F32)
        lwb = sb("lwb", [128, L * 512], BF16)
        gub = sb("gub", [128, L * 512], BF16)
        Hm = sb("Hm", [128, L, 4, 64], F32)
        Hb = sb("Hb", [128, L, 4, 64], BF16)
        hist2 = [sb(f"hist{i}", [128, L * 14], F32) for i in range(2)]
        omu = sb("omu", [128, L * 14], F32)
        der2 = sb("der2", [128, L * 32], F32)
        zh = sb("zh", [128, L, 4, 2], F32)

        def V(l, col, n=1):
            return vecs[:, l * NV + col: l * NV + col + n]

        def DER(l, col, n=1):
            return der[:, l * 48 + col: l * 48 + col + n]

        with ExitStack() as pst:
            NSTG = 4
            stg = [sb(f"stg{i}", [128, SLAB], F32, pst) for i in range(NSTG)]
            stb = [sb(f"stb{i}", [128, SLAB], BF16, pst) for i in range(NSTG)]
            P.dma('c0', consts[:], consts_d[:, :], writes=['consts'])
            P.dma('c1', vecs[:], vecs_d[:, :], writes=['vecs'])
            P.dma('c2', cT[:], cT_d[:, :], writes=['cT'])
            P.op('dve', lambda e: e.tensor_copy(out=identb[:], in_=ident), reads=['consts'], writes=['identb'])
            P.op('dve', lambda e: e.tensor_copy(out=onesb[:], in_=ones), reads=['consts'], writes=['onesb'])
            P.op('act', lambda e: e.activation(out=scT[:], in_=cT[:], func=AF.Silu), reads=['cT'], writes=['scT'])
            P.op('pool', lambda e: e.memset(Hm[:], 0.0), writes=['Hm'])
            P.op('pool', lambda e: e.memset(Hb[:], 0.0), writes=['Hb'])
            P.op('pool', lambda e: e.memset(hist2[0][:], 0.0), writes=['hist0'])
            P.op('pool', lambda e: e.memset(hist2[1][:], 0.0), writes=['hist1'])
            P.op('pool', lambda e: e.memset(zh[:], 0.0), writes=['zh'])
            P.dma('stg0', stg[0][:, 0:L * 512], lw_d[:, :], writes=['stg0'])
            P.op('dve', lambda e: e.tensor_copy(out=lwb[:], in_=stg[0][:, 0:L * 512]), reads=['stg0'], writes=['lwb'])
            P.dma('stg1', stg[1][:, 0:L * 512], gu_d[:, :], writes=['stg1'])
            P.op('dve', lambda e: e.tensor_copy(out=gub[:], in_=stg[1][:, 0:L * 512]), reads=['stg1'], writes=['gub'])
            q = 0
            for l in range(L):
                for s in range(NMOD):
                    b = q % NSTG
                    q += 1
                    P.dma(f'stg{b}', stg[b][:], wmod_d[l * NMOD + s, :, :], writes=[f'stg{b}'])
                    for jj in range(4):
                        col = l * 48 + s * 4 + jj
                        for kc in range(8):
                            P.op('pe', lambda e, b=b, jj=jj, kc=kc, col=col: e.matmul(
                                hp[:, col:col + 1], lhsT=stg[b][:, kc * 512 + jj * 128: kc * 512 + jj * 128 + 128],
                                rhs=scT[:, kc:kc + 1], start=(kc == 0), stop=(kc == 7)),
                                reads=[f'stg{b}', 'scT'], writes=['hp'], inc=(kc == 7))
            for l in range(L):
                P.op('dve', lambda e, l=l: e.tensor_tensor(out=mod[:, l * 48:(l + 1) * 48], in0=hp[:, l * 48:(l + 1) * 48],
                                                          in1=V(l, V_BM, 48), op=ALU.add),
                     reads=['vecs'], writes=['hp', 'mod'])
                for (dst, src, kind) in ((DB1, 0, 'c'), (DA1, 8, 'p1'), (DG1, 16, 'g'),
                                         (DB2, 24, 'c'), (DA2, 32, 'p1'), (DG2, 40, 'g')):
                    o = DER(l, dst, 8)
                    i = mod[:, l * 48 + src: l * 48 + src + 8]
                    if kind == 'c':
                        P.op('pool', lambda e, o=o, i=i: e.tensor_copy(out=o, in_=i), reads=['mod'], writes=['der'])
                    elif kind == 'p1':
                        P.op('pool', lambda e, o=o, i=i: e.tensor_scalar_add(out=o, in0=i, scalar1=1.0), reads=['mod'], writes=['der'])
                    else:
                        P.op('pool', lambda e, o=o, i=i: e.tensor_scalar(out=o, in0=i, scalar1=1.0, scalar2=1.0 / ALPHA,
                                                                        op0=ALU.add, op1=ALU.mult), reads=['mod'], writes=['der'])
                P.op('pool', lambda e, l=l: e.tensor_scalar(out=omka[:, l * 4:(l + 1) * 4], in0=V(l, V_KA, 4), scalar1=-1.0, scalar2=1.0,
                                                            op0=ALU.mult, op1=ALU.add), reads=['vecs'], writes=['omka'])
                P.op('pool', lambda e, l=l: e.tensor_scalar(out=omu[:, l * 14:(l + 1) * 14], in0=V(l, V_MU, 14), scalar1=-1.0, scalar2=1.0,
                                                            op0=ALU.mult, op1=ALU.add), reads=['vecs'], writes=['omu'])
            for l in range(L):
                for (dst, gcol, bcol, lsrc, Acol, Bcol) in ((0, V_L1G, V_L1B, l, DA2, DB2), (16, V_L2G, V_L2B, l + 1, DA1, DB1)):
                    if lsrc >= L:
                        continue
                    ga = der2[:, l * 32 + dst: l * 32 + dst + 8]
                    ba = der2[:, l * 32 + dst + 8: l * 32 + dst + 16]
                    P.op('dve', lambda e, ga=ga, l=l, gcol=gcol, lsrc=lsrc, Acol=Acol: e.tensor_tensor(out=ga, in0=V(l, gcol, 8), in1=DER(lsrc, Acol, 8), op=ALU.mult),
                         reads=['vecs', 'der'], writes=['der2'])
                    P.op('dve', lambda e, ba=ba, l=l, bcol=bcol, lsrc=lsrc, Acol=Acol: e.tensor_tensor(out=ba, in0=V(l, bcol, 8), in1=DER(lsrc, Acol, 8), op=ALU.mult),
                         reads=['vecs', 'der'], writes=['der2'])
                    P.op('dve', lambda e, ba=ba, lsrc=lsrc, Bcol=Bcol: e.tensor_tensor(out=ba, in0=ba, in1=DER(lsrc, Bcol, 8), op=ALU.add),
                         reads=['der'], writes=['der2'])
            cast_engs = ('dve', 'pool', 'act')
            for i in range(L * NSLAB):
                b = q % NSTG
                q += 1
                P.dma(f'stg{b}', stg[b][:], wslab_d[i, :, :], writes=[f'stg{b}'])
                ce = cast_engs[i % 3]
                if ce == 'act':
                    P.op('act', lambda e, b=b: e.copy(out=stb[b][:], in_=stg[b][:]), reads=[f'stg{b}'], writes=[f'stb{b}'])
                else:
                    P.op(ce, lambda e, b=b: e.tensor_copy(out=stb[b][:], in_=stg[b][:]), reads=[f'stg{b}'], writes=[f'stb{b}'])
                P.dma(f'stb{b}', wsb_d[i, :, :], stb[b][:], reads=[f'stb{b}'])
            P.barrier()
            P.emit()

        NSLOT = 4
        ring = sb("ring", [128, NSLOT, SLAB], BF16)
        xT = sb("xT", [128, NB, T], F32)
        hT = sb("hT", [128, NB, T], BF16)
        ycat = sb("ycat", [128, NB, T], BF16)
        zbuf = sb("zbuf", [128, 4, T + 2], F32)
        usb = sb("usb", [128, 4, T], F32)
        rT = sb("rT", [128, 4, T], F32)
        kT = sb("kT", [128, 4, T], F32)
        vT = sb("vT", [128, 4, T], F32)
        gsb = sb("gsb", [128, 4, T], F32)
        bonus = sb("bonus", [128, 4, T], F32)
        waf = sb("waf", [128, T], F32)
        glf = sb("glf", [128, T], F32)
        tw = sb("tw", [128, T], BF16)
        sgT = sb("sgT", [128, T], BF16)
        tnames = ['sg', 'aa', 'cs', 'eP', 'eN', 'eT', 'kkr', 'sq', 'kmod', 'bvec', 'prod']
        tmp = {n: sb("t_" + n, [128, 2, T], F32) for n in tnames}
        nb = sb("nb", [128, 2, NCH], F32)
        dCt = sb("dCt", [128, NCH, 4], F32)
        AR = sb("AR", [128, 4, NCH, 256], BF16)
        KT = sb("KT", [128, 4, T], BF16)
        BT = sb("BT", [128, 4, T], BF16)
        KTL = sb("KTL", [128, 4, T], BF16)
        BTL = sb("BTL", [128, 4, T], BF16)
        Vf = sb("Vf", [128, 512], F32)
        Vb = sb("Vb", [128, 512], BF16)
        KTLt = sb("KTLt", [128, 512], BF16)
        BTLt = sb("BTLt", [128, 512], BF16)
        Sc = sb("Sc", [128, 8, 512], BF16)
        Nm = [sb(f"Nm{i}", [128, 8, 128], BF16) for i in range(2)]
        Am = [sb(f"Am{i}", [128, 8, 128], BF16) for i in range(2)]
        Pm = [sb(f"Pm{i}", [128, 8, 128], BF16) for i in range(2)]
        Xb = sb("Xb", [128, 512], BF16)
        Ub = sb("Ub", [128, 512], BF16)
        ysb = sb("ysb", [128, 512], F32)
        ysq = sb("ysq", [128, 512], F32)
        ync = sb("ync", [128, 512], F32)
        ynb = sb("ynb", [128, 512], BF16)
        gst = sb("gst", [128, 40], F32)
        otmp = [sb(f"otmp{i}", [128, 128], F32) for i in range(2)]
        zb = [sb(f"zb{i}", [128, T], BF16) for i in range(2)]
        zq = [sb(f"zq{i}", [128, T], BF16) for i in range(2)]
        lnm = sb("lnm", [128, T], F32)
        lnv = sb("lnv", [128, T], F32)
        lnr = sb("lnr", [128, T], F32)
        act = sb("act", [128, NF, T], BF16)
        slt = [sb(f"slt{i}", [128, T], F32) for i in range(2)]

        total_slabs = NT * L * NSLAB
        state = {'loaded': 0, 'mmq': 0}

        def ensure_loaded(upto):
            while state['loaded'] <= min(upto, total_slabs - 1):
                i = state['loaded']
                slot = i % NSLOT
                P.dma(f'ring{slot}', ring[:, slot, :], wsb_d[i % (L * NSLAB), :, :], writes=[f'ring{slot}'])
                state['loaded'] += 1

        def use_slab(i):
            ensure_loaded(i + NSLOT - 1)
            return i % NSLOT

        rot = [(mm0, 'mm0'), (mm1, 'mm1'), (sA, 'sA'), (sB, 'sB'), (xu, 'xu'), (hp, 'hp')]

        def mmbank():
            state['mmq'] += 1
            return rot[state['mmq'] % len(rot)]

        def dense(bank, bkey, slot, stride, off, rhs_t, rkeys, nk):
            for kc in range(nk):
                P.op('pe', lambda e, kc=kc: e.matmul(bank[:, 0:T], lhsT=ring[:, slot, kc * stride + off: kc * stride + off + 128],
                                                     rhs=rhs_t[:, kc, :], start=(kc == 0), stop=(kc == nk - 1)),
                     reads=[f'ring{slot}'] + rkeys, writes=[bkey], inc=(kc == nk - 1))

        def layernorm(l, gcol, bcol, d2off, lnext):
            for j in range(NB):
                b = j % 2
                P.op('pool', lambda e, j=j, b=b: e.tensor_copy(out=zb[b][:], in_=xT[:, j, :]), reads=[f'xT{j}'], writes=[f'zb{b}'])
                P.op('act', lambda e, j=j, b=b: e.activation(out=zq[b][:], in_=xT[:, j, :], func=AF.Square), reads=[f'xT{j}'], writes=[f'zq{b}'])
                P.op('pe', lambda e, j=j, b=b: e.matmul(mm0[:, 0:T], lhsT=onesb[:], rhs=zb[b][:], start=(j == 0), stop=(j == NB - 1)),
                     reads=[f'zb{b}'], writes=['mm0'])
                P.op('pe', lambda e, j=j, b=b: e.matmul(mm1[:, 0:T], lhsT=onesb[:], rhs=zq[b][:], start=(j == 0), stop=(j == NB - 1)),
                     reads=[f'zq{b}'], writes=['mm1'])
            P.op('act', lambda e: e.activation(out=lnm[:], in_=mm0[:, 0:T], func=AF.Copy, scale=1.0 / D), writes=['mm0', 'lnm'])
            P.op('act', lambda e: e.activation(out=lnv[:], in_=mm0[:, 0:T], func=AF.Square, scale=1.0 / D), writes=['mm0', 'lnv'])
            P.op('dve', lambda e: e.scalar_tensor_tensor(out=lnv[:], in0=mm1[:, 0:T], scalar=1.0 / D, in1=lnv[:], op0=ALU.mult, op1=ALU.subtract),
                 writes=['mm1', 'lnv'])
            P.op('act', lambda e: e.activation(out=lnr[:], in_=lnv[:], func=AF.Ln, bias=LN_EPS / (ALPHA * ALPHA)), reads=['lnv'], writes=['lnr'])
            P.op('act', lambda e: e.activation(out=lnr[:], in_=lnr[:], func=AF.Exp, scale=-0.5), writes=['lnr'])
            xall = [f'xT{j}' for j in range(NB)]
            for hh in range(2):
                hsl = slice(4 * hh, 4 * hh + 4)
                P.op('dve', lambda e, hsl=hsl: e.tensor_tensor(out=xT[:, hsl, :], in0=xT[:, hsl, :], in1=lnm[:].unsqueeze(1).to_broadcast([128, 4, T]), op=ALU.subtract),
                     reads=['lnm'], writes=xall[4 * hh:4 * hh + 4])
                P.op('pool', lambda e, hsl=hsl: e.tensor_tensor(out=xT[:, hsl, :], in0=xT[:, hsl, :], in1=lnr[:].unsqueeze(1).to_broadcast([128, 4, T]), op=ALU.mult),
                     reads=['lnr'], writes=xall[4 * hh:4 * hh + 4])
            for j in range(NB):
                if lnext is not None:
                    P.op('dve' if j % 2 == 0 else 'pool', lambda e, j=j: e.tensor_scalar(out=hT[:, j, :], in0=xT[:, j, :], scalar1=der2[:, l * 32 + d2off + j: l * 32 + d2off + j + 1],
                                                               scalar2=der2[:, l * 32 + d2off + 8 + j: l * 32 + d2off + 8 + j + 1], op0=ALU.mult, op1=ALU.add),
                         reads=[f'xT{j}'], writes=[f'hT{j}'])
                P.op('act', lambda e, j=j: e.activation(out=xT[:, j, :], in_=xT[:, j, :], func=AF.Identity,
                                                        scale=V(l, gcol + j), bias=V(l, bcol + j)), writes=[f'xT{j}'])

        hkeys = [f'hT{j}' for j in range(NB)]
        ykeys = [f'ycat{j}' for j in range(NB)]
        si = 0
        for ti in range(NT):
            t0 = ti * T
            P.dma('xin', xT[:], xT_d[:, t0:t0 + T].rearrange("(j p) t -> p j t", p=128), writes=[f'xT{j}' for j in range(NB)])
            for j in range(NB):
                P.op('act', lambda e, j=j: e.activation(out=hT[:, j, :], in_=xT[:, j, :], func=AF.Identity,
                                                        scale=DER(0, DA1 + j), bias=DER(0, DB1 + j)),
                     reads=[f'xT{j}'], writes=[f'hT{j}'])
            for l in range(L):
                if KSTOP < 2:
                    break
                def shift_lerp(bank, bk, mucol, dst, dkey):
                    hold = hist2[ti % 2][:, l * 14 + mucol: l * 14 + mucol + 1]
                    hnew = hist2[(ti + 1) % 2][:, l * 14 + mucol: l * 14 + mucol + 1]
                    ko, kn = f'hist{ti % 2}_{l}_{mucol}', f'hist{(ti + 1) % 2}_{l}_{mucol}'
                    P.op('act', lambda e: e.activation(out=dst, in_=bank[:, 0:T], func=AF.Identity, scale=omu[:, l * 14 + mucol: l * 14 + mucol + 1]),
                         writes=[bk, dkey])
                    P.op('act', lambda e: e.copy(out=hnew, in_=bank[:, T - 1:T]), writes=[bk, kn])
                    P.op('dve', lambda e: e.scalar_tensor_tensor(out=dst[:, 1:T], in0=bank[:, 0:T - 1], scalar=V(l, V_MU + mucol), in1=dst[:, 1:T],
                                                                 op0=ALU.mult, op1=ALU.add), writes=[bk, dkey])
                    P.op('dve', lambda e: e.scalar_tensor_tensor(out=dst[:, 0:1], in0=hold, scalar=V(l, V_MU + mucol), in1=dst[:, 0:1],
                                                                 op0=ALU.mult, op1=ALU.add), reads=[ko], writes=[dkey])

                for gi, (dstT, nm) in enumerate(((rT, 'rT'), (kT, 'kT'), (vT, 'vT'))):
                    slot = use_slab(si); si += 1
                    for j in range(4):
                        bank, bk = mmbank()
                        dense(bank, bk, slot, 512, j * 128, hT, hkeys, NB)
                        shift_lerp(bank, bk, gi * 4 + j, dstT[:, j, :], f'{nm}{j}')
                slot = use_slab(si); si += 1
                bank, bk = mmbank()
                dense(bank, bk, slot, 256, 0, hT, hkeys, NB)
                shift_lerp(bank, bk, 12, waf[:], 'waf')
                P.op('act', lambda e: e.activation(out=tw[0:64, :], in_=waf[0:64, :], func=AF.Tanh), reads=['waf'], writes=['tw'])
                P.op('pool', lambda e: e.tensor_copy(out=tw[64:128, :], in_=waf[64:128, :]), reads=['waf'], writes=['tw'])
                bank, bk = mmbank()
                dense(bank, bk, slot, 256, 128, hT, hkeys, NB)
                shift_lerp(bank, bk, 13, glf[:], 'glf')
                P.op('act', lambda e: e.activation(out=sgT[:], in_=glf[:], func=AF.Sigmoid), reads=['glf'], writes=['sgT'])

                def conv_u(slot):
                    for j in range(4):
                        bank, bk = mmbank()
                        dense(bank, bk, slot, 512, j * 128, hT, hkeys, NB)
                        P.op('act', lambda e, j=j, bank=bank: e.copy(out=usb[:, j, :], in_=bank[:, 0:T]), writes=[bk, f'usb{j}'])
                def conv_C(slot):
                    for j in range(4):
                        bank, bk = mmbank()
                        dense(bank, bk, slot, 512, j * 128, hT, hkeys, NB)
                        P.op('pool', lambda e, j=j: e.tensor_copy(out=zbuf[:, j, 0:2], in_=zh[:, l, j, :]), reads=[f'zh{l}_{j}'], writes=[f'zbuf{j}'])
                        P.op('dve', lambda e, j=j, bank=bank: e.tensor_tensor(out=zbuf[:, j, 2:T + 2], in0=bank[:, 0:T], in1=usb[:, j, :], op=ALU.mult),
                             reads=[f'usb{j}'], writes=[bk, f'zbuf{j}'])
                        P.op('pool', lambda e, j=j: e.tensor_copy(out=zh[:, l, j, :], in_=zbuf[:, j, T:T + 2]), reads=[f'zbuf{j}'], writes=[f'zh{l}_{j}'])
                        P.op('pool', lambda e, j=j: e.tensor_scalar(out=usb[:, j, :], in0=zbuf[:, j, 0:T], scalar1=V(l, V_CW + j), scalar2=None, op0=ALU.mult),
                             reads=[f'zbuf{j}'], writes=[f'usb{j}'])
                        P.op('dve', lambda e, j=j: e.scalar_tensor_tensor(out=usb[:, j, :], in0=zbuf[:, j, 1:T + 1], scalar=V(l, V_CW + 4 + j), in1=usb[:, j, :],
                                                                          op0=ALU.mult, op1=ALU.add), reads=[f'zbuf{j}'], writes=[f'usb{j}'])
                        P.op('dve', lambda e, j=j: e.scalar_tensor_tensor(out=usb[:, j, :], in0=zbuf[:, j, 2:T + 2], scalar=V(l, V_CW + 8 + j), in1=usb[:, j, :],
                                                                           op0=ALU.mult, op1=ALU.add), reads=[f'zbuf{j}'], writes=[f'usb{j}'])
                def conv_B(slot):
                    for j in range(4):
                        bank, bk = mmbank()
                        dense(bank, bk, slot, 512, j * 128, hT, hkeys, NB)
                        P.op('dve', lambda e, j=j, bank=bank: e.tensor_tensor(out=ycat[:, j, :], in0=bank[:, 0:T], in1=usb[:, j, :], op=ALU.mult),
                             reads=[f'usb{j}'], writes=[bk, f'ycat{j}'])
                conv_steps = [conv_u, conv_C, conv_B]
                if KSTOP < 3:
                    break
                tm = tmp
                v3 = lambda a: a.rearrange("p (c t) -> p c t", t=128)
                for pr in range(2):
                    slot = use_slab(si); si += 1
                    conv_steps[pr](slot)
                    K2 = lambda n: [n + '0', n + '1']
                    for i in range(2):
                        cc = 2 * pr + i
                        wcol = l * 512 + cc * 128
                        bank, bk = mmbank()
                        P.op('pe', lambda e, bank=bank, wcol=wcol: e.matmul(bank[:, 0:T], lhsT=lwb[0:64, wcol:wcol + 128], rhs=tw[0:64, :], start=True, stop=True),
                             reads=['tw'], writes=[bk])
                        P.op('act', lambda e, bank=bank, cc=cc, i=i: e.activation(out=tm['sg'][:, i, :], in_=bank[:, 0:T], func=AF.Sigmoid, bias=V(l, V_W0 + cc)),
                             writes=[bk, f'sg{i}'])
                        bank, bk = mmbank()
                        P.op('pe', lambda e, bank=bank, wcol=wcol: e.matmul(bank[:, 0:T], lhsT=lwb[64:128, wcol:wcol + 128], rhs=tw[64:128, :], start=True, stop=True),
                             reads=['tw'], writes=[bk])
                        P.op('act', lambda e, bank=bank, cc=cc, i=i: e.activation(out=tm['aa'][:, i, :], in_=bank[:, 0:T], func=AF.Sigmoid, bias=V(l, V_A0 + cc)),
                             writes=[bk, f'aa{i}'])
                    for i in range(2):
                        cc = 2 * pr + i
                        wcol = l * 512 + cc * 128
                        bank, bk = mmbank()
                        P.op('pe', lambda e, bank=bank, wcol=wcol: e.matmul(bank[:, 0:T], lhsT=gub[:, wcol:wcol + 128], rhs=sgT[:], start=True, stop=True),
                             reads=['sgT'], writes=[bk])
                        P.op('act', lambda e, bank=bank, cc=cc: e.copy(out=gsb[:, cc, :], in_=bank[:, 0:T]), writes=[bk, f'gsb{cc}'])
                    for i in range(2):
                        cc = 2 * pr + i
                        for ch in range(NCH):
                            cs_ = slice(ch * 128, (ch + 1) * 128)
                            P.op('dve', lambda e, cs_=cs_, i=i: e.tensor_tensor_scan(out=tm['cs'][:, i, cs_], data0=ones, data1=tm['sg'][:, i, cs_], initial=0.0,
                                                                                     op0=ALU.mult, op1=ALU.add), reads=[f'sg{i}'], writes=[f'cs{i}'])
                        P.op('dve', lambda e, cc=cc, i=i: e.tensor_scalar(out=tm['kkr'][:, i, :], in0=kT[:, cc, :], scalar1=V(l, V_KK + cc), scalar2=None, op0=ALU.mult),
                             reads=[f'kT{cc}'], writes=[f'kkr{i}'])
                    P.op('act', lambda e: e.activation(out=tm['sq'][:], in_=tm['kkr'][:], func=AF.Square), reads=K2('kkr'), writes=K2('sq'))
                    bank, bk = mmbank()
                    P.op('pe', lambda e, bank=bank: e.matmul(bank[:, 0:2 * T], lhsT=bones, rhs=tm['sq'][:].rearrange("p i t -> p (i t)"), start=True, stop=True),
                         reads=K2('sq'), writes=[bk])
                    P.op('dve', lambda e, bank=bank: e.tensor_scalar(out=tm['sq'][:].rearrange("p i t -> p (i t)"), in0=bank[:, 0:2 * T], scalar1=1e-24, scalar2=None, op0=ALU.max),
                         writes=[bk] + K2('sq'))
                    P.op('pool', lambda e: e.tensor_tensor(out=tm['sg'][:], in0=tm['cs'][:], in1=tm['sg'][:], op=ALU.subtract), reads=K2('cs'), writes=K2('sg'))
                    P.op('act', lambda e: e.activation(out=tm['eP'][:], in_=tm['cs'][:], func=AF.Exp, scale=-CDEC), reads=K2('cs'), writes=K2('eP'))
                    P.op('act', lambda e: e.activation(out=tm['eN'][:], in_=tm['cs'][:], func=AF.Exp, scale=CDEC), reads=K2('cs'), writes=K2('eN'))
                    P.op('act', lambda e: e.activation(out=tm['sg'][:], in_=tm['sg'][:], func=AF.Exp, scale=-CDEC), writes=K2('sg'))
                    P.op('act', lambda e: e.activation(out=tm['sq'][:], in_=tm['sq'][:], func=AF.Ln), writes=K2('sq'))
                    P.op('act', lambda e: e.activation(out=tm['sq'][:], in_=tm['sq'][:], func=AF.Exp, scale=-0.5), writes=K2('sq'))
                    P.op('dve', lambda e: e.tensor_scalar(out=nb[:].rearrange("p i c -> p (i c)"), in0=tm['cs'][:].rearrange("p i (c t) -> p (i c) t", t=128)[:, :, 127],
                                                          scalar1=-CDEC, scalar2=None, op0=ALU.mult), reads=K2('cs'), writes=['nb'])
                    for i in range(2):
                        cc = 2 * pr + i
                        P.op('act', lambda e, cc=cc, i=i: e.activation(out=dCt[:, :, cc], in_=nb[:, i, :], func=AF.Exp), reads=['nb'], writes=['dCt'])
                        for ch in range(NCH):
                            cs_ = slice(ch * 128, (ch + 1) * 128)
                            P.op('act', lambda e, cs_=cs_, ch=ch, i=i: e.activation(out=tm['eT'][:, i, cs_], in_=tm['cs'][:, i, cs_], func=AF.Exp, scale=CDEC, bias=nb[:, i, ch:ch + 1]),
                                 reads=[f'cs{i}', 'nb'], writes=[f'eT{i}'])
                    P.op('pool', lambda e: e.tensor_tensor(out=tm['kkr'][:], in0=tm['kkr'][:], in1=tm['sq'][:], op=ALU.mult), reads=K2('sq'), writes=K2('kkr'))
                    P.op('pool', lambda e: e.tensor_tensor(out=tm['bvec'][:], in0=tm['kkr'][:], in1=tm['aa'][:], op=ALU.mult), reads=K2('kkr') + K2('aa'), writes=K2('bvec'))
                    for i in range(2):
                        cc = 2 * pr + i
                        P.op('dve', lambda e, cc=cc, i=i: e.tensor_scalar(out=tm['aa'][:, i, :], in0=tm['aa'][:, i, :], scalar1=V(l, V_KA + cc), scalar2=omka[:, l * 4 + cc: l * 4 + cc + 1],
                                                                          op0=ALU.mult, op1=ALU.add), writes=[f'aa{i}'])
                    P.op('pool', lambda e: e.tensor_tensor(out=tm['kmod'][:], in0=kT[:, 2 * pr:2 * pr + 2, :], in1=tm['aa'][:], op=ALU.mult),
                         reads=[f'kT{2 * pr}', f'kT{2 * pr + 1}'] + K2('aa'), writes=K2('kmod'))
                    for i in range(2):
                        cc = 2 * pr + i
                        P.op('dve', lambda e, cc=cc, i=i: e.scalar_tensor_tensor(out=tm['prod'][:, i, :], in0=rT[:, cc, :], scalar=V(l, V_RK + cc), in1=tm['kmod'][:, i, :], op0=ALU.mult, op1=ALU.mult),
                             reads=[f'rT{cc}', f'kmod{i}'], writes=[f'prod{i}'])
                    bank, bk = mmbank()
                    P.op('pe', lambda e, bank=bank: e.matmul(bank[:, 0:2 * T], lhsT=bones, rhs=tm['prod'][:].rearrange("p i t -> p (i t)"), start=True, stop=True),
                         reads=K2('prod'), writes=[bk])
                    P.op('dve', lambda e, bank=bank: e.tensor_tensor(out=bonus[:, 2 * pr:2 * pr + 2, :].rearrange("p i t -> p (i t)"), in0=bank[:, 0:2 * T],
                                                                      in1=vT[:, 2 * pr:2 * pr + 2, :].rearrange("p i t -> p (i t)"), op=ALU.mult),
                         reads=[f'vT{2 * pr}', f'vT{2 * pr + 1}'], writes=[bk, f'bonus{2 * pr}', f'bonus{2 * pr + 1}'])
                    for i in range(2):
                        cc = 2 * pr + i
                        P.op('dve', lambda e, cc=cc, i=i: e.scalar_tensor_tensor(out=AR[:, cc, :, 0:128], in0=v3(tm['kkr'][:, i, :]), scalar=-1.0, in1=v3(tm['sg'][:, i, :]), op0=ALU.mult, op1=ALU.mult),
                             reads=[f'kkr{i}', f'sg{i}'], writes=[f'AR{cc}'])
                        P.op('pool', lambda e, cc=cc, i=i: e.tensor_tensor(out=AR[:, cc, :, 128:256], in0=v3(rT[:, cc, :]), in1=v3(tm['eP'][:, i, :]), op=ALU.mult),
                             reads=[f'rT{cc}', f'eP{i}'], writes=[f'AR{cc}'])
                    pk = lambda n: [f'{n}{2 * pr}', f'{n}{2 * pr + 1}']
                    P.op('dve', lambda e: e.tensor_tensor(out=KT[:, 2 * pr:2 * pr + 2, :], in0=tm['kmod'][:], in1=tm['eN'][:], op=ALU.mult), reads=K2('kmod') + K2('eN'), writes=pk('KT'))
                    P.op('pool', lambda e: e.tensor_tensor(out=BT[:, 2 * pr:2 * pr + 2, :], in0=tm['bvec'][:], in1=tm['eN'][:], op=ALU.mult), reads=K2('bvec') + K2('eN'), writes=pk('BT'))
                    P.op('dve', lambda e: e.tensor_tensor(out=KTL[:, 2 * pr:2 * pr + 2, :], in0=tm['kmod'][:], in1=tm['eT'][:], op=ALU.mult), reads=K2('kmod') + K2('eT'), writes=pk('KTL'))
                    P.op('pool', lambda e: e.tensor_tensor(out=BTL[:, 2 * pr:2 * pr + 2, :], in0=tm['bvec'][:], in1=tm['eT'][:], op=ALU.mult), reads=K2('bvec') + K2('eT'), writes=pk('BTL'))
                slot = use_slab(si); si += 1
                conv_steps[2](slot)

                if KSTOP < 4:
                    break
                for ch in range(NCH):
                    cs_ = slice(ch * 128, (ch + 1) * 128)
                    for cc in range(4):
                        P.op('pe', lambda e, cc=cc: e.transpose(out=tpf[:, cc * 128:(cc + 1) * 128], in_=vT[:, cc, cs_], identity=ident),
                             reads=[f'vT{cc}'], writes=['tpf'], inc=(cc == 3))
                    P.op('act', lambda e: e.copy(out=Vf[:], in_=tpf[:]), writes=['tpf', 'Vf'])
                    P.op('pool', lambda e: e.tensor_copy(out=Vb[:], in_=Vf[:]), reads=['Vf'], writes=['Vb'])
                    for cc in range(4):
                        P.op('pe', lambda e, cc=cc: e.transpose(out=tpb[:, cc * 128:(cc + 1) * 128], in_=KTL[:, cc, cs_], identity=identb[:]),
                             reads=[f'KTL{cc}'], writes=['tpb'], inc=(cc == 3))
                    P.op('dve', lambda e: e.tensor_copy(out=KTLt[:], in_=tpb[:, 0:512]), writes=['tpb', 'KTLt'])
                    for cc in range(4):
                        P.op('pe', lambda e, cc=cc: e.transpose(out=tpb[:, cc * 128:(cc + 1) * 128], in_=BTL[:, cc, cs_], identity=identb[:]),
                             reads=[f'BTL{cc}'], writes=['tpb'], inc=(cc == 3))
                    P.op('act', lambda e: e.copy(out=BTLt[:], in_=tpb[:, 0:512]), writes=['tpb', 'BTLt'])
                    for h in range(8):
                        cc, po = h // 2, 64 * (h % 2)
                        bank, bk = (sA, 'sA') if h % 2 == 0 else (sB, 'sB')
                        P.op('pe', lambda e, cc=cc, po=po, bank=bank: e.matmul(bank[:, 0:256], lhsT=BT[po:po + 64, cc, cs_], rhs=AR[po:po + 64, cc, ch, :], start=True, stop=True),
                             reads=[f'BT{cc}', f'AR{cc}'], writes=[bk], inc=False)
                        P.op('pe', lambda e, cc=cc, po=po, bank=bank: e.matmul(bank[:, 256:512], lhsT=KT[po:po + 64, cc, cs_], rhs=AR[po:po + 64, cc, ch, :], start=True, stop=True),
                             reads=[f'KT{cc}', f'AR{cc}'], writes=[bk])
                        P.op('dve', lambda e, h=h, bank=bank: e.tensor_tensor(out=Sc[:, h, :], in0=bank[:], in1=mask4, op=ALU.mult), writes=[bk, f'Sc{h}'])
                    for g in range(2):
                        bank, bk = (sA, 'sA') if g == 0 else (sB, 'sB')
                        po = 64 * g
                        for hh in range(4):
                            h = 2 * hh + g
                            cc = h // 2
                            P.op('pe', lambda e, cc=cc, po=po, bank=bank, hh=hh: e.matmul(bank[:, hh * 128:(hh + 1) * 128], lhsT=AR[po:po + 64, cc, ch, 0:128], rhs=BT[po:po + 64, cc, cs_],
                                                                                          start=True, stop=True), reads=[f'BT{cc}', f'AR{cc}'], writes=[bk], inc=(hh == 3))
                        P.op('dve', lambda e, g=g, bank=bank: e.tensor_tensor(out=Nm[0][:, g::2, :], in0=bank[:].rearrange("p (h t) -> p h t", t=128),
                                                                              in1=maskN4.rearrange("p (h t) -> p h t", t=128), op=ALU.mult), writes=[bk, 'Nm0_0', 'Nm0_1'])
                    for g in range(2):
                        P.op('pool', lambda e, g=g: e.tensor_tensor(out=Pm[0][:, 4 * g:4 * g + 4, :], in0=Sc[:, 4 * g:4 * g + 4, 0:128],
                                                                    in1=identb[:].unsqueeze(1).to_broadcast([128, 4, 128]), op=ALU.add),
                             reads=[f'Sc{4 * g + i}' for i in range(4)], writes=[f'Pm0_{g}'])
                    abank = [(sA, 'sA'), (xu, 'xu')]
                    nbank = [(sB, 'sB'), (hp, 'hp')]
                    pbank = [(mm0, 'mm0'), (mm1, 'mm1')]
                    for lev in range(6):
                        ci, ni = lev % 2, (lev + 1) % 2
                        for g in range(2):
                            hs = [4 * g + i for i in range(4)]
                            if lev == 0:
                                Acur = lambda h: Sc[:, h, 0:128]
                                akeys = [f'Sc{h}' for h in hs]
                            else:
                                Acur = lambda h, ci=ci: Am[ci][:, h, :]
                                akeys = [f'Am{ci}_{g}']
                            nkeys = [f'Nm{ci}_{g}']
                            if lev < 5:
                                bank, bk = abank[g]
                                for hh, h in enumerate(hs):
                                    P.op('pe', lambda e, hh=hh, h=h, Acur=Acur, ci=ci, bank=bank: e.matmul(bank[:, hh * 128:(hh + 1) * 128], lhsT=Nm[ci][:, h, :], rhs=Acur(h), start=True, stop=True),
                                         reads=akeys + nkeys, writes=[bk], inc=(hh == 3))
                                P.op('act', lambda e, g=g, ni=ni, bank=bank: e.copy(out=Am[ni][:, 4 * g:4 * g + 4, :], in_=bank[:].rearrange("p (h t) -> p h t", t=128)),
                                     writes=[bk, f'Am{ni}_{g}'])
                            bank, bk = nbank[g]
                            for hh, h in enumerate(hs):
                                P.op('pe', lambda e, hh=hh, h=h, Acur=Acur, ci=ci, bank=bank: e.matmul(bank[:, hh * 128:(hh + 1) * 128], lhsT=Acur(h), rhs=Nm[ci][:, h, :], start=True, stop=True),
                                     reads=akeys + nkeys, writes=[bk], inc=(hh == 3))
                            P.op('dve', lambda e, g=g, ni=ni, bank=bank: e.tensor_copy(out=Nm[ni][:, 4 * g:4 * g + 4, :], in_=bank[:].rearrange("p (h t) -> p h t", t=128)),
                                 writes=[bk, f'Nm{ni}_{g}'])
                        for g in range(2):
                            hs = [4 * g + i for i in range(4)]
                            bank, bk = pbank[g]
                            for hh, h in enumerate(hs):
                                P.op('pe', lambda e, hh=hh, h=h, ni=ni, ci=ci, bank=bank: e.matmul(bank[:, hh * 128:(hh + 1) * 128], lhsT=Nm[ni][:, h, :], rhs=Pm[ci][:, h, :], start=True, stop=True),
                                     reads=[f'Nm{ni}_{g}', f'Pm{ci}_{g}'], writes=[bk], inc=(hh == 3))
                            P.op('dve', lambda e, g=g, ni=ni, ci=ci, bank=bank: e.tensor_tensor(out=Pm[ni][:, 4 * g:4 * g + 4, :], in0=bank[:].rearrange("p (h t) -> p h t", t=128),
                                                                                              in1=Pm[ci][:, 4 * g:4 * g + 4, :], op=ALU.add),
                                 reads=[f'Pm{ci}_{g}'], writes=[bk, f'Pm{ni}_{g}'])
                    Pf = Pm[0]
                    pfk = ['Pm0_0', 'Pm0_1']
                    sck = [f'Sc{h}' for h in range(8)]
                    for h in range(8):
                        cc, po = h // 2, 64 * (h % 2)
                        hs_ = slice(h * 64, (h + 1) * 64)
                        P.op('pe', lambda e, h=h, hs_=hs_: e.matmul(xu[:, hs_], lhsT=Sc[:, h, 256:384], rhs=Vb[:, hs_], start=True, stop=False),
                             reads=[f'Sc{h}', 'Vb'], writes=['xu'], inc=False)
                        P.op('pe', lambda e, cc=cc, po=po, hs_=hs_: e.matmul(xu[:, hs_], lhsT=AR[po:po + 64, cc, ch, 0:128], rhs=Hb[po:po + 64, l, cc, :], start=False, stop=True),
                             reads=[f'AR{cc}', f'Hb{l}'], writes=['xu'], inc=(h == 7))
                    P.op('act', lambda e: e.copy(out=Xb[:], in_=xu[:]), writes=['xu', 'Xb'])
                    for h in range(8):
                        hs_ = slice(h * 64, (h + 1) * 64)
                        P.op('pe', lambda e, h=h, hs_=hs_: e.matmul(xu[:, hs_], lhsT=Pf[:, h, :], rhs=Xb[:, hs_], start=True, stop=True),
                             reads=pfk + ['Xb'], writes=['xu'], inc=(h == 7))
                    P.op('dve', lambda e: e.tensor_copy(out=Ub[:], in_=xu[:]), writes=['xu', 'Ub'])
                    for h in range(8):
                        cc, po = h // 2, 64 * (h % 2)
                        hs_ = slice(h * 64, (h + 1) * 64)
                        P.op('pe', lambda e, cc=cc, po=po, hs_=hs_: e.matmul(tpf[:, hs_], lhsT=AR[po:po + 64, cc, ch, 128:256], rhs=Hb[po:po + 64, l, cc, :], start=True, stop=False),
                             reads=[f'AR{cc}', f'Hb{l}'], writes=['tpf'], inc=False)
                        P.op('pe', lambda e, h=h, hs_=hs_: e.matmul(tpf[:, hs_], lhsT=Sc[:, h, 128:256], rhs=Ub[:, hs_], start=False, stop=False),
                             reads=[f'Sc{h}', 'Ub'], writes=['tpf'], inc=False)
                        P.op('pe', lambda e, h=h, hs_=hs_: e.matmul(tpf[:, hs_], lhsT=Sc[:, h, 384:512], rhs=Vb[:, hs_], start=False, stop=True),
                             reads=[f'Sc{h}', 'Vb'], writes=['tpf'], inc=(h == 7))
                    for h in range(8):
                        cc, po = h // 2, 64 * (h % 2)
                        hs_ = slice(h * 64, (h + 1) * 64)
                        P.op('pe', lambda e, cc=cc, po=po, hs_=hs_: e.matmul(hp[po:po + 64, cc * 64:(cc + 1) * 64], lhsT=BTLt[:, hs_], rhs=Ub[:, hs_], start=True, stop=False),
                             reads=['BTLt', 'Ub'], writes=['hp'], inc=False)
                        P.op('pe', lambda e, cc=cc, po=po, hs_=hs_: e.matmul(hp[po:po + 64, cc * 64:(cc + 1) * 64], lhsT=KTLt[:, hs_], rhs=Vb[:, hs_], start=False, stop=True),
                             reads=['KTLt', 'Vb'], writes=['hp'], inc=(h == 7))
                    P.op('dve', lambda e: e.tensor_tensor(out=Hm[:, l, :, :], in0=Hm[:, l, :, :], in1=dCt[:, ch, :].unsqueeze(2).to_broadcast([128, 4, 64]), op=ALU.mult),
                         reads=['dCt', f'Hb{l}'], writes=[f'Hm{l}'])
                    P.op('dve', lambda e: e.tensor_tensor(out=Hm[:, l, :, :], in0=Hm[:, l, :, :], in1=hp[:, 0:256].rearrange("p (c v) -> p c v", v=64), op=ALU.add),
                         writes=['hp', f'Hm{l}'])
                    P.op('pool', lambda e: e.tensor_copy(out=Hb[:, l, :, :], in_=Hm[:, l, :, :]), reads=[f'Hm{l}'], writes=[f'Hb{l}'])
                    P.op('act', lambda e: e.copy(out=ysb[:], in_=tpf[:]), writes=['tpf', 'ysb'])
                    P.op('pool', lambda e: e.tensor_tensor(out=ysq[:], in0=ysb[:], in1=ysb[:], op=ALU.mult), reads=['ysb'], writes=['ysq'])
                    y3 = lambda a: a.rearrange("p (h v) -> p h v", v=64)
                    P.op('dve', lambda e: e.tensor_reduce(out=gst[:, 0:8], in_=y3(ysb[:]), axis=AX.X, op=ALU.add), reads=['ysb'], writes=['gst0'])
                    P.op('dve', lambda e: e.tensor_reduce(out=gst[:, 8:16], in_=y3(ysq[:]), axis=AX.X, op=ALU.add), reads=['ysq'], writes=['gst1'])
                    P.op('dve', lambda e: e.tensor_scalar(out=gst[:, 16:24], in0=gst[:, 0:8], scalar1=1.0 / 64, scalar2=None, op0=ALU.mult), reads=['gst0'], writes=['gst2'])
                    P.op('pool', lambda e: e.tensor_tensor(out=gst[:, 24:32], in0=gst[:, 16:24], in1=gst[:, 16:24], op=ALU.mult), reads=['gst2'], writes=['gst3'])
                    P.op('dve', lambda e: e.scalar_tensor_tensor(out=gst[:, 32:40], in0=gst[:, 8:16], scalar=1.0 / 64, in1=gst[:, 24:32], op0=ALU.mult, op1=ALU.subtract),
                         reads=['gst1', 'gst3'], writes=['gst4'])
                    P.op('act', lambda e: e.activation(out=gst[:, 24:32], in_=gst[:, 32:40], func=AF.Ln, bias=GN_EPS), reads=['gst4'], writes=['gst3'])
                    P.op('act', lambda e: e.activation(out=gst[:, 24:32], in_=gst[:, 24:32], func=AF.Exp, scale=-0.5), writes=['gst3'])
                    P.op('dve', lambda e: e.tensor_tensor(out=y3(ync[:]), in0=y3(ysb[:]), in1=gst[:, 16:24].unsqueeze(2).to_broadcast([128, 8, 64]), op=ALU.subtract),
                         reads=['ysb', 'gst2'], writes=['ync'])
                    P.op('pool', lambda e: e.tensor_tensor(out=y3(ynb[:]), in0=y3(ync[:]), in1=gst[:, 24:32].unsqueeze(2).to_broadcast([128, 8, 64]), op=ALU.mult),
                         reads=['ync', 'gst3'], writes=['ynb'])
                    for cc in range(4):
                        P.op('pe', lambda e, cc=cc: e.transpose(out=tpb[:, cc * 128:(cc + 1) * 128], in_=ynb[:, cc * 128:(cc + 1) * 128], identity=identb[:]),
                             reads=['ynb'], writes=['tpb'], inc=(cc == 3))
                    for cc in range(4):
                        b = cc % 2
                        P.op('dve', lambda e, cc=cc, b=b: e.tensor_scalar(out=otmp[b][:], in0=tpb[:, cc * 128:(cc + 1) * 128], scalar1=V(l, V_LG + cc), scalar2=V(l, V_LB + cc),
                                                                          op0=ALU.mult, op1=ALU.add), writes=['tpb', f'otmp{b}'])
                        P.op('pool', lambda e, cc=cc, b=b: e.tensor_tensor(out=otmp[b][:], in0=otmp[b][:], in1=bonus[:, cc, cs_], op=ALU.add),
                             reads=[f'bonus{cc}'], writes=[f'otmp{b}'])
                        P.op('dve', lambda e, cc=cc, b=b: e.tensor_tensor(out=ycat[:, 4 + cc, cs_], in0=otmp[b][:], in1=gsb[:, cc, cs_], op=ALU.mult),
                             reads=[f'otmp{b}', f'gsb{cc}'], writes=[f'ycat{4 + cc}'])

                if KSTOP < 5:
                    break
                for half in range(2):
                    slot = use_slab(si); si += 1
                    for jj in range(4):
                        j = half * 4 + jj
                        bank, bk = mmbank()
                        dense(bank, bk, slot, 512, jj * 128, ycat, ykeys, NB)
                        P.op('dve', lambda e, j=j, bank=bank: e.scalar_tensor_tensor(out=xT[:, j, :], in0=bank[:, 0:T], scalar=DER(l, DG1 + j), in1=xT[:, j, :], op0=ALU.mult, op1=ALU.add),
                             writes=[bk, f'xT{j}'])
                layernorm(l, V_L1G, V_L1B, 0, l)
                if KSTOP < 6:
                    break
                for s in range(11):
                    slot = use_slab(si); si += 1
                    for qq in range(2):
                        (gbank, gk), (ubank, uk) = [((mm0, 'mm0'), (mm1, 'mm1')), ((sA, 'sA'), (sB, 'sB')), ((xu, 'xu'), (hp, 'hp'))][(2 * s + qq) % 3]
                        for kc in range(NB):
                            P.op('pe', lambda e, kc=kc, qq=qq, slot=slot, gbank=gbank: e.matmul(gbank[:, 0:T], lhsT=ring[:, slot, kc * 512 + qq * 128: kc * 512 + qq * 128 + 128], rhs=hT[:, kc, :],
                                                                                                start=(kc == 0), stop=(kc == NB - 1)), reads=[f'ring{slot}'] + hkeys, writes=[gk], inc=(kc == NB - 1))
                        for kc in range(NB):
                            P.op('pe', lambda e, kc=kc, qq=qq, slot=slot, ubank=ubank: e.matmul(ubank[:, 0:T], lhsT=ring[:, slot, kc * 512 + 256 + qq * 128: kc * 512 + 256 + qq * 128 + 128], rhs=hT[:, kc, :],
                                                                                                start=(kc == 0), stop=(kc == NB - 1)), reads=[f'ring{slot}'] + hkeys, writes=[uk], inc=(kc == NB - 1))
                        b = qq
                        P.op('act', lambda e, b=b, gbank=gbank: e.activation(out=slt[b][:], in_=gbank[:, 0:T], func=AF.Silu), writes=[gk, f'slt{b}'])
                        P.op('dve', lambda e, b=b, s=s, qq=qq, ubank=ubank: e.tensor_tensor(out=act[:, 2 * s + qq, :], in0=ubank[:, 0:T], in1=slt[b][:], op=ALU.mult),
                             reads=[f'slt{b}'], writes=[uk, f'act{2 * s + qq}'])
                akeys_ = [f'act{f}' for f in range(NF)]
                for j in range(NB):
                    slot = use_slab(si); si += 1
                    bank, bk = mmbank()
                    dense(bank, bk, slot, 128, 0, act, akeys_, NF)
                    P.op('dve', lambda e, j=j, bank=bank: e.scalar_tensor_tensor(out=xT[:, j, :], in0=bank[:, 0:T], scalar=DER(l, DG2 + j), in1=xT[:, j, :], op0=ALU.mult, op1=ALU.add),
                         writes=[bk, f'xT{j}'])
                layernorm(l, V_L2G, V_L2B, 16, (l + 1) if l + 1 < L else None)
            P.dma('yout', yT_d[:, t0:t0 + T].rearrange("(j p) t -> p j t", p=128), xT[:], reads=[f'xT{j}' for j in range(NB)])
        P._wait('sp', ('yout', P.chan_count['yout']))
        P.emit()
        build_nc.stats = (P.nops, P.nwaits)
    return nc


def _slab_std(W, c0, ncols):
    blk = W.reshape(8, 128, W.shape[1])[:, :, c0:c0 + ncols].transpose(1, 0, 2).reshape(128, 8 * ncols)
    out = np.zeros((128, SLAB), np.float32)
    out[:, :8 * ncols] = blk
    return out


def _vec_cols(v):
    v = np.asarray(v, np.float32).reshape(-1, 128)
    return v.T


def prepare_shared(inp, L):
    wslab = np.zeros((L * NSLAB, 128, SLAB), np.float32)
    wmod = np.zeros((L * NMOD, 128, SLAB), np.float32)
    vecs = np.zeros((128, L, NV), np.float32)
    lw = np.zeros((128, L, 512), np.float32)
    gu = np.zeros((128, L, 512), np.float32)
    for l in range(L):
        Win = np.asarray(inp["w_in"][l], np.float32)
        s = l * NSLAB
        for k, c0 in ((0, 1536), (1, 2048), (2, 2560), (4, 1024), (5, 512), (6, 0)):
            wslab[s + k] = _slab_std(Win, c0, 512)
        wslab[s + 3] = _slab_std(Win, 3072, 256)
        Wout = np.asarray(inp["w_out"][l], np.float32)
        wslab[s + 7] = _slab_std(Wout, 0, 512)
        wslab[s + 8] = _slab_std(Wout, 512, 512)
        W1 = np.asarray(inp["w_ffn_in"][l], np.float32)
        W1r = W1.reshape(8, 128, 2 * DFF)
        for k in range(11):
            g = W1r[:, :, 256 * k:256 * k + 256]
            u = W1r[:, :, DFF + 256 * k:DFF + 256 * k + 256]
            wslab[s + 9 + k] = np.concatenate([g, u], axis=2).transpose(1, 0, 2).reshape(128, SLAB)
        W2 = np.asarray(inp["w_ffn_out"][l], np.float32).reshape(NF, 128, D)
        for j in range(8):
            wslab[s + 20 + j, :, :NF * 128] = W2[:, :, j * 128:(j + 1) * 128].transpose(1, 0, 2).reshape(128, NF * 128)
        Wm = np.asarray(inp["w_mod"][l], np.float32)
        for k in range(NMOD):
            wmod[l * NMOD + k] = _slab_std(Wm, 512 * k, 512)
        mu = np.asarray(inp["mu_shift"][l], np.float32)
        cols = [_vec_cols(mu)]
        for nm in ("w0", "a0", "k_k", "k_a"):
            cols.append(_vec_cols(inp[nm][l]))
        cols.append(_vec_cols(np.asarray(inp["r_k"][l]).reshape(-1)))
        cols.append(_vec_cols(inp["lnx_g"][l]))
        cols.append(_vec_cols(inp["lnx_b"][l]))
        cw = np.asarray(inp["conv_w"][l], np.float32)
        for tap in range(3):
            cols.append(_vec_cols(cw[tap]))
        for nm in ("ln1_g", "ln1_b", "ln2_g", "ln2_b"):
            cols.append(_vec_cols(inp[nm][l]))
        cols.append(_vec_cols(inp["b_mod"][l]))
        vecs[:, l, :] = np.concatenate(cols, axis=1)
        lw[0:64, l, :] = np.asarray(inp["w_decay_up"][l], np.float32)
        lw[64:128, l, :] = np.asarray(inp["a_up"][l], np.float32)
        gu[:, l, :] = np.asarray(inp["g_up"][l], np.float32)
    consts = np.zeros((128, 1408), np.float32)
    consts[:, 0:128] = np.eye(128)
    s_idx = np.arange(128)[:, None]
    t_idx = np.arange(128)[None, :]
    strict = (s_idx < t_idx).astype(np.float32)
    incl = (s_idx <= t_idx).astype(np.float32)
    consts[:, 128:640] = np.concatenate([strict, incl, strict, incl], axis=1)
    lowN = (t_idx < s_idx).astype(np.float32)
    consts[:, 640:1152] = np.concatenate([lowN] * 4, axis=1)
    consts[:, 1152:1280] = ((s_idx // 64) == (t_idx // 64)).astype(np.float32)
    consts[:, 1280:1408] = 1.0
    return dict(wslab=wslab, wmod=wmod, vecs=np.ascontiguousarray(vecs.reshape(128, L * NV)),
                lw=np.ascontiguousarray(lw.reshape(128, L * 512)), gu=np.ascontiguousarray(gu.reshape(128, L * 512)), consts=consts)


def run(inp, SEQ, L, T, runner):
    x = np.asarray(inp["x"], np.float32)
    c = np.asarray(inp["c"], np.float32)
    B = x.shape[0]
    shared = prepare_shared(inp, L)
    nc = build_nc(SEQ, L, T)
    seq_cores = [0, 2, 4, 6][:B]
    zeros = {k: np.zeros_like(v) for k, v in shared.items()}
    zeros["xT"] = np.zeros((D, SEQ), np.float32)
    zeros["cT"] = np.zeros((128, 8), np.float32)
    in_maps = []
    for core in range(8):
        if core in seq_cores:
            b = seq_cores.index(core)
            m = dict(shared)
            m["xT"] = np.ascontiguousarray(x[b].T)
            m["cT"] = np.ascontiguousarray(c[b].reshape(8, 128).T)
        else:
            m = zeros
        in_maps.append(m)
    res = runner(nc, in_maps)
    out = np.stack([np.ascontiguousarray(res[seq_cores[b]]["yT"].T) for b in range(B)], axis=0)
    return out.astype(np.float32)


def kernel(**inputs):
    def runner(nc, in_maps):
        r = run_bass_kernel_spmd(nc, in_maps, core_ids=list(range(8)))
        return r.results
    return run(inputs, 8192, 4, 256, runner)
```

```python
import math
import os
KSTOP = int(os.environ.get('KSTOP', '99'))
import numpy as np
from contextlib import ExitStack
import concourse.bass as bass
import concourse.mybir as mybir
from concourse.bass_utils import run_bass_kernel_spmd

F32 = mybir.dt.float32
BF16 = mybir.dt.bfloat16
AF = mybir.ActivationFunctionType
ALU = mybir.AluOpType
AX = mybir.AxisListType

D = 1024
NB = 8
DFF = 2816
NF = 22
NSLAB = 28
NMOD = 12
NV = 134
SLAB = 4096
ALPHA = (2.0 * 4) ** 0.25
LN_EPS = 1e-5
GN_EPS = 64e-5
CDEC = math.exp(-0.5)

V_MU = 0
V_W0 = 14
V_A0 = 18
V_KK = 22
V_KA = 26
V_RK = 30
V_LG = 34
V_LB = 38
V_CW = 42
V_L1G = 54
V_L1B = 62
V_L2G = 70
V_L2B = 78
V_BM = 86
DB1, DA1, DG1, DB2, DA2, DG2 = 0, 8, 16, 24, 32, 40


class _Rec:
    def __getattr__(self, name):
        return lambda *a, **k: (name, a, k)


_REC = _Rec()


class Prog:
    EPOCH = 16000

    def __init__(self, nc, stack):
        self.nc = nc
        self.stack = stack
        self.engs = ('sp', 'pe', 'act', 'dve', 'pool')
        self.streams = {e: [] for e in self.engs}
        self.count = {e: 0 for e in self.engs}
        self.clock = {e: {} for e in self.engs}
        self.iclock = {}
        self.sems = {}
        self.last_write = {}
        self.reads = {}
        self.chan_count = {}
        self.nwaits = 0
        self.nops = 0

    def _sem(self, name):
        if name not in self.sems:
            self.sems[name] = self.stack.enter_context(self.nc.semaphore(name.replace(':', '_')))
        return self.sems[name]

    def _wait(self, eng, dep, force=False):
        e2, n = dep
        if n <= 0:
            return
        if not force and self.clock[eng].get(e2, 0) >= n:
            return
        if e2 in self.engs:
            ep = (n - 1) // self.EPOCH
            sem = self._sem(f"{e2}:{ep}")
            val = n - ep * self.EPOCH
        else:
            sem = self._sem(e2)
            val = 16 * n
        self.streams[eng].append(('wait', sem, val))
        self.nwaits += 1
        ck = self.clock[eng]
        for k, v in self.iclock.get(dep, {}).items():
            if ck.get(k, 0) < v:
                ck[k] = v
        if ck.get(e2, 0) < n:
            ck[e2] = n

    def _deps(self, eng, ident, reads, writes):
        deps = set()
        for k in reads:
            lw = self.last_write.get(k)
            if lw is not None:
                deps.add(lw)
        for k in writes:
            lw = self.last_write.get(k)
            if lw is not None:
                deps.add(lw)
            for r in self.reads.get(k, ()):
                deps.add(r)
        for d in sorted(deps):
            e2, n = d
            if e2 == eng:
                if eng != 'pe' and eng != 'sp':
                    self._wait(eng, d)
                continue
            self._wait(eng, d)
        for k in reads:
            self.reads.setdefault(k, []).append(ident)
        for k in writes:
            self.last_write[k] = ident
            self.reads[k] = []

    def op(self, eng, fn, reads=(), writes=(), inc=True):
        n = self.count[eng] + 1
        ident = (eng, n)
        self._deps(eng, ident, reads, writes)
        self.nops += 1
        if not inc:
            self.streams[eng].append(('op', fn(_REC), None, 0))
            return
        self.count[eng] = n
        ep = (n - 1) // self.EPOCH
        sem = self._sem(f"{eng}:{ep}")
        self.streams[eng].append(('op', fn(_REC), sem, 1))
        self.iclock[ident] = dict(self.clock[eng])

    def dma(self, chan, out, in_, reads=(), writes=(), eng='sp'):
        n = self.chan_count.get(chan, 0) + 1
        ident = (chan, n)
        self._deps(eng, ident, reads, writes)
        self.chan_count[chan] = n
        sem = self._sem(chan)
        self.streams[eng].append(('op', ('dma_start', (), dict(out=out, in_=in_)), sem, 16))
        self.iclock[ident] = dict(self.clock[eng])
        self.nops += 1

    def barrier(self):
        for e in self.engs:
            for e2 in self.engs:
                if e2 != e and e2 != 'sp':
                    self._wait(e, (e2, self.count[e2]))
            for ch, n in self.chan_count.items():
                self._wait(e, (ch, n))
        self.last_write = {}
        self.reads = {}

    def emit(self):
        with self.nc.Block() as block:
            for ename, deco in (('sp', block.sync), ('pe', block.tensor), ('act', block.scalar),
                                ('dve', block.vector), ('pool', block.gpsimd)):
                stream = self.streams[ename]

                def body(e, stream=stream):
                    for it in stream:
                        if it[0] == 'wait':
                            e.wait_ge(it[1], it[2])
                        else:
                            name, a, k = it[1]
                            ins = getattr(e, name)(*a, **k)
                            if it[2] is not None:
                                ins.then_inc(it[2], it[3])
                deco(body)
        self.streams = {e: [] for e in self.engs}


def build_nc(SEQ, L, T):
    NCH = T // 128
    NT = SEQ // T
    nc = bass.Bass("TRN2", target_bir_lowering=False)
    xT_d = nc.dram_tensor("xT", [D, SEQ], F32, kind="ExternalInput").ap()
    cT_d = nc.dram_tensor("cT", [128, 8], F32, kind="ExternalInput").ap()
    vecs_d = nc.dram_tensor("vecs", [128, L * NV], F32, kind="ExternalInput").ap()
    wslab_d = nc.dram_tensor("wslab", [L * NSLAB, 128, SLAB], F32, kind="ExternalInput").ap()
    wmod_d = nc.dram_tensor("wmod", [L * NMOD, 128, SLAB], F32, kind="ExternalInput").ap()
    lw_d = nc.dram_tensor("lw", [128, L * 512], F32, kind="ExternalInput").ap()
    gu_d = nc.dram_tensor("gu", [128, L * 512], F32, kind="ExternalInput").ap()
    consts_d = nc.dram_tensor("consts", [128, 1408], F32, kind="ExternalInput").ap()
    yT_d = nc.dram_tensor("yT", [D, SEQ], F32, kind="ExternalOutput").ap()
    wsb_d = nc.dram_tensor("wsb", [L * NSLAB, 128, SLAB], BF16, kind="Internal").ap()

    with ExitStack() as st:
        P = Prog(nc, st)

        def sb(name, shape, dt, stack=st):
            return stack.enter_context(nc.sbuf_tensor("s_" + name, shape, dt))

        def ps(name, shape, dt):
            return st.enter_context(nc.psum_tensor("p_" + name, shape, dt))

        mm0 = ps("mm0", [128, 512], F32)
        mm1 = ps("mm1", [128, 512], F32)
        sA = ps("sA", [128, 512], F32)
        sB = ps("sB", [128, 512], F32)
        tpf = ps("tpf", [128, 512], F32)
        tpb = ps("tpb", [128, 1024], BF16)
        xu = ps("xu", [128, 512], F32)
        hp = ps("hp", [128, 512], F32)

        consts = sb("consts", [128, 1408], F32)
        ident = consts[:, 0:128]
        mask4 = consts[:, 128:640]
        maskN4 = consts[:, 640:1152]
        bones = consts[:, 1152:1280]
        ones = consts[:, 1280:1408]
        identb = sb("identb", [128, 128], BF16)
        onesb = sb("onesb", [128, 128], BF16)
        vecs = sb("vecs", [128, L * NV], F32)
        cT = sb("cT", [128, 8], F32)
        scT = sb("scT", [128, 8], F32)
        mod = sb("mod", [128, L * 48], F32)
        der = sb("der", [128, L * 48], F32)
        omka = sb("omka", [128, L * 4], F32)
        lwb = sb("lwb", [128, L * 512], BF16)
        gub = sb("gub", [128, L * 512], BF16)
        Hm = sb("Hm", [128, L, 4, 64], F32)
        Hb = sb("Hb", [128, L, 4, 64], BF16)
        hist2 = [sb(f"hist{i}", [128, L * 14], F32) for i in range(2)]
        omu = sb("omu", [128, L * 14], F32)
        der2 = sb("der2", [128, L * 32], F32)
        zh = sb("zh", [128, L, 4, 2], F32)

        def V(l, col, n=1):
            return vecs[:, l * NV + col: l * NV + col + n]

        def DER(l, col, n=1):
            return der[:, l * 48 + col: l * 48 + col + n]

        with ExitStack() as pst:
            stg = [sb(f"stg{i}", [128, SLAB], F32, pst) for i in range(2)]
            stb = [sb(f"stb{i}", [128, SLAB], BF16, pst) for i in range(2)]
            P.dma('c0', consts[:], consts_d[:, :], writes=['consts'])
            P.dma('c1', vecs[:], vecs_d[:, :], writes=['vecs'])
            P.dma('c2', cT[:], cT_d[:, :], writes=['cT'])
            P.op('dve', lambda e: e.tensor_copy(out=identb[:], in_=ident), reads=['consts'], writes=['identb'])
            P.op('dve', lambda e: e.tensor_copy(out=onesb[:], in_=ones), reads=['consts'], writes=['onesb'])
            P.op('act', lambda e: e.activation(out=scT[:], in_=cT[:], func=AF.Silu), reads=['cT'], writes=['scT'])
            P.op('pool', lambda e: e.memset(Hm[:], 0.0), writes=['Hm'])
            P.op('pool', lambda e: e.memset(Hb[:], 0.0), writes=['Hb'])
            P.op('pool', lambda e: e.memset(hist2[0][:], 0.0), writes=['hist0'])
            P.op('pool', lambda e: e.memset(hist2[1][:], 0.0), writes=['hist1'])
            P.op('pool', lambda e: e.memset(zh[:], 0.0), writes=['zh'])
            P.dma('stg0', stg[0][:, 0:L * 512], lw_d[:, :], writes=['stg0'])
            P.op('dve', lambda e: e.tensor_copy(out=lwb[:], in_=stg[0][:, 0:L * 512]), reads=['stg0'], writes=['lwb'])
            P.dma('stg1', stg[1][:, 0:L * 512], gu_d[:, :], writes=['stg1'])
            P.op('dve', lambda e: e.tensor_copy(out=gub[:], in_=stg[1][:, 0:L * 512]), reads=['stg1'], writes=['gub'])
            q = 0
            for l in range(L):
                for s in range(NMOD):
                    b = q % 2
                    q += 1
                    P.dma(f'stg{b}', stg[b][:], wmod_d[l * NMOD + s, :, :], writes=[f'stg{b}'])
                    for jj in range(4):
                        col = l * 48 + s * 4 + jj
                        for kc in range(8):
                            P.op('pe', lambda e, b=b, jj=jj, kc=kc, col=col: e.matmul(
                                hp[:, col:col + 1], lhsT=stg[b][:, kc * 512 + jj * 128: kc * 512 + jj * 128 + 128],
                                rhs=scT[:, kc:kc + 1], start=(kc == 0), stop=(kc == 7)),
                                reads=[f'stg{b}', 'scT'], writes=['hp'], inc=(kc == 7))
            for l in range(L):
                P.op('dve', lambda e, l=l: e.tensor_tensor(out=mod[:, l * 48:(l + 1) * 48], in0=hp[:, l * 48:(l + 1) * 48],
                                                          in1=V(l, V_BM, 48), op=ALU.add),
                     reads=['vecs'], writes=['hp', 'mod'])
                for (dst, src, kind) in ((DB1, 0, 'c'), (DA1, 8, 'p1'), (DG1, 16, 'g'),
                                         (DB2, 24, 'c'), (DA2, 32, 'p1'), (DG2, 40, 'g')):
                    o = DER(l, dst, 8)
                    i = mod[:, l * 48 + src: l * 48 + src + 8]
                    if kind == 'c':
                        P.op('pool', lambda e, o=o, i=i: e.tensor_copy(out=o, in_=i), reads=['mod'], writes=['der'])
                    elif kind == 'p1':
                        P.op('pool', lambda e, o=o, i=i: e.tensor_scalar_add(out=o, in0=i, scalar1=1.0), reads=['mod'], writes=['der'])
                    else:
                        P.op('pool', lambda e, o=o, i=i: e.tensor_scalar(out=o, in0=i, scalar1=1.0, scalar2=1.0 / ALPHA,
                                                                        op0=ALU.add, op1=ALU.mult), reads=['mod'], writes=['der'])
                P.op('pool', lambda e, l=l: e.tensor_scalar(out=omka[:, l * 4:(l + 1) * 4], in0=V(l, V_KA, 4), scalar1=-1.0, scalar2=1.0,
                                                            op0=ALU.mult, op1=ALU.add), reads=['vecs'], writes=['omka'])
                P.op('pool', lambda e, l=l: e.tensor_scalar(out=omu[:, l * 14:(l + 1) * 14], in0=V(l, V_MU, 14), scalar1=-1.0, scalar2=1.0,
                                                            op0=ALU.mult, op1=ALU.add), reads=['vecs'], writes=['omu'])
            for l in range(L):
                for (dst, gcol, bcol, lsrc, Acol, Bcol) in ((0, V_L1G, V_L1B, l, DA2, DB2), (16, V_L2G, V_L2B, l + 1, DA1, DB1)):
                    if lsrc >= L:
                        continue
                    ga = der2[:, l * 32 + dst: l * 32 + dst + 8]
                    ba = der2[:, l * 32 + dst + 8: l * 32 + dst + 16]
                    P.op('dve', lambda e, ga=ga, l=l, gcol=gcol, lsrc=lsrc, Acol=Acol: e.tensor_tensor(out=ga, in0=V(l, gcol, 8), in1=DER(lsrc, Acol, 8), op=ALU.mult),
                         reads=['vecs', 'der'], writes=['der2'])
                    P.op('dve', lambda e, ba=ba, l=l, bcol=bcol, lsrc=lsrc, Acol=Acol: e.tensor_tensor(out=ba, in0=V(l, bcol, 8), in1=DER(lsrc, Acol, 8), op=ALU.mult),
                         reads=['vecs', 'der'], writes=['der2'])
                    P.op('dve', lambda e, ba=ba, lsrc=lsrc, Bcol=Bcol: e.tensor_tensor(out=ba, in0=ba, in1=DER(lsrc, Bcol, 8), op=ALU.add),
                         reads=['der'], writes=['der2'])
            cast_engs = ('dve', 'pool', 'act')
            for i in range(L * NSLAB):
                b = q % 2
                q += 1
                P.dma(f'stg{b}', stg[b][:], wslab_d[i, :, :], writes=[f'stg{b}'])
                ce = cast_engs[i % 3]
                if ce == 'act':
                    P.op('act', lambda e, b=b: e.copy(out=stb[b][:], in_=stg[b][:]), reads=[f'stg{b}'], writes=[f'stb{b}'])
                else:
                    P.op(ce, lambda e, b=b: e.tensor_copy(out=stb[b][:], in_=stg[b][:]), reads=[f'stg{b}'], writes=[f'stb{b}'])
                P.dma(f'stb{b}', wsb_d[i, :, :], stb[b][:], reads=[f'stb{b}'])
            P.barrier()
            P.emit()

        NSLOT = 4
        ring = sb("ring", [128, NSLOT, SLAB], BF16)
        xT = sb("xT", [128, NB, T], F32)
        hT = sb("hT", [128, NB, T], BF16)
        ycat = sb("ycat", [128, NB, T], BF16)
        zbuf = sb("zbuf", [128, 4, T + 2], F32)
        usb = sb("usb", [128, 4, T], F32)
        rT = sb("rT", [128, 4, T], F32)
        kT = sb("kT", [128, 4, T], F32)
        vT = sb("vT", [128, 4, T], F32)
        gsb = sb("gsb", [128, 4, T], F32)
        bonus = sb("bonus", [128, 4, T], F32)
        waf = sb("waf", [128, T], F32)
        glf = sb("glf", [128, T], F32)
        tw = sb("tw", [128, T], BF16)
        sgT = sb("sgT", [128, T], BF16)
        tnames = ['sg', 'aa', 'cs', 'eP', 'eN', 'eT', 'kkr', 'sq', 'kmod', 'bvec', 'prod']
        tmp = {n: sb("t_" + n, [128, 2, T], F32) for n in tnames}
        nb = sb("nb", [128, 2, NCH], F32)
        dCt = sb("dCt", [128, NCH, 4], F32)
        AR = sb("AR", [128, 4, NCH, 256], BF16)
        KT = sb("KT", [128, 4, T], BF16)
        BT = sb("BT", [128, 4, T], BF16)
        KTL = sb("KTL", [128, 4, T], BF16)
        BTL = sb("BTL", [128, 4, T], BF16)
        Vf = sb("Vf", [128, 512], F32)
        Vb = sb("Vb", [128, 512], BF16)
        KTLt = sb("KTLt", [128, 512], BF16)
        BTLt = sb("BTLt", [128, 512], BF16)
        Sc = sb("Sc", [128, 8, 512], BF16)
        Nm = [sb(f"Nm{i}", [128, 8, 128], BF16) for i in range(2)]
        Am = [sb(f"Am{i}", [128, 8, 128], BF16) for i in range(2)]
        Pm = [sb(f"Pm{i}", [128, 8, 128], BF16) for i in range(2)]
        Xb = sb("Xb", [128, 512], BF16)
        Ub = sb("Ub", [128, 512], BF16)
        ysb = sb("ysb", [128, 512], F32)
        ysq = sb("ysq", [128, 512], F32)
        ync = sb("ync", [128, 512], F32)
        ynb = sb("ynb", [128, 512], BF16)
        gst = sb("gst", [128, 40], F32)
        otmp = [sb(f"otmp{i}", [128, 128], F32) for i in range(2)]
        zb = [sb(f"zb{i}", [128, T], BF16) for i in range(2)]
        zq = [sb(f"zq{i}", [128, T], BF16) for i in range(2)]
        lnm = sb("lnm", [128, T], F32)
        lnv = sb("lnv", [128, T], F32)
        lnr = sb("lnr", [128, T], F32)
        act = sb("act", [128, NF, T], BF16)
        slt = [sb(f"slt{i}", [128, T], F32) for i in range(2)]

        total_slabs = NT * L * NSLAB
        state = {'loaded': 0, 'mmq': 0}

        def ensure_loaded(upto):
            while state['loaded'] <= min(upto, total_slabs - 1):
                i = state['loaded']
                slot = i % NSLOT
                P.dma(f'ring{slot}', ring[:, slot, :], wsb_d[i % (L * NSLAB), :, :], writes=[f'ring{slot}'])
                state['loaded'] += 1

        def use_slab(i):
            ensure_loaded(i + NSLOT - 1)
            return i % NSLOT

        rot = [(mm0, 'mm0'), (mm1, 'mm1'), (sA, 'sA'), (sB, 'sB'), (xu, 'xu'), (hp, 'hp')]

        def mmbank():
            state['mmq'] += 1
            return rot[state['mmq'] % len(rot)]

        def dense(bank, bkey, slot, stride, off, rhs_t, rkeys, nk):
            for kc in range(nk):
                P.op('pe', lambda e, kc=kc: e.matmul(bank[:, 0:T], lhsT=ring[:, slot, kc * stride + off: kc * stride + off + 128],
                                                     rhs=rhs_t[:, kc, :], start=(kc == 0), stop=(kc == nk - 1)),
                     reads=[f'ring{slot}'] + rkeys, writes=[bkey], inc=(kc == nk - 1))

        def layernorm(l, gcol, bcol, d2off, lnext):
            for j in range(NB):
                b = j % 2
                P.op('pool', lambda e, j=j, b=b: e.tensor_copy(out=zb[b][:], in_=xT[:, j, :]), reads=[f'xT{j}'], writes=[f'zb{b}'])
                P.op('act', lambda e, j=j, b=b: e.activation(out=zq[b][:], in_=xT[:, j, :], func=AF.Square), reads=[f'xT{j}'], writes=[f'zq{b}'])
                P.op('pe', lambda e, j=j, b=b: e.matmul(mm0[:, 0:T], lhsT=onesb[:], rhs=zb[b][:], start=(j == 0), stop=(j == NB - 1)),
                     reads=[f'zb{b}'], writes=['mm0'])
                P.op('pe', lambda e, j=j, b=b: e.matmul(mm1[:, 0:T], lhsT=onesb[:], rhs=zq[b][:], start=(j == 0), stop=(j == NB - 1)),
                     reads=[f'zq{b}'], writes=['mm1'])
            P.op('act', lambda e: e.activation(out=lnm[:], in_=mm0[:, 0:T], func=AF.Copy, scale=1.0 / D), writes=['mm0', 'lnm'])
            P.op('act', lambda e: e.activation(out=lnv[:], in_=mm0[:, 0:T], func=AF.Square, scale=1.0 / D), writes=['mm0', 'lnv'])
            P.op('dve', lambda e: e.scalar_tensor_tensor(out=lnv[:], in0=mm1[:, 0:T], scalar=1.0 / D, in1=lnv[:], op0=ALU.mult, op1=ALU.subtract),
                 writes=['mm1', 'lnv'])
            P.op('act', lambda e: e.activation(out=lnr[:], in_=lnv[:], func=AF.Ln, bias=LN_EPS / (ALPHA * ALPHA)), reads=['lnv'], writes=['lnr'])
            P.op('act', lambda e: e.activation(out=lnr[:], in_=lnr[:], func=AF.Exp, scale=-0.5), writes=['lnr'])
            xall = [f'xT{j}' for j in range(NB)]
            for hh in range(2):
                hsl = slice(4 * hh, 4 * hh + 4)
                P.op('dve', lambda e, hsl=hsl: e.tensor_tensor(out=xT[:, hsl, :], in0=xT[:, hsl, :], in1=lnm[:].unsqueeze(1).to_broadcast([128, 4, T]), op=ALU.subtract),
                     reads=['lnm'], writes=xall[4 * hh:4 * hh + 4])
                P.op('pool', lambda e, hsl=hsl: e.tensor_tensor(out=xT[:, hsl, :], in0=xT[:, hsl, :], in1=lnr[:].unsqueeze(1).to_broadcast([128, 4, T]), op=ALU.mult),
                     reads=['lnr'], writes=xall[4 * hh:4 * hh + 4])
            for j in range(NB):
                if lnext is not None:
                    P.op('dve', lambda e, j=j: e.tensor_scalar(out=hT[:, j, :], in0=xT[:, j, :], scalar1=der2[:, l * 32 + d2off + j: l * 32 + d2off + j + 1],
                                                               scalar2=der2[:, l * 32 + d2off + 8 + j: l * 32 + d2off + 8 + j + 1], op0=ALU.mult, op1=ALU.add),
                         reads=[f'xT{j}'], writes=[f'hT{j}'])
                P.op('act', lambda e, j=j: e.activation(out=xT[:, j, :], in_=xT[:, j, :], func=AF.Identity,
                                                        scale=V(l, gcol + j), bias=V(l, bcol + j)), writes=[f'xT{j}'])

        hkeys = [f'hT{j}' for j in range(NB)]
        ykeys = [f'ycat{j}' for j in range(NB)]
        si = 0
        for ti in range(NT):
            t0 = ti * T
            P.dma('xin', xT[:], xT_d[:, t0:t0 + T].rearrange("(j p) t -> p j t", p=128), writes=[f'xT{j}' for j in range(NB)])
            for j in range(NB):
                P.op('act', lambda e, j=j: e.activation(out=hT[:, j, :], in_=xT[:, j, :], func=AF.Identity,
                                                        scale=DER(0, DA1 + j), bias=DER(0, DB1 + j)),
                     reads=[f'xT{j}'], writes=[f'hT{j}'])
            for l in range(L):
                if KSTOP < 2:
                    break
                def shift_lerp(bank, bk, mucol, dst, dkey):
                    hold = hist2[ti % 2][:, l * 14 + mucol: l * 14 + mucol + 1]
                    hnew = hist2[(ti + 1) % 2][:, l * 14 + mucol: l * 14 + mucol + 1]
                    ko, kn = f'hist{ti % 2}_{l}_{mucol}', f'hist{(ti + 1) % 2}_{l}_{mucol}'
                    P.op('act', lambda e: e.activation(out=dst, in_=bank[:, 0:T], func=AF.Identity, scale=omu[:, l * 14 + mucol: l * 14 + mucol + 1]),
                         writes=[bk, dkey])
                    P.op('act', lambda e: e.copy(out=hnew, in_=bank[:, T - 1:T]), writes=[bk, kn])
                    P.op('dve', lambda e: e.scalar_tensor_tensor(out=dst[:, 1:T], in0=bank[:, 0:T - 1], scalar=V(l, V_MU + mucol), in1=dst[:, 1:T],
                                                                 op0=ALU.mult, op1=ALU.add), writes=[bk, dkey])
                    P.op('dve', lambda e: e.scalar_tensor_tensor(out=dst[:, 0:1], in0=hold, scalar=V(l, V_MU + mucol), in1=dst[:, 0:1],
                                                                 op0=ALU.mult, op1=ALU.add), reads=[ko], writes=[dkey])

                for gi, (dstT, nm) in enumerate(((rT, 'rT'), (kT, 'kT'), (vT, 'vT'))):
                    slot = use_slab(si); si += 1
                    for j in range(4):
                        bank, bk = mmbank()
                        dense(bank, bk, slot, 512, j * 128, hT, hkeys, NB)
                        shift_lerp(bank, bk, gi * 4 + j, dstT[:, j, :], f'{nm}{j}')
                slot = use_slab(si); si += 1
                bank, bk = mmbank()
                dense(bank, bk, slot, 256, 0, hT, hkeys, NB)
                shift_lerp(bank, bk, 12, waf[:], 'waf')
                P.op('act', lambda e: e.activation(out=tw[0:64, :], in_=waf[0:64, :], func=AF.Tanh), reads=['waf'], writes=['tw'])
                P.op('pool', lambda e: e.tensor_copy(out=tw[64:128, :], in_=waf[64:128, :]), reads=['waf'], writes=['tw'])
                bank, bk = mmbank()
                dense(bank, bk, slot, 256, 128, hT, hkeys, NB)
                shift_lerp(bank, bk, 13, glf[:], 'glf')
                P.op('act', lambda e: e.activation(out=sgT[:], in_=glf[:], func=AF.Sigmoid), reads=['glf'], writes=['sgT'])

                def conv_u(slot):
                    for j in range(4):
                        bank, bk = mmbank()
                        dense(bank, bk, slot, 512, j * 128, hT, hkeys, NB)
                        P.op('act', lambda e, j=j, bank=bank: e.copy(out=usb[:, j, :], in_=bank[:, 0:T]), writes=[bk, f'usb{j}'])
                def conv_C(slot):
                    for j in range(4):
                        bank, bk = mmbank()
                        dense(bank, bk, slot, 512, j * 128, hT, hkeys, NB)
                        P.op('pool', lambda e, j=j: e.tensor_copy(out=zbuf[:, j, 0:2], in_=zh[:, l, j, :]), reads=[f'zh{l}_{j}'], writes=[f'zbuf{j}'])
                        P.op('dve', lambda e, j=j, bank=bank: e.tensor_tensor(out=zbuf[:, j, 2:T + 2], in0=bank[:, 0:T], in1=usb[:, j, :], op=ALU.mult),
                             reads=[f'usb{j}'], writes=[bk, f'zbuf{j}'])
                        P.op('pool', lambda e, j=j: e.tensor_copy(out=zh[:, l, j, :], in_=zbuf[:, j, T:T + 2]), reads=[f'zbuf{j}'], writes=[f'zh{l}_{j}'])
                        P.op('pool', lambda e, j=j: e.tensor_scalar(out=usb[:, j, :], in0=zbuf[:, j, 0:T], scalar1=V(l, V_CW + j), scalar2=None, op0=ALU.mult),
                             reads=[f'zbuf{j}'], writes=[f'usb{j}'])
                        P.op('dve', lambda e, j=j: e.scalar_tensor_tensor(out=usb[:, j, :], in0=zbuf[:, j, 1:T + 1], scalar=V(l, V_CW + 4 + j), in1=usb[:, j, :],
                                                                          op0=ALU.mult, op1=ALU.add), reads=[f'zbuf{j}'], writes=[f'usb{j}'])
                        P.op('dve', lambda e, j=j: e.scalar_tensor_tensor(out=usb[:, j, :], in0=zbuf[:, j, 2:T + 2], scalar=V(l, V_CW + 8 + j), in1=usb[:, j, :],
                                                                           op0=ALU.mult, op1=ALU.add), reads=[f'zbuf{j}'], writes=[f'usb{j}'])
                def conv_B(slot):
                    for j in range(4):
                        bank, bk = mmbank()
                        dense(bank, bk, slot, 512, j * 128, hT, hkeys, NB)
                        P.op('dve', lambda e, j=j, bank=bank: e.tensor_tensor(out=ycat[:, j, :], in0=bank[:, 0:T], in1=usb[:, j, :], op=ALU.mult),
                             reads=[f'usb{j}'], writes=[bk, f'ycat{j}'])
                conv_steps = [conv_u, conv_C, conv_B]
                if KSTOP < 3:
                    break
                tm = tmp
                v3 = lambda a: a.rearrange("p (c t) -> p c t", t=128)
                for pr in range(2):
                    slot = use_slab(si); si += 1
                    conv_steps[pr](slot)
                    K2 = lambda n: [n + '0', n + '1']
                    for i in range(2):
                        cc = 2 * pr + i
                        wcol = l * 512 + cc * 128
                        bank, bk = mmbank()
                        P.op('pe', lambda e, bank=bank, wcol=wcol: e.matmul(bank[:, 0:T], lhsT=lwb[0:64, wcol:wcol + 128], rhs=tw[0:64, :], start=True, stop=True),
                             reads=['tw'], writes=[bk])
                        P.op('act', lambda e, bank=bank, cc=cc, i=i: e.activation(out=tm['sg'][:, i, :], in_=bank[:, 0:T], func=AF.Sigmoid, bias=V(l, V_W0 + cc)),
                             writes=[bk, f'sg{i}'])
                        bank, bk = mmbank()
                        P.op('pe', lambda e, bank=bank, wcol=wcol: e.matmul(bank[:, 0:T], lhsT=lwb[64:128, wcol:wcol + 128], rhs=tw[64:128, :], start=True, stop=True),
                             reads=['tw'], writes=[bk])
                        P.op('act', lambda e, bank=bank, cc=cc, i=i: e.activation(out=tm['aa'][:, i, :], in_=bank[:, 0:T], func=AF.Sigmoid, bias=V(l, V_A0 + cc)),
                             writes=[bk, f'aa{i}'])
                    for i in range(2):
                        cc = 2 * pr + i
                        wcol = l * 512 + cc * 128
                        bank, bk = mmbank()
                        P.op('pe', lambda e, bank=bank, wcol=wcol: e.matmul(bank[:, 0:T], lhsT=gub[:, wcol:wcol + 128], rhs=sgT[:], start=True, stop=True),
                             reads=['sgT'], writes=[bk])
                        P.op('act', lambda e, bank=bank, cc=cc: e.copy(out=gsb[:, cc, :], in_=bank[:, 0:T]), writes=[bk, f'gsb{cc}'])
                    for i in range(2):
                        cc = 2 * pr + i
                        for ch in range(NCH):
                            cs_ = slice(ch * 128, (ch + 1) * 128)
                            P.op('dve', lambda e, cs_=cs_, i=i: e.tensor_tensor_scan(out=tm['cs'][:, i, cs_], data0=ones, data1=tm['sg'][:, i, cs_], initial=0.0,
                                                                                     op0=ALU.mult, op1=ALU.add), reads=[f'sg{i}'], writes=[f'cs{i}'])
                        P.op('dve', lambda e, cc=cc, i=i: e.tensor_scalar(out=tm['kkr'][:, i, :], in0=kT[:, cc, :], scalar1=V(l, V_KK + cc), scalar2=None, op0=ALU.mult),
                             reads=[f'kT{cc}'], writes=[f'kkr{i}'])
                    P.op('act', lambda e: e.activation(out=tm['sq'][:], in_=tm['kkr'][:], func=AF.Square), reads=K2('kkr'), writes=K2('sq'))
                    bank, bk = mmbank()
                    P.op('pe', lambda e, bank=bank: e.matmul(bank[:, 0:2 * T], lhsT=bones, rhs=tm['sq'][:].rearrange("p i t -> p (i t)"), start=True, stop=True),
                         reads=K2('sq'), writes=[bk])
                    P.op('dve', lambda e, bank=bank: e.tensor_scalar(out=tm['sq'][:].rearrange("p i t -> p (i t)"), in0=bank[:, 0:2 * T], scalar1=1e-24, scalar2=None, op0=ALU.max),
                         writes=[bk] + K2('sq'))
                    P.op('pool', lambda e: e.tensor_tensor(out=tm['sg'][:], in0=tm['cs'][:], in1=tm['sg'][:], op=ALU.subtract), reads=K2('cs'), writes=K2('sg'))
                    P.op('act', lambda e: e.activation(out=tm['eP'][:], in_=tm['cs'][:], func=AF.Exp, scale=-CDEC), reads=K2('cs'), writes=K2('eP'))
                    P.op('act', lambda e: e.activation(out=tm['eN'][:], in_=tm['cs'][:], func=AF.Exp, scale=CDEC), reads=K2('cs'), writes=K2('eN'))
                    P.op('act', lambda e: e.activation(out=tm['sg'][:], in_=tm['sg'][:], func=AF.Exp, scale=-CDEC), writes=K2('sg'))
                    P.op('act', lambda e: e.activation(out=tm['sq'][:], in_=tm['sq'][:], func=AF.Ln), writes=K2('sq'))
                    P.op('act', lambda e: e.activation(out=tm['sq'][:], in_=tm['sq'][:], func=AF.Exp, scale=-0.5), writes=K2('sq'))
                    P.op('dve', lambda e: e.tensor_scalar(out=nb[:].rearrange("p i c -> p (i c)"), in0=tm['cs'][:].rearrange("p i (c t) -> p (i c) t", t=128)[:, :, 127],
                                                          scalar1=-CDEC, scalar2=None, op0=ALU.mult), reads=K2('cs'), writes=['nb'])
                    for i in range(2):
                        cc = 2 * pr + i
                        P.op('act', lambda e, cc=cc, i=i: e.activation(out=dCt[:, :, cc], in_=nb[:, i, :], func=AF.Exp), reads=['nb'], writes=['dCt'])
                        for ch in range(NCH):
                            cs_ = slice(ch * 128, (ch + 1) * 128)
                            P.op('act', lambda e, cs_=cs_, ch=ch, i=i: e.activation(out=tm['eT'][:, i, cs_], in_=tm['cs'][:, i, cs_], func=AF.Exp, scale=CDEC, bias=nb[:, i, ch:ch + 1]),
                                 reads=[f'cs{i}', 'nb'], writes=[f'eT{i}'])
                    P.op('pool', lambda e: e.tensor_tensor(out=tm['kkr'][:], in0=tm['kkr'][:], in1=tm['sq'][:], op=ALU.mult), reads=K2('sq'), writes=K2('kkr'))
                    P.op('pool', lambda e: e.tensor_tensor(out=tm['bvec'][:], in0=tm['kkr'][:], in1=tm['aa'][:], op=ALU.mult), reads=K2('kkr') + K2('aa'), writes=K2('bvec'))
                    for i in range(2):
                        cc = 2 * pr + i
                        P.op('dve', lambda e, cc=cc, i=i: e.tensor_scalar(out=tm['aa'][:, i, :], in0=tm['aa'][:, i, :], scalar1=V(l, V_KA + cc), scalar2=omka[:, l * 4 + cc: l * 4 + cc + 1],
                                                                          op0=ALU.mult, op1=ALU.add), writes=[f'aa{i}'])
                    P.op('pool', lambda e: e.tensor_tensor(out=tm['kmod'][:], in0=kT[:, 2 * pr:2 * pr + 2, :], in1=tm['aa'][:], op=ALU.mult),
                         reads=[f'kT{2 * pr}', f'kT{2 * pr + 1}'] + K2('aa'), writes=K2('kmod'))
                    for i in range(2):
                        cc = 2 * pr + i
                        P.op('dve', lambda e, cc=cc, i=i: e.scalar_tensor_tensor(out=tm['prod'][:, i, :], in0=rT[:, cc, :], scalar=V(l, V_RK + cc), in1=tm['kmod'][:, i, :], op0=ALU.mult, op1=ALU.mult),
                             reads=[f'rT{cc}', f'kmod{i}'], writes=[f'prod{i}'])
                    bank, bk = mmbank()
                    P.op('pe', lambda e, bank=bank: e.matmul(bank[:, 0:2 * T], lhsT=bones, rhs=tm['prod'][:].rearrange("p i t -> p (i t)"), start=True, stop=True),
                         reads=K2('prod'), writes=[bk])
                    P.op('dve', lambda e, bank=bank: e.tensor_tensor(out=bonus[:, 2 * pr:2 * pr + 2, :].rearrange("p i t -> p (i t)"), in0=bank[:, 0:2 * T],
                                                                      in1=vT[:, 2 * pr:2 * pr + 2, :].rearrange("p i t -> p (i t)"), op=ALU.mult),
                         reads=[f'vT{2 * pr}', f'vT{2 * pr + 1}'], writes=[bk, f'bonus{2 * pr}', f'bonus{2 * pr + 1}'])
                    for i in range(2):
                        cc = 2 * pr + i
                        P.op('dve', lambda e, cc=cc, i=i: e.scalar_tensor_tensor(out=AR[:, cc, :, 0:128], in0=v3(tm['kkr'][:, i, :]), scalar=-1.0, in1=v3(tm['sg'][:, i, :]), op0=ALU.mult, op1=ALU.mult),
                             reads=[f'kkr{i}', f'sg{i}'], writes=[f'AR{cc}'])
                        P.op('pool', lambda e, cc=cc, i=i: e.tensor_tensor(out=AR[:, cc, :, 128:256], in0=v3(rT[:, cc, :]), in1=v3(tm['eP'][:, i, :]), op=ALU.mult),
                             reads=[f'rT{cc}', f'eP{i}'], writes=[f'AR{cc}'])
                    pk = lambda n: [f'{n}{2 * pr}', f'{n}{2 * pr + 1}']
                    P.op('dve', lambda e: e.tensor_tensor(out=KT[:, 2 * pr:2 * pr + 2, :], in0=tm['kmod'][:], in1=tm['eN'][:], op=ALU.mult), reads=K2('kmod') + K2('eN'), writes=pk('KT'))
                    P.op('pool', lambda e: e.tensor_tensor(out=BT[:, 2 * pr:2 * pr + 2, :], in0=tm['bvec'][:], in1=tm['eN'][:], op=ALU.mult), reads=K2('bvec') + K2('eN'), writes=pk('BT'))
                    P.op('dve', lambda e: e.tensor_tensor(out=KTL[:, 2 * pr:2 * pr + 2, :], in0=tm['kmod'][:], in1=tm['eT'][:], op=ALU.mult), reads=K2('kmod') + K2('eT'), writes=pk('KTL'))
                    P.op('pool', lambda e: e.tensor_tensor(out=BTL[:, 2 * pr:2 * pr + 2, :], in0=tm['bvec'][:], in1=tm['eT'][:], op=ALU.mult), reads=K2('bvec') + K2('eT'), writes=pk('BTL'))
                slot = use_slab(si); si += 1
                conv_steps[2](slot)

                if KSTOP < 4:
                    break
                for ch in range(NCH):
                    cs_ = slice(ch * 128, (ch + 1) * 128)
                    for cc in range(4):
                        P.op('pe', lambda e, cc=cc: e.transpose(out=tpf[:, cc * 128:(cc + 1) * 128], in_=vT[:, cc, cs_], identity=ident),
                             reads=[f'vT{cc}'], writes=['tpf'], inc=(cc == 3))
                    P.op('act', lambda e: e.copy(out=Vf[:], in_=tpf[:]), writes=['tpf', 'Vf'])
                    P.op('pool', lambda e: e.tensor_copy(out=Vb[:], in_=Vf[:]), reads=['Vf'], writes=['Vb'])
                    for cc in range(4):
                        P.op('pe', lambda e, cc=cc: e.transpose(out=tpb[:, cc * 128:(cc + 1) * 128], in_=KTL[:, cc, cs_], identity=identb[:]),
                             reads=[f'KTL{cc}'], writes=['tpb'], inc=(cc == 3))
                    P.op('dve', lambda e: e.tensor_copy(out=KTLt[:], in_=tpb[:, 0:512]), writes=['tpb', 'KTLt'])
                    for cc in range(4):
                        P.op('pe', lambda e, cc=cc: e.transpose(out=tpb[:, cc * 128:(cc + 1) * 128], in_=BTL[:, cc, cs_], identity=identb[:]),
                             reads=[f'BTL{cc}'], writes=['tpb'], inc=(cc == 3))
                    P.op('act', lambda e: e.copy(out=BTLt[:], in_=tpb[:, 0:512]), writes=['tpb', 'BTLt'])
                    for h in range(8):
                        cc, po = h // 2, 64 * (h % 2)
                        bank, bk = [(sA, 'sA'), (sB, 'sB'), (xu, 'xu'), (hp, 'hp')][h % 4]
                        P.op('pe', lambda e, cc=cc, po=po, bank=bank: e.matmul(bank[:, 0:256], lhsT=BT[po:po + 64, cc, cs_], rhs=AR[po:po + 64, cc, ch, :], start=True, stop=True),
                             reads=[f'BT{cc}', f'AR{cc}'], writes=[bk], inc=False)
                        P.op('pe', lambda e, cc=cc, po=po, bank=bank: e.matmul(bank[:, 256:512], lhsT=KT[po:po + 64, cc, cs_], rhs=AR[po:po + 64, cc, ch, :], start=True, stop=True),
                             reads=[f'KT{cc}', f'AR{cc}'], writes=[bk])
                        P.op('dve', lambda e, h=h, bank=bank: e.tensor_tensor(out=Sc[:, h, :], in0=bank[:], in1=mask4, op=ALU.mult), writes=[bk, f'Sc{h}'])
                    for g in range(2):
                        bank, bk = (sA, 'sA') if g == 0 else (sB, 'sB')
                        po = 64 * g
                        for hh in range(4):
                            h = 2 * hh + g
                            cc = h // 2
                            P.op('pe', lambda e, cc=cc, po=po, bank=bank, hh=hh: e.matmul(bank[:, hh * 128:(hh + 1) * 128], lhsT=AR[po:po + 64, cc, ch, 0:128], rhs=BT[po:po + 64, cc, cs_],
                                                                                          start=True, stop=True), reads=[f'BT{cc}', f'AR{cc}'], writes=[bk], inc=(hh == 3))
                        P.op('dve', lambda e, g=g, bank=bank: e.tensor_tensor(out=Nm[0][:, g::2, :], in0=bank[:].rearrange("p (h t) -> p h t", t=128),
                                                                              in1=maskN4.rearrange("p (h t) -> p h t", t=128), op=ALU.mult), writes=[bk, 'Nm0_0', 'Nm0_1'])
                    for g in range(2):
                        P.op('pool', lambda e, g=g: e.tensor_tensor(out=Pm[0][:, 4 * g:4 * g + 4, :], in0=Sc[:, 4 * g:4 * g + 4, 0:128],
                                                                    in1=identb[:].unsqueeze(1).to_broadcast([128, 4, 128]), op=ALU.add),
                             reads=[f'Sc{4 * g + i}' for i in range(4)], writes=[f'Pm0_{g}'])
                    abank = [(sA, 'sA'), (xu, 'xu')]
                    nbank = [(sB, 'sB'), (hp, 'hp')]
                    pbank = [(mm0, 'mm0'), (mm1, 'mm1')]
                    for lev in range(6):
                        ci, ni = lev % 2, (lev + 1) % 2
                        for g in range(2):
                            hs = [4 * g + i for i in range(4)]
                            if lev == 0:
                                Acur = lambda h: Sc[:, h, 0:128]
                                akeys = [f'Sc{h}' for h in hs]
                            else:
                                Acur = lambda h, ci=ci: Am[ci][:, h, :]
                                akeys = [f'Am{ci}_{g}']
                            nkeys = [f'Nm{ci}_{g}']
                            if lev < 5:
                                bank, bk = abank[g]
                                for hh, h in enumerate(hs):
                                    P.op('pe', lambda e, hh=hh, h=h, Acur=Acur, ci=ci, bank=bank: e.matmul(bank[:, hh * 128:(hh + 1) * 128], lhsT=Nm[ci][:, h, :], rhs=Acur(h), start=True, stop=True),
                                         reads=akeys + nkeys, writes=[bk], inc=(hh == 3))
                                P.op('act', lambda e, g=g, ni=ni, bank=bank: e.copy(out=Am[ni][:, 4 * g:4 * g + 4, :], in_=bank[:].rearrange("p (h t) -> p h t", t=128)),
                                     writes=[bk, f'Am{ni}_{g}'])
                            bank, bk = nbank[g]
                            for hh, h in enumerate(hs):
                                P.op('pe', lambda e, hh=hh, h=h, Acur=Acur, ci=ci, bank=bank: e.matmul(bank[:, hh * 128:(hh + 1) * 128], lhsT=Acur(h), rhs=Nm[ci][:, h, :], start=True, stop=True),
                                     reads=akeys + nkeys, writes=[bk], inc=(hh == 3))
                            P.op('dve', lambda e, g=g, ni=ni, bank=bank: e.tensor_copy(out=Nm[ni][:, 4 * g:4 * g + 4, :], in_=bank[:].rearrange("p (h t) -> p h t", t=128)),
                                 writes=[bk, f'Nm{ni}_{g}'])
                        for g in range(2):
                            hs = [4 * g + i for i in range(4)]
                            bank, bk = pbank[g]
                            for hh, h in enumerate(hs):
                                P.op('pe', lambda e, hh=hh, h=h, ni=ni, ci=ci, bank=bank: e.matmul(bank[:, hh * 128:(hh + 1) * 128], lhsT=Nm[ni][:, h, :], rhs=Pm[ci][:, h, :], start=True, stop=True),
                                     reads=[f'Nm{ni}_{g}', f'Pm{ci}_{g}'], writes=[bk], inc=(hh == 3))
                            P.op('dve', lambda e, g=g, ni=ni, ci=ci, bank=bank: e.tensor_tensor(out=Pm[ni][:, 4 * g:4 * g + 4, :], in0=bank[:].rearrange("p (h t) -> p h t", t=128),
                                                                                              in1=Pm[ci][:, 4 * g:4 * g + 4, :], op=ALU.add),
                                 reads=[f'Pm{ci}_{g}'], writes=[bk, f'Pm{ni}_{g}'])
                    Pf = Pm[0]
                    pfk = ['Pm0_0', 'Pm0_1']
                    sck = [f'Sc{h}' for h in range(8)]
                    for h in range(8):
                        cc, po = h // 2, 64 * (h % 2)
                        hs_ = slice(h * 64, (h + 1) * 64)
                        P.op('pe', lambda e, h=h, hs_=hs_: e.matmul(xu[:, hs_], lhsT=Sc[:, h, 256:384], rhs=Vb[:, hs_], start=True, stop=False),
                             reads=[f'Sc{h}', 'Vb'], writes=['xu'], inc=False)
                        P.op('pe', lambda e, cc=cc, po=po, hs_=hs_: e.matmul(xu[:, hs_], lhsT=AR[po:po + 64, cc, ch, 0:128], rhs=Hb[po:po + 64, l, cc, :], start=False, stop=True),
                             reads=[f'AR{cc}', f'Hb{l}'], writes=['xu'], inc=(h == 7))
                    P.op('act', lambda e: e.copy(out=Xb[:], in_=xu[:]), writes=['xu', 'Xb'])
                    for h in range(8):
                        hs_ = slice(h * 64, (h + 1) * 64)
                        P.op('pe', lambda e, h=h, hs_=hs_: e.matmul(xu[:, hs_], lhsT=Pf[:, h, :], rhs=Xb[:, hs_], start=True, stop=True),
                             reads=pfk + ['Xb'], writes=['xu'], inc=(h == 7))
                    P.op('dve', lambda e: e.tensor_copy(out=Ub[:], in_=xu[:]), writes=['xu', 'Ub'])
                    for h in range(8):
                        cc, po = h // 2, 64 * (h % 2)
                        hs_ = slice(h * 64, (h + 1) * 64)
                        P.op('pe', lambda e, cc=cc, po=po, hs_=hs_: e.matmul(tpf[:, hs_], lhsT=AR[po:po + 64, cc, ch, 128:256], rhs=Hb[po:po + 64, l, cc, :], start=True, stop=False),
                             reads=[f'AR{cc}', f'Hb{l}'], writes=['tpf'], inc=False)
                        P.op('pe', lambda e, h=h, hs_=hs_: e.matmul(tpf[:, hs_], lhsT=Sc[:, h, 128:256], rhs=Ub[:, hs_], start=False, stop=False),
                             reads=[f'Sc{h}', 'Ub'], writes=['tpf'], inc=False)
                        P.op('pe', lambda e, h=h, hs_=hs_: e.matmul(tpf[:, hs_], lhsT=Sc[:, h, 384:512], rhs=Vb[:, hs_], start=False, stop=True),
                             reads=[f'Sc{h}', 'Vb'], writes=['tpf'], inc=(h == 7))
                    for h in range(8):
                        cc, po = h // 2, 64 * (h % 2)
                        hs_ = slice(h * 64, (h + 1) * 64)
                        P.op('pe', lambda e, cc=cc, po=po, hs_=hs_: e.matmul(hp[po:po + 64, cc * 64:(cc + 1) * 64], lhsT=BTLt[:, hs_], rhs=Ub[:, hs_], start=True, stop=False),
                             reads=['BTLt', 'Ub'], writes=['hp'], inc=False)
                        P.op('pe', lambda e, cc=cc, po=po, hs_=hs_: e.matmul(hp[po:po + 64, cc * 64:(cc + 1) * 64], lhsT=KTLt[:, hs_], rhs=Vb[:, hs_], start=False, stop=True),
                             reads=['KTLt', 'Vb'], writes=['hp'], inc=(h == 7))
                    P.op('dve', lambda e: e.tensor_tensor(out=Hm[:, l, :, :], in0=Hm[:, l, :, :], in1=dCt[:, ch, :].unsqueeze(2).to_broadcast([128, 4, 64]), op=ALU.mult),
                         reads=['dCt', f'Hb{l}'], writes=[f'Hm{l}'])
                    P.op('dve', lambda e: e.tensor_tensor(out=Hm[:, l, :, :], in0=Hm[:, l, :, :], in1=hp[:, 0:256].rearrange("p (c v) -> p c v", v=64), op=ALU.add),
                         writes=['hp', f'Hm{l}'])
                    P.op('pool', lambda e: e.tensor_copy(out=Hb[:, l, :, :], in_=Hm[:, l, :, :]), reads=[f'Hm{l}'], writes=[f'Hb{l}'])
                    P.op('act', lambda e: e.copy(out=ysb[:], in_=tpf[:]), writes=['tpf', 'ysb'])
                    P.op('pool', lambda e: e.tensor_tensor(out=ysq[:], in0=ysb[:], in1=ysb[:], op=ALU.mult), reads=['ysb'], writes=['ysq'])
                    y3 = lambda a: a.rearrange("p (h v) -> p h v", v=64)
                    P.op('dve', lambda e: e.tensor_reduce(out=gst[:, 0:8], in_=y3(ysb[:]), axis=AX.X, op=ALU.add), reads=['ysb'], writes=['gst0'])
                    P.op('dve', lambda e: e.tensor_reduce(out=gst[:, 8:16], in_=y3(ysq[:]), axis=AX.X, op=ALU.add), reads=['ysq'], writes=['gst1'])
                    P.op('dve', lambda e: e.tensor_scalar(out=gst[:, 16:24], in0=gst[:, 0:8], scalar1=1.0 / 64, scalar2=None, op0=ALU.mult), reads=['gst0'], writes=['gst2'])
                    P.op('pool', lambda e: e.tensor_tensor(out=gst[:, 24:32], in0=gst[:, 16:24], in1=gst[:, 16:24], op=ALU.mult), reads=['gst2'], writes=['gst3'])
                    P.op('dve', lambda e: e.scalar_tensor_tensor(out=gst[:, 32:40], in0=gst[:, 8:16], scalar=1.0 / 64, in1=gst[:, 24:32], op0=ALU.mult, op1=ALU.subtract),
                         reads=['gst1', 'gst3'], writes=['gst4'])
                    P.op('act', lambda e: e.activation(out=gst[:, 24:32], in_=gst[:, 32:40], func=AF.Ln, bias=GN_EPS), reads=['gst4'], writes=['gst3'])
                    P.op('act', lambda e: e.activation(out=gst[:, 24:32], in_=gst[:, 24:32], func=AF.Exp, scale=-0.5), writes=['gst3'])
                    P.op('dve', lambda e: e.tensor_tensor(out=y3(ync[:]), in0=y3(ysb[:]), in1=gst[:, 16:24].unsqueeze(2).to_broadcast([128, 8, 64]), op=ALU.subtract),
                         reads=['ysb', 'gst2'], writes=['ync'])
                    P.op('pool', lambda e: e.tensor_tensor(out=y3(ynb[:]), in0=y3(ync[:]), in1=gst[:, 24:32].unsqueeze(2).to_broadcast([128, 8, 64]), op=ALU.mult),
                         reads=['ync', 'gst3'], writes=['ynb'])
                    for cc in range(4):
                        P.op('pe', lambda e, cc=cc: e.transpose(out=tpb[:, cc * 128:(cc + 1) * 128], in_=ynb[:, cc * 128:(cc + 1) * 128], identity=identb[:]),
                             reads=['ynb'], writes=['tpb'], inc=(cc == 3))
                    for cc in range(4):
                        b = cc % 2
                        P.op('dve', lambda e, cc=cc, b=b: e.tensor_scalar(out=otmp[b][:], in0=tpb[:, cc * 128:(cc + 1) * 128], scalar1=V(l, V_LG + cc), scalar2=V(l, V_LB + cc),
                                                                          op0=ALU.mult, op1=ALU.add), writes=['tpb', f'otmp{b}'])
                        P.op('pool', lambda e, cc=cc, b=b: e.tensor_tensor(out=otmp[b][:], in0=otmp[b][:], in1=bonus[:, cc, cs_], op=ALU.add),
                             reads=[f'bonus{cc}'], writes=[f'otmp{b}'])
                        P.op('dve', lambda e, cc=cc, b=b: e.tensor_tensor(out=ycat[:, 4 + cc, cs_], in0=otmp[b][:], in1=gsb[:, cc, cs_], op=ALU.mult),
                             reads=[f'otmp{b}', f'gsb{cc}'], writes=[f'ycat{4 + cc}'])

                if KSTOP < 5:
                    break
                for half in range(2):
                    slot = use_slab(si); si += 1
                    for jj in range(4):
                        j = half * 4 + jj
                        bank, bk = mmbank()
                        dense(bank, bk, slot, 512, jj * 128, ycat, ykeys, NB)
                        P.op('dve', lambda e, j=j, bank=bank: e.scalar_tensor_tensor(out=xT[:, j, :], in0=bank[:, 0:T], scalar=DER(l, DG1 + j), in1=xT[:, j, :], op0=ALU.mult, op1=ALU.add),
                             writes=[bk, f'xT{j}'])
                layernorm(l, V_L1G, V_L1B, 0, l)
                if KSTOP < 6:
                    break
                for s in range(11):
                    slot = use_slab(si); si += 1
                    for qq in range(2):
                        (gbank, gk), (ubank, uk) = [((mm0, 'mm0'), (mm1, 'mm1')), ((sA, 'sA'), (sB, 'sB')), ((xu, 'xu'), (hp, 'hp'))][(2 * s + qq) % 3]
                        for kc in range(NB):
                            P.op('pe', lambda e, kc=kc, qq=qq, slot=slot, gbank=gbank: e.matmul(gbank[:, 0:T], lhsT=ring[:, slot, kc * 512 + qq * 128: kc * 512 + qq * 128 + 128], rhs=hT[:, kc, :],
                                                                                                start=(kc == 0), stop=(kc == NB - 1)), reads=[f'ring{slot}'] + hkeys, writes=[gk], inc=(kc == NB - 1))
                        for kc in range(NB):
                            P.op('pe', lambda e, kc=kc, qq=qq, slot=slot, ubank=ubank: e.matmul(ubank[:, 0:T], lhsT=ring[:, slot, kc * 512 + 256 + qq * 128: kc * 512 + 256 + qq * 128 + 128], rhs=hT[:, kc, :],
                                                                                                start=(kc == 0), stop=(kc == NB - 1)), reads=[f'ring{slot}'] + hkeys, writes=[uk], inc=(kc == NB - 1))
                        b = qq
                        P.op('act', lambda e, b=b, gbank=gbank: e.activation(out=slt[b][:], in_=gbank[:, 0:T], func=AF.Silu), writes=[gk, f'slt{b}'])
                        P.op('dve', lambda e, b=b, s=s, qq=qq, ubank=ubank: e.tensor_tensor(out=act[:, 2 * s + qq, :], in0=ubank[:, 0:T], in1=slt[b][:], op=ALU.mult),
                             reads=[f'slt{b}'], writes=[uk, f'act{2 * s + qq}'])
                akeys_ = [f'act{f}' for f in range(NF)]
                for j in range(NB):
                    slot = use_slab(si); si += 1
                    bank, bk = mmbank()
                    dense(bank, bk, slot, 128, 0, act, akeys_, NF)
                    P.op('dve', lambda e, j=j, bank=bank: e.scalar_tensor_tensor(out=xT[:, j, :], in0=bank[:, 0:T], scalar=DER(l, DG2 + j), in1=xT[:, j, :], op0=ALU.mult, op1=ALU.add),
                         writes=[bk, f'xT{j}'])
                layernorm(l, V_L2G, V_L2B, 16, (l + 1) if l + 1 < L else None)
            P.dma('yout', yT_d[:, t0:t0 + T].rearrange("(j p) t -> p j t", p=128), xT[:], reads=[f'xT{j}' for j in range(NB)])
        P._wait('sp', ('yout', P.chan_count['yout']))
        P.emit()
        build_nc.stats = (P.nops, P.nwaits)
    return nc


def _slab_std(W, c0, ncols):
    blk = W.reshape(8, 128, W.shape[1])[:, :, c0:c0 + ncols].transpose(1, 0, 2).reshape(128, 8 * ncols)
    out = np.zeros((128, SLAB), np.float32)
    out[:, :8 * ncols] = blk
    return out


def _vec_cols(v):
    v = np.asarray(v, np.float32).reshape(-1, 128)
    return v.T


def prepare_shared(inp, L):
    wslab = np.zeros((L * NSLAB, 128, SLAB), np.float32)
    wmod = np.zeros((L * NMOD, 128, SLAB), np.float32)
    vecs = np.zeros((128, L, NV), np.float32)
    lw = np.zeros((128, L, 512), np.float32)
    gu = np.zeros((128, L, 512), np.float32)
    for l in range(L):
        Win = np.asarray(inp["w_in"][l], np.float32)
        s = l * NSLAB
        for k, c0 in ((0, 1536), (1, 2048), (2, 2560), (4, 1024), (5, 512), (6, 0)):
            wslab[s + k] = _slab_std(Win, c0, 512)
        wslab[s + 3] = _slab_std(Win, 3072, 256)
        Wout = np.asarray(inp["w_out"][l], np.float32)
        wslab[s + 7] = _slab_std(Wout, 0, 512)
        wslab[s + 8] = _slab_std(Wout, 512, 512)
        W1 = np.asarray(inp["w_ffn_in"][l], np.float32)
        W1r = W1.reshape(8, 128, 2 * DFF)
        for k in range(11):
            g = W1r[:, :, 256 * k:256 * k + 256]
            u = W1r[:, :, DFF + 256 * k:DFF + 256 * k + 256]
            wslab[s + 9 + k] = np.concatenate([g, u], axis=2).transpose(1, 0, 2).reshape(128, SLAB)
        W2 = np.asarray(inp["w_ffn_out"][l], np.float32).reshape(NF, 128, D)
        for j in range(8):
            wslab[s + 20 + j, :, :NF * 128] = W2[:, :, j * 128:(j + 1) * 128].transpose(1, 0, 2).reshape(128, NF * 128)
        Wm = np.asarray(inp["w_mod"][l], np.float32)
        for k in range(NMOD):
            wmod[l * NMOD + k] = _slab_std(Wm, 512 * k, 512)
        mu = np.asarray(inp["mu_shift"][l], np.float32)
        cols = [_vec_cols(mu)]
        for nm in ("w0", "a0", "k_k", "k_a"):
            cols.append(_vec_cols(inp[nm][l]))
        cols.append(_vec_cols(np.asarray(inp["r_k"][l]).reshape(-1)))
        cols.append(_vec_cols(inp["lnx_g"][l]))
        cols.append(_vec_cols(inp["lnx_b"][l]))
        cw = np.asarray(inp["conv_w"][l], np.float32)
        for tap in range(3):
            cols.append(_vec_cols(cw[tap]))
        for nm in ("ln1_g", "ln1_b", "ln2_g", "ln2_b"):
            cols.append(_vec_cols(inp[nm][l]))
        cols.append(_vec_cols(inp["b_mod"][l]))
        vecs[:, l, :] = np.concatenate(cols, axis=1)
        lw[0:64, l, :] = np.asarray(inp["w_decay_up"][l], np.float32)
        lw[64:128, l, :] = np.asarray(inp["a_up"][l], np.float32)
        gu[:, l, :] = np.asarray(inp["g_up"][l], np.float32)
    consts = np.zeros((128, 1408), np.float32)
    consts[:, 0:128] = np.eye(128)
    s_idx = np.arange(128)[:, None]
    t_idx = np.arange(128)[None, :]
    strict = (s_idx < t_idx).astype(np.float32)
    incl = (s_idx <= t_idx).astype(np.float32)
    consts[:, 128:640] = np.concatenate([strict, incl, strict, incl], axis=1)
    lowN = (t_idx < s_idx).astype(np.float32)
    consts[:, 640:1152] = np.concatenate([lowN] * 4, axis=1)
    consts[:, 1152:1280] = ((s_idx // 64) == (t_idx // 64)).astype(np.float32)
    consts[:, 1280:1408] = 1.0
    return dict(wslab=wslab, wmod=wmod, vecs=np.ascontiguousarray(vecs.reshape(128, L * NV)),
                lw=np.ascontiguousarray(lw.reshape(128, L * 512)), gu=np.ascontiguousarray(gu.reshape(128, L * 512)), consts=consts)


def run(inp, SEQ, L, T, runner):
    x = np.asarray(inp["x"], np.float32)
    c = np.asarray(inp["c"], np.float32)
    B = x.shape[0]
    shared = prepare_shared(inp, L)
    nc = build_nc(SEQ, L, T)
    seq_cores = [0, 2, 4, 6][:B]
    zeros = {k: np.zeros_like(v) for k, v in shared.items()}
    zeros["xT"] = np.zeros((D, SEQ), np.float32)
    zeros["cT"] = np.zeros((128, 8), np.float32)
    in_maps = []
    for core in range(8):
        if core in seq_cores:
            b = seq_cores.index(core)
            m = dict(shared)
            m["xT"] = np.ascontiguousarray(x[b].T)
            m["cT"] = np.ascontiguousarray(c[b].reshape(8, 128).T)
        else:
            m = zeros
        in_maps.append(m)
    res = runner(nc, in_maps)
    out = np.stack([np.ascontiguousarray(res[seq_cores[b]]["yT"].T) for b in range(B)], axis=0)
    return out.astype(np.float32)


def kernel(**inputs):
    def runner(nc, in_maps):
        r = run_bass_kernel_spmd(nc, in_maps, core_ids=list(range(8)))
        return r.results
    return run(inputs, 8192, 4, 256, runner)
```
